# Optimizing a Trainium2 kernel written in Bass

```python
import math
import jax, jax.numpy as jnp
from jax import lax
import numpy as np

D_MODEL = 1024
BATCH = 16
SEQ = 2048
DEPTH = 4
DEC_BATCH = 2
DEC_SEQ = 8192
PAST_LEN = 128

PLE_DIM = 256
SSD_WIDTH = D_MODEL // 2
SSD_HEAD_DIM = 64
SSD_HEADS = SSD_WIDTH // SSD_HEAD_DIM
SSD_GROUPS = 2
SSD_STATE = 128
SSD_CONV = 3
SSD_CHUNK = 128
S5_WIDTH = D_MODEL // 4
S5_GROUP = 16
S5_GROUPS = S5_WIDTH // S5_GROUP
S5_STATE = 64
SGU_WIDTH = D_MODEL // 4
SGU_HEAD_DIM = 64
SGU_HEADS = SGU_WIDTH // SGU_HEAD_DIM
SGU_CHUNK = 128
D_FF = ((8 * D_MODEL // 3) + 127) // 128 * 128
FFN_CONV = 3
EPS = 1e-6

SSD_XBC = SSD_WIDTH + 2 * SSD_GROUPS * SSD_STATE
O_Z = 0
O_XBC = O_Z + SSD_WIDTH
O_DT = O_XBC + SSD_XBC
O_S5 = O_DT + 2 * SSD_HEADS
O_SGU = O_S5 + S5_WIDTH
IN_WIDTH = O_SGU + 2 * SGU_WIDTH

kernel_name = "hybrid_bidir_ssd_s5_sgu_encoder"

F32 = jnp.float32


def rmsnorm(x, w):
    xf = x.astype(F32)
    y = xf * lax.rsqrt(jnp.mean(xf * xf, axis=-1, keepdims=True) + EPS) * w.astype(F32)
    return y.astype(x.dtype)


def layernorm(x, w, b):
    xf = x.astype(F32)
    mu = jnp.mean(xf, axis=-1, keepdims=True)
    xc = xf - mu
    var = jnp.mean(xc * xc, axis=-1, keepdims=True)
    return xc * lax.rsqrt(var + 1e-5) * w.astype(F32) + b.astype(F32)


def dwconv_centred(x, w, b):
    k_w = w.shape[0]
    pad = k_w // 2
    seq = x.shape[1]
    xp = jnp.pad(x, ((0, 0), (pad, k_w - 1 - pad), (0, 0)))
    out = b
    for k in range(k_w):
        out = out + xp[:, k:k + seq] * w[k]
    return out


def ssd_scan(x, dt, a, b_in, c_in):
    bsz, seq, n_h, p_dim = x.shape
    n_g, n_s = b_in.shape[2], b_in.shape[3]
    n_r = n_h // n_g
    n_c = seq // SSD_CHUNK
    xr = (x * dt[..., None]).reshape(bsz, n_c, SSD_CHUNK, n_g, n_r, p_dim)
    adt = jnp.moveaxis((dt * a).reshape(bsz, n_c, SSD_CHUNK, n_g, n_r), 2, -1)
    cs = jnp.cumsum(adt, axis=-1)
    br = b_in.reshape(bsz, n_c, SSD_CHUNK, n_g, n_s)
    cr = c_in.reshape(bsz, n_c, SSD_CHUNK, n_g, n_s)
    lower = jnp.tril(jnp.ones((SSD_CHUNK, SSD_CHUNK), dtype=bool))
    diff = cs[..., :, None] - cs[..., None, :]
    lmat = jnp.exp(jnp.where(lower, diff, -jnp.inf))
    cb = jnp.einsum('bclgn,bcsgn->bcgls', cr, br)
    y_diag = jnp.einsum('bcgls,bcgrls,bcsgrp->bclgrp', cb, lmat, xr)
    decay = jnp.exp(cs[..., -1:] - cs)
    states = jnp.einsum('bclgn,bcgrl,bclgrp->bcgrpn', br, decay, xr)
    chunk_decay = jnp.exp(cs[..., -1])

    def step(h, inp):
        s, d = inp
        return h * d[..., None, None] + s, h

    h0 = jnp.zeros((bsz, n_g, n_r, p_dim, n_s), F32)
    _, states_in = lax.scan(step, h0, (jnp.moveaxis(states, 1, 0), jnp.moveaxis(chunk_decay, 1, 0)))
    states_in = jnp.moveaxis(states_in, 0, 1)
    y_off = jnp.einsum('bclgn,bcgrpn,bcgrl->bclgrp', cr, states_in, jnp.exp(cs))
    return (y_diag + y_off).reshape(bsz, seq, n_h, p_dim)


def ssd_mixer(z, xbc, dt_raw, conv_w, conv_b, dt_bias, a_log, d_skip, norm_w):
    bsz, seq, _ = z.shape
    xbc = jax.nn.silu(dwconv_centred(xbc.astype(F32), conv_w.astype(F32), conv_b.astype(F32)))
    gn = SSD_GROUPS * SSD_STATE
    xs = xbc[..., :SSD_WIDTH].reshape(bsz, seq, SSD_HEADS, SSD_HEAD_DIM)
    bs = xbc[..., SSD_WIDTH:SSD_WIDTH + gn].reshape(bsz, seq, SSD_GROUPS, SSD_STATE)
    cs = xbc[..., SSD_WIDTH + gn:].reshape(bsz, seq, SSD_GROUPS, SSD_STATE)
    dt = jax.nn.softplus(dt_raw.astype(F32).reshape(bsz, seq, 2, SSD_HEADS) + dt_bias.astype(F32))
    a = -jnp.exp(a_log.astype(F32))
    flip = lambda t: jnp.flip(t, axis=1)
    y_f = ssd_scan(xs, dt[:, :, 0], a[0], bs, cs)
    y_b = flip(ssd_scan(flip(xs), flip(dt[:, :, 1]), a[1], flip(bs), flip(cs)))
    y = (y_f + y_b + d_skip.astype(F32)[:, None] * xs).reshape(bsz, seq, SSD_WIDTH)
    y = y * jax.nn.silu(z.astype(F32))
    return rmsnorm(y, norm_w)


def complex_linear_combine(e1, e2):
    a1r, a1i, b1r, b1i = e1
    a2r, a2i, b2r, b2i = e2
    return (a2r * a1r - a2i * a1i,
            a2r * a1i + a2i * a1r,
            a2r * b1r - a2i * b1i + b2r,
            a2r * b1i + a2i * b1r + b2i)


def s5_mixer(u, lam_re, lam_im, log_step, b_re, b_im, c_re, c_im, d_skip, glu_w, glu_b):
    bsz, seq, width = u.shape
    uf = u.astype(F32)
    ug = uf.reshape(bsz, seq, S5_GROUPS, S5_GROUP)
    br, bi = b_re.astype(F32), b_im.astype(F32)
    cr, ci = c_re.astype(F32), c_im.astype(F32)
    y = uf * d_skip.astype(F32)
    for k, rev in ((0, False), (1, True)):
        step = jnp.exp(log_step[k].astype(F32))[:, None]
        lr, li = lam_re[k].astype(F32), lam_im[k].astype(F32)
        mag = jnp.exp(lr * step)
        ab_re, ab_im = mag * jnp.cos(li * step), mag * jnp.sin(li * step)
        den = lr * lr + li * li
        f_re = ((ab_re - 1.0) * lr + ab_im * li) / den
        f_im = (ab_im * lr - (ab_re - 1.0) * li) / den
        bb_re = f_re[..., None] * br - f_im[..., None] * bi
        bb_im = f_re[..., None] * bi + f_im[..., None] * br
        bu_re = jnp.einsum('blgi,gpi->blgp', ug, bb_re)
        bu_im = jnp.einsum('blgi,gpi->blgp', ug, bb_im)
        a_re = jnp.broadcast_to(ab_re, bu_re.shape)
        a_im = jnp.broadcast_to(ab_im, bu_im.shape)
        _, _, s_re, s_im = lax.associative_scan(
            complex_linear_combine, (a_re, a_im, bu_re, bu_im), reverse=rev, axis=1)
        y_dir = jnp.einsum('blgp,gip->blgi', s_re, cr) - jnp.einsum('blgp,gip->blgi', s_im, ci)
        y = y + y_dir.reshape(bsz, seq, width)
    y = jax.nn.gelu(y)
    return y * jax.nn.sigmoid(y @ glu_w.astype(F32) + glu_b.astype(F32))


def sgu_mixer(uv, norm_w, norm_b, w_s, b_s):
    bsz, seq, _ = uv.shape
    uv = jax.nn.gelu(uv.astype(F32))
    u, v = uv[..., :SGU_WIDTH], uv[..., SGU_WIDTH:]
    v = layernorm(v, norm_w, norm_b)
    vr = v.reshape(bsz, seq // SGU_CHUNK, SGU_CHUNK, SGU_HEADS, SGU_HEAD_DIM)
    mixed = jnp.einsum('hts,bcshd->bcthd', w_s.astype(F32), vr) + jnp.transpose(b_s.astype(F32))[:, :, None]
    return u * mixed.reshape(bsz, seq, SGU_WIDTH)


def encoder(x, p, norm_mix, w_in, ssd_conv_w, ssd_conv_b, ssd_dt_bias, ssd_a_log, ssd_d, ssd_norm,
            s5_lambda_re, s5_lambda_im, s5_log_step, s5_b_re, s5_b_im, s5_c_re, s5_c_im, s5_d,
            s5_glu_w, s5_glu_b, s5_out_norm, sgu_norm_w, sgu_norm_b, sgu_w, sgu_b, sgu_out_norm,
            w_out, norm_ffn, ffn_w_up, ffn_conv_w, ffn_conv_b, ffn_w_down,
            ple_proj, ple_norm, ple_gate_w, final_norm):
    for i in range(DEPTH):
        h = rmsnorm(x, norm_mix[i])
        proj = h @ w_in[i]
        y_ssd = ssd_mixer(proj[..., O_Z:O_XBC], proj[..., O_XBC:O_DT], proj[..., O_DT:O_S5],
                          ssd_conv_w[i], ssd_conv_b[i], ssd_dt_bias[i], ssd_a_log[i], ssd_d[i], ssd_norm[i])
        y_s5 = rmsnorm(s5_mixer(proj[..., O_S5:O_SGU], s5_lambda_re[i], s5_lambda_im[i], s5_log_step[i],
                                s5_b_re[i], s5_b_im[i], s5_c_re[i], s5_c_im[i], s5_d[i],
                                s5_glu_w[i], s5_glu_b[i]), s5_out_norm[i])
        y_sgu = rmsnorm(sgu_mixer(proj[..., O_SGU:], sgu_norm_w[i], sgu_norm_b[i], sgu_w[i], sgu_b[i]),
                        sgu_out_norm[i])
        mix = jnp.concatenate([y_ssd, y_s5, y_sgu], axis=-1).astype(x.dtype)
        x = x + mix @ w_out[i]
        h = rmsnorm(x, norm_ffn[i])
        up = dwconv_centred(h @ ffn_w_up[i], ffn_conv_w[i], ffn_conv_b[i])
        gate, val = up[..., :D_FF], up[..., D_FF:]
        x = x + (jax.nn.silu(gate) * val) @ ffn_w_down[i]
        e = p[i] @ ple_proj[i]
        g = jax.nn.sigmoid(rmsnorm(x, ple_norm[i]) @ ple_gate_w[i])
        x = x + g * e
    return rmsnorm(x, final_norm)


def setup_inputs(seed: int = 0) -> dict:
    key = jax.random.key(seed)
    keys = jax.random.split(key, 64)
    counter = [0]

    def nk():
        counter[0] += 1
        return keys[counter[0] - 1]

    def nrm(shape, scale):
        return scale * jax.random.normal(nk(), shape, F32)

    def gain(shape):
        return 1.0 + nrm(shape, 0.02)

    def log_uniform(shape, lo, hi):
        return jax.random.uniform(nk(), shape, F32, minval=math.log(lo), maxval=math.log(hi))

    L = DEPTH
    ssd_dt = jnp.exp(log_uniform((L, 2, SSD_HEADS), 1e-3, 1e-1))
    lam_im = jnp.broadcast_to(math.pi * jnp.arange(S5_STATE, dtype=F32), (L, 2, S5_GROUPS, S5_STATE))
    return {
        "x_prompt": nrm((BATCH, SEQ, D_MODEL), 1.0),
        "x_sample": nrm((DEC_BATCH, DEC_SEQ, D_MODEL), 1.0),
        "p_prompt": nrm((DEPTH, BATCH, SEQ, PLE_DIM), 1.0),
        "p_sample": nrm((DEPTH, DEC_BATCH, DEC_SEQ, PLE_DIM), 1.0),
        "norm_mix": gain((L, D_MODEL)),
        "w_in": nrm((L, D_MODEL, IN_WIDTH), D_MODEL ** -0.5),
        "ssd_conv_w": nrm((L, SSD_CONV, SSD_XBC), SSD_CONV ** -0.5),
        "ssd_conv_b": nrm((L, SSD_XBC), 0.01),
        "ssd_dt_bias": ssd_dt + jnp.log(-jnp.expm1(-ssd_dt)),
        "ssd_a_log": jnp.log(jax.random.uniform(nk(), (L, 2, SSD_HEADS), F32, minval=1.0, maxval=16.0)),
        "ssd_d": gain((L, SSD_HEADS)),
        "ssd_norm": gain((L, SSD_WIDTH)),
        "s5_lambda_re": -0.5 + nrm((L, 2, S5_GROUPS, S5_STATE), 0.01),
        "s5_lambda_im": lam_im,
        "s5_log_step": log_uniform((L, 2, S5_GROUPS), 1e-3, 1e-1),
        "s5_b_re": nrm((L, S5_GROUPS, S5_STATE, S5_GROUP), (2 * S5_GROUP) ** -0.5),
        "s5_b_im": nrm((L, S5_GROUPS, S5_STATE, S5_GROUP), (2 * S5_GROUP) ** -0.5),
        "s5_c_re": nrm((L, S5_GROUPS, S5_GROUP, S5_STATE), S5_STATE ** -0.5),
        "s5_c_im": nrm((L, S5_GROUPS, S5_GROUP, S5_STATE), S5_STATE ** -0.5),
        "s5_d": nrm((L, S5_WIDTH), 1.0),
        "s5_glu_w": nrm((L, S5_WIDTH, S5_WIDTH), S5_WIDTH ** -0.5),
        "s5_glu_b": nrm((L, S5_WIDTH), 0.01),
        "s5_out_norm": gain((L, S5_WIDTH)),
        "sgu_norm_w": gain((L, SGU_WIDTH)),
        "sgu_norm_b": nrm((L, SGU_WIDTH), 0.01),
        "sgu_w": nrm((L, SGU_HEADS, SGU_CHUNK, SGU_CHUNK), SGU_CHUNK ** -0.5),
        "sgu_b": gain((L, SGU_HEADS, SGU_CHUNK)),
        "sgu_out_norm": gain((L, SGU_WIDTH)),
        "w_out": nrm((L, D_MODEL, D_MODEL), D_MODEL ** -0.5),
        "norm_ffn": gain((L, D_MODEL)),
        "ffn_w_up": nrm((L, D_MODEL, 2 * D_FF), D_MODEL ** -0.5),
        "ffn_conv_w": nrm((L, FFN_CONV, 2 * D_FF), FFN_CONV ** -0.5),
        "ffn_conv_b": nrm((L, 2 * D_FF), 0.01),
        "ffn_w_down": nrm((L, D_FF, D_MODEL), D_FF ** -0.5),
        "ple_proj": nrm((L, PLE_DIM, D_MODEL), PLE_DIM ** -0.5),
        "ple_norm": gain((L, D_MODEL)),
        "ple_gate_w": nrm((L, D_MODEL, D_MODEL), D_MODEL ** -0.5),
        "final_norm": gain((D_MODEL,)),
    }


def reference(x_prompt, x_sample, p_prompt, p_sample, norm_mix, w_in, ssd_conv_w, ssd_conv_b,
              ssd_dt_bias, ssd_a_log, ssd_d, ssd_norm, s5_lambda_re, s5_lambda_im, s5_log_step,
              s5_b_re, s5_b_im, s5_c_re, s5_c_im, s5_d, s5_glu_w, s5_glu_b, s5_out_norm,
              sgu_norm_w, sgu_norm_b, sgu_w, sgu_b, sgu_out_norm, w_out, norm_ffn, ffn_w_up,
              ffn_conv_w, ffn_conv_b, ffn_w_down, ple_proj, ple_norm, ple_gate_w, final_norm):
    weights = (norm_mix, w_in, ssd_conv_w, ssd_conv_b, ssd_dt_bias, ssd_a_log, ssd_d, ssd_norm,
               s5_lambda_re, s5_lambda_im, s5_log_step, s5_b_re, s5_b_im, s5_c_re, s5_c_im, s5_d,
               s5_glu_w, s5_glu_b, s5_out_norm, sgu_norm_w, sgu_norm_b, sgu_w, sgu_b, sgu_out_norm,
               w_out, norm_ffn, ffn_w_up, ffn_conv_w, ffn_conv_b, ffn_w_down,
               ple_proj, ple_norm, ple_gate_w, final_norm)
    y_prompt = encoder(x_prompt, p_prompt, *weights)
    y_sample = encoder(x_sample, p_sample, *weights)
    return (y_prompt, y_sample)
```

```python
import contextlib
import math
import numpy as np
import ml_dtypes
import concourse.bass as bass
import concourse.mybir as mybir
from concourse.bass_utils import run_bass_kernel_spmd

F32 = mybir.dt.float32
BF16 = mybir.dt.bfloat16
I32 = mybir.dt.int32
AF = mybir.ActivationFunctionType
ALU = mybir.AluOpType

D = 1024
KT = 8
PLE = 256
DFF = 2816
NFF = 44
INW = 2320
EPS = 1e-6
O_Z, O_XBC, O_DT, O_S5, O_SGU = 0, 512, 1536, 1552, 1808


class T:
    __slots__ = ("w", "r", "ps")

    def __init__(self, ps=False):
        self.w = None
        self.r = []
        self.ps = ps


NDMA = 24


import os
STOP_AFTER = int(os.environ["KSTOP"]) if "KSTOP" in os.environ else None
SKIP = os.environ.get("KSKIP", "")


class _Stop(Exception):
    pass


class FW:
    def __init__(self, nc, sem_phases, dsems, needed=None):
        self.nc = nc
        self.e = {"pe": nc.tensor, "act": nc.scalar, "dve": nc.vector, "pool": nc.gpsimd, "sp": nc.sync}
        self.sem_phases = sem_phases
        self.ep = 0
        self.sem = dict(sem_phases[0])
        for j, d in enumerate(dsems):
            self.sem["q%d" % j] = d
        self.idx = {k: 0 for k in self.sem}
        self.val = {k: 0 for k in self.sem}
        self.rank = {k: {} for k in self.sem}
        self.seen = {k: {} for k in self.e}
        self.ndma = 0
        self.rec = None
        self.dry = needed is None
        self.needed = set() if needed is None else needed
        self.ninc = 0

    def _deps(self, R, W, en=None):
        deps = {}
        ep = self.ep
        for t in R:
            if t.w is not None:
                k, c, e = t.w
                if (e == ep or k[0] == "q") and deps.get(k, 0) < c:
                    deps[k] = c
            if t.ps:
                for (k, c, e) in t.r:
                    if k != en and (e == ep or k[0] == "q") and deps.get(k, 0) < c:
                        deps[k] = c
        for t in W:
            if t.w is not None:
                k, c, e = t.w
                if (e == ep or k[0] == "q") and deps.get(k, 0) < c:
                    deps[k] = c
            for (k, c, e) in t.r:
                if (e == ep or k[0] == "q") and deps.get(k, 0) < c:
                    deps[k] = c
        return deps

    def _wait(self, en, k, c):
        if k[0] == "q":
            self.e[en].wait_ge(self.sem[k], c)
        elif self.dry:
            self.needed.add((k, self.ep, c))
            self.e[en].wait_ge(self.sem[k], c)
        else:
            self.e[en].wait_ge(self.sem[k], self.rank[k][c])
        self.seen[en][k] = c

    def op(self, en, fn, R=(), W=(), inc=1):
        if self.rec is not None:
            self.rec.append((en, fn, tuple(R), tuple(W), inc))
            return None
        deps = self._deps(R, W, en)
        sk = en
        if en == "sp":
            sk = "q%d" % (self.ndma % NDMA)
            self.ndma += 1
            c = self.idx[sk]
            if c > 0 and deps.get(sk, 0) < c:
                deps[sk] = c
        seen = self.seen[en]
        for k, c in deps.items():
            if k == "pe" and en == "pe":
                continue
            if seen.get(k, 0) < c:
                self._wait(en, k, c)
        ins = fn(self.e[en])
        if en == "sp":
            self.idx[sk] += inc
            ins.then_inc(self.sem[sk], inc)
        else:
            self.idx[sk] += 1
            i = self.idx[sk]
            if self.dry or (sk, self.ep, i) in self.needed:
                self.val[sk] += 1
                self.rank[sk][i] = self.val[sk]
                ins.then_inc(self.sem[sk], 1)
                self.ninc += 1
        self.last_ins = ins
        tok = (sk, self.idx[sk], self.ep)
        for t in R:
            t.r.append(tok)
            if len(t.r) > 12:
                m = {}
                for (k, c, e) in t.r:
                    if (e == self.ep or k[0] == "q") and m.get(k, (0, 0))[0] < c:
                        m[k] = (c, e)
                t.r = [(k, c, e) for k, (c, e) in m.items()]
        for t in W:
            t.w = tok
            t.r = []
        return ins

    def record(self, f):
        outer = self.rec
        self.rec = []
        f()
        r, self.rec = self.rec, outer
        return r

    def emit_merged(self, a, b):
        ia = ib = 0
        na, nb = len(a), len(b)
        while ia < na or ib < nb:
            if ib >= nb or (ia < na and ia * nb <= ib * na):
                self.op(*a[ia]); ia += 1
            else:
                self.op(*b[ib]); ib += 1

    def barrier(self):
        for en in self.e:
            for k in self.sem:
                if k == en:
                    continue
                c = self.idx[k]
                if c > 0 and self.seen[en].get(k, 0) < c:
                    self._wait(en, k, c)

    def new_phase(self):
        self.barrier()
        if STOP_AFTER is not None and self.ep + 1 >= STOP_AFTER:
            raise _Stop()
        self.ep += 1
        assert self.ep < len(self.sem_phases)
        for k, h in self.sem_phases[self.ep].items():
            self.sem[k] = h
            self.idx[k] = 0
            self.val[k] = 0
            self.rank[k] = {}
            for en in self.seen:
                self.seen[en].pop(k, None)


def build(L, NSEG, SEG):
    needed = None
    for _pass in range(2):
        hold = {}
        try:
            nc = _build(L, NSEG, SEG, hold, needed)
        except _Stop:
            hold["fw"].barrier()
            nc = hold["nc"]
        needed = hold["fw"].needed
        print("pass", _pass, "incs", hold["fw"].ninc)
    return nc


def _build(L, NSEG, SEG, hold, needed):
    NT = NSEG * SEG
    TB = 512
    TS = 256
    TF = 256
    NTB = NT // TB
    NTF = NT // TF
    nc = bass.Bass("TRN2", target_bir_lowering=False)

    def din(name, shape, dt=F32):
        return nc.dram_tensor(name, list(shape), dt, kind="ExternalInput").ap()

    xT = din("xT", [128, NT // 512, KT, 512])
    pT = din("pT", [L, 128, NT // 512, 2, 512])
    cm_d = din("cm", [128, NSEG + 1])
    consts_d = din("consts", [128, 6, 128])
    w_in_d = din("w_in", [L, D, INW])
    w_out_d = din("w_out", [L, D, D])
    w_up_d = din("w_up", [L, D, 2 * DFF])
    w_down_d = din("w_down", [L, DFF, D])
    ple_proj_d = din("ple_proj", [L, PLE, D])
    ple_gate_d = din("ple_gate", [L, D, D])
    glu_w_d = din("glu_w", [L, 256, 256])
    sgu_wT_d = din("sgu_wT", [L, 128, 4, 128])
    nrm_d = din("nrm", [L, 128, 4, 8])
    mixn_d = din("mixn", [L, 128, 8])
    ssdcw_d = din("ssdcw", [L, 128, 8, 4])
    ffncw_d = din("ffncw", [L, 128, NFF, 4])
    dtrow_d = din("dtrow", [L, 3, 16])
    s5vec_d = din("s5vec", [L, 128, 2, 6])
    sgub_d = din("sgub", [L, 128, 4])
    lamS_d = din("lamS", [L, 2, 128, 3, 8])
    lamR_d = din("lamR", [L, 2, 3, 1024])
    bsp_d = din("bsp", [L, 2, 128, 8, 128])
    csp_d = din("csp", [L, 2, 128, 8, 128])
    yT = nc.dram_tensor("yT", [128, NT // 512, KT, 512], F32, kind="ExternalOutput").ap()
    xA = nc.dram_tensor("xA", [128, NT // 512, KT, 512], F32).ap()
    xB = nc.dram_tensor("xB", [128, NT // 512, KT, 512], F32).ap()
    xbcS = nc.dram_tensor("xbcS", [128, NT // 512, KT, 512], BF16).ap()
    uS = nc.dram_tensor("uS", [128, NT // 512, 2, 512], BF16).ap()
    ybS = nc.dram_tensor("ybS", [NT, 512], F32).ap()
    yb5S = nc.dram_tensor("yb5S", [128, NT // 512, 2, 512], F32).ap()

    with contextlib.ExitStack() as top:
        nph = 3 * L + 2
        sem_phases = [{n: top.enter_context(nc.semaphore("%s_%d" % (n, ph))) for n in ("pe", "act", "dve", "pool")} for ph in range(nph)]
        dsems = [top.enter_context(nc.semaphore("dq%d" % j)) for j in range(NDMA)]
        fw = FW(nc, sem_phases, dsems, needed)
        hold["fw"] = fw
        hold["nc"] = nc
        op = fw.op

        _cnt = [0]

        def sbt(st, name, shape, dt):
            _cnt[0] += 1
            return st.enter_context(nc.sbuf_tensor("s%d_%s" % (_cnt[0], name), list(shape), dt))

        def dma(out, in_, R=(), W=()):
            return op("sp", lambda e: e.dma_start(out=out, in_=in_), R, W, inc=16)

        def mm(out, lhsT, rhs, start, stop, R, W):
            return op("pe", lambda e: e.matmul(out, lhsT=lhsT, rhs=rhs, start=start, stop=stop), R, W)

        def tr(out, in_, ident, R, W):
            return op("pe", lambda e: e.transpose(out, in_, ident), R, W)

        def act(out, in_, func, R, W, bias=None, scale=None, accum=None):
            kw = {}
            if bias is not None:
                kw["bias"] = bias
            if scale is not None:
                kw["scale"] = scale
            if accum is not None:
                kw["accum_out"] = accum
            return op("act", lambda e: e.activation(out=out, in_=in_, func=func, **kw), R, W)

        def tt(en, out, in0, in1, o, R, W):
            return op(en, lambda e: e.tensor_tensor(out=out, in0=in0, in1=in1, op=o), R, W)

        def ts(en, out, in0, s1, o0, R, W, s2=None, o1=None):
            if o1 is None:
                return op(en, lambda e: e.tensor_scalar(out=out, in0=in0, scalar1=s1, scalar2=None, op0=o0), R, W)
            return op(en, lambda e: e.tensor_scalar(out=out, in0=in0, scalar1=s1, scalar2=s2, op0=o0, op1=o1), R, W)

        def stt(out, in0, sc, in1, o0, o1, R, W):
            return op("dve", lambda e: e.scalar_tensor_tensor(out=out, in0=in0, scalar=sc, in1=in1, op0=o0, op1=o1), R, W)

        def cp(en, out, in_, R, W):
            if en == "act":
                return act(out, in_, AF.Copy, R, W)
            return op(en, lambda e: e.tensor_copy(out=out, in_=in_), R, W)

        def mset(en, ap, val, W):
            return op(en, lambda e: e.memset(ap, val), (), W)

        cst = sbt(top, "cst", [128, 6, 128], F32); Tcst = T()
        cstb = sbt(top, "cstb", [128, 6, 128], BF16); Tcstb = T()
        cm = sbt(top, "cm", [128, NSEG + 1], F32); Tcm = T()
        epsb = sbt(top, "epsb", [128, 3], F32); Teps = T()
        dma(cst[:], consts_d[:, :, :], W=[Tcst])
        dma(cm[:], cm_d[:, :], W=[Tcm])
        cp("dve", cstb[:], cst[:], [Tcst], [Tcstb])
        mset("dve", epsb[:, 0:1], EPS, [Teps])
        mset("dve", epsb[:, 1:2], 1e-5, [Teps])
        mset("dve", epsb[:, 2:3], 1.0, [Teps])
        ident_b = cstb[:, 0, :]
        triF = cst[:, 1, :]
        triB = cst[:, 2, :]
        UF = cst[:, 3, :]
        UB = cst[:, 4, :]
        ones_f = cst[:, 5, :]
        ones_b = cstb[:, 5, :]
        maskF_b = cstb[:, 1, :]
        maskB_b = cstb[:, 2, :]

        P = []
        TP = []
        for i in range(7):
            P.append(top.enter_context(nc.psum_tensor("P%d" % i, [128, 512], F32)))
            TP.append(T(ps=True))
        PT = top.enter_context(nc.psum_tensor("PT", [128, 1024], BF16)); TPT = T(ps=True)

        TxA = [T() for _ in range(NT // 256)]
        TxB = [T() for _ in range(NT // 256)]
        TxIn = [T() for _ in range(NT // 256)]

        def xtr(TL, t0, n):
            return [TL[b] for b in range(t0 // 256, (t0 + n + 255) // 256)]

        Txbc = [T() for _ in range(NTB)]
        Tu = [T() for _ in range(NTB)]
        Tyb = [T() for _ in range(NT // 128)]
        Tyb5 = [T() for _ in range(NTB)]

        def load_w(stage, Tstage, dst, Tdst, src_kpn, nk, cols, scale_tile=None, Tscale=None, chunk=1024):
            si = 0
            for k in range(nk):
                for c0 in range(0, cols, chunk):
                    c1 = min(cols, c0 + chunk)
                    s, Ts = stage[si % len(stage)], Tstage[si % len(stage)]
                    si += 1
                    dma(s[:, 0:c1 - c0], src_kpn(k, c0, c1), W=[Ts])
                    if scale_tile is None:
                        act(dst[:, k, c0:c1], s[:, 0:c1 - c0], AF.Copy, [Ts], [Tdst])
                    else:
                        act(dst[:, k, c0:c1], s[:, 0:c1 - c0], AF.Identity, [Ts, Tscale], [Tdst], scale=scale_tile[:, k:k + 1])

        def norm_stage(xt, Txt, nkt, sq, Tsq, hT_main, ThT, width, nfeat, rtmp, Trt, pbank=6, eng="pool"):
            act(sq[:, 0:nkt, 0:width], xt, AF.Square, [Txt], [Tsq])
            for k in range(nkt):
                mm(P[pbank][:, 0:width], ones_b, sq[:, k, 0:width], k == 0, k == nkt - 1, [Tcstb, Tsq], [TP[pbank]])
            act(rtmp[:, 0:width], P[pbank][:, 0:width], AF.Sqrt, [TP[pbank], Teps], [Trt], bias=epsb[:, 0:1], scale=1.0 / nfeat)
            op("dve", lambda e: e.reciprocal(out=rtmp[:, 0:width], in_=rtmp[:, 0:width]), [Trt], [Trt])
            tt(eng, hT_main, xt, rtmp[:, 0:width].unsqueeze(1).broadcast_to([128, nkt, width]), ALU.mult, [Txt, Trt], [ThT])

        for l in range(L):
            last_layer = (l == L - 1)
            with contextlib.ExitStack() as lay:
                nrm = sbt(lay, "nrm", [128, 4, 8], F32); Tnrm = T()
                mixn = sbt(lay, "mixn", [128, 8], F32); Tmixn = T()
                ssdcw = sbt(lay, "ssdcw", [128, 8, 4], F32); Tssdcw = T()
                dtrow = sbt(lay, "dtrow", [128, 3, 16], F32); Tdtrow = T()
                arow = sbt(lay, "arow", [128, 16], F32)
                s5vec = sbt(lay, "s5vec", [128, 2, 6], F32); Ts5vec = T()
                sgub = sbt(lay, "sgub", [128, 4], F32); Tsgub = T()
                carry = sbt(lay, "carry", [128, 2, 8], F32); Tcarry = T()
                Hst = sbt(lay, "Hst", [128, 512], F32); THst = T()
                Hbf = sbt(lay, "Hbf", [128, 512], BF16); THbf = T()
                dma(nrm[:], nrm_d[l], W=[Tnrm])
                dma(mixn[:], mixn_d[l], W=[Tmixn])
                dma(ssdcw[:], ssdcw_d[l], W=[Tssdcw])
                dma(dtrow[:].rearrange("p a b -> p (a b)"), dtrow_d[l:l + 1].rearrange("o a b -> o (a b)").partition_broadcast(128), W=[Tdtrow])
                dma(s5vec[:], s5vec_d[l], W=[Ts5vec])
                dma(sgub[:], sgub_d[l], W=[Tsgub])
                act(arow[:], dtrow[:, 1, :], AF.Exp, [Tdtrow], [Tdtrow])
                ts("dve", arow[:], arow[:], -1.0, ALU.mult, [Tdtrow], [Tdtrow])

                for d in (1, 0):
                    with contextlib.ExitStack() as sw:
                        xt1 = sbt(sw, "xt1", [128, KT, TB], F32); Txt1 = T()
                        xt2 = [xt1, xt1]; Txt2 = [Txt1, Txt1]
                        sq = sbt(sw, "sq", [128, KT, TB], BF16); Tsq = T()
                        hT = [sbt(sw, "hT%d" % i, [128, KT, TB + 2], BF16) for i in range(3)]; ThT = [T(), T(), T()]
                        rtmp = sbt(sw, "rtmp", [128, TB], F32); Trt = T()
                        if d == 1:
                            gen = sbt(sw, "gen", [128, 2560], F32)
                            ncol = 1296
                            Win = sbt(sw, "Win", [128, KT, ncol], BF16); TWin = T()
                            cXBC, cDT, cS5 = 0, 1024, 1040
                            if l > 0:
                                Wg = sbt(sw, "Wg", [128, KT, D], BF16); TWg = T()
                                Wp = sbt(sw, "Wp", [128, 2, D], BF16); TWp = T()
                                pt = sbt(sw, "pt", [128, 2, TB], F32); Tpt = T()
                                ptb = sbt(sw, "ptb", [128, 2, TB], BF16); Tptb = T()
                                gsb = sbt(sw, "gsb", [128, TB], F32); Tgsb = T()
                        else:
                            ncol = 1040
                            Win = sbt(sw, "Win", [128, KT, ncol], BF16); TWin = T()
                            cZ, cDT, cSGU = 0, 512, 528
                            Wout = sbt(sw, "Wout", [128, KT, D], BF16); TWout = T()
                            gluw = sbt(sw, "gluw", [128, 2, 256], BF16); Tgluw = T()
                            sguw = sbt(sw, "sguw", [128, 4, 128], BF16); Tsguw = T()
                            mixT = sbt(sw, "mixT", [128, KT, TB], BF16); TmixT = T()
                            gen = sbt(sw, "gen", [128, 2560], F32)
                            uvg = gen[:, 0:4 * TB].rearrange("p (a b) -> p a b", a=4); Tuvg = T()
                            vtok = sbt(sw, "vtok", [128, 256], BF16); Tvtok = T()
                            mixtok = sbt(sw, "mixtok", [128, 256], BF16); Tmixtok = T()
                            ybl = sbt(sw, "ybl", [128, 512], F32); Tybl = T()
                            yb5l = sbt(sw, "yb5l", [128, 2, TB], F32); Tyb5l = T()
                            y5g = sbt(sw, "y5g", [128, 2, TB], BF16); Ty5g = T()
                            ztmp = sbt(sw, "ztmp", [128, 512], F32); Tztmp = T()
                            ssq = sbt(sw, "ssq", [128, 4], F32); Tssq = T()
                            ytok = sbt(sw, "ytok", [128, 512], BF16); Tytok = T()
                        xbc = sbt(sw, "xbc", [128, KT, TB], BF16); Txbcl = T()
                        uT = sbt(sw, "uT", [128, 2, TB], BF16); TuT = T()
                        acc = [sbt(sw, "acc%d" % i, [128, TB], F32) for i in range(2)]; Tacc = [T(), T()]
                        sm = sbt(sw, "sm", [128, 12, 32], F32); Tsm = T()
                        seg = [sbt(sw, "seg%d" % i, [128, 128], F32) for i in range(4)]; Tseg = [T() for _ in range(4)]
                        CBm2 = [sbt(sw, "CBm%d" % i, [128, 128], BF16) for i in range(2)]; TCBm2 = [T(), T()]
                        Lt2 = [sbt(sw, "Lt%d" % i, [128, 512], BF16) for i in range(2)]; TLt2 = [T(), T()]
                        Mt2 = [sbt(sw, "Mt%d" % i, [128, 512], BF16) for i in range(2)]; TMt2 = [T(), T()]
                        xr = sbt(sw, "xr", [128, 512], BF16); Txr = T()
                        xrd = sbt(sw, "xrd", [128, 512], BF16); Txrd = T()
                        Btok = sbt(sw, "Btok", [128, 256], BF16); TBtok = T()
                        ych = sbt(sw, "ych", [128, 512], F32); Tych = T()
                        ytmp = sbt(sw, "ytmp", [128, 512], F32); Tytmp = T()
                        BB = sbt(sw, "BB", [128, 2, 8, 128], BF16); TBB = T()
                        CC = sbt(sw, "CC", [128, 2, 8, 128], BF16); TCC = T()
                        tab = sbt(sw, "tab", [128, 2, 8, TS], F32); Ttab = T()
                        s5s = sbt(sw, "s5s", [128, 8, 8], F32); Ts5s = T()
                        s5w1 = sbt(sw, "s5w1", [128, 8, TS], F32); Ts5w1 = T()
                        s5w2 = sbt(sw, "s5w2", [128, 8, TS], F32); Ts5w2 = T()
                        s5w = [s5w1, s5w2]; Ts5w = [Ts5w1, Ts5w2]
                        Tslot = [[T() for _ in range(8)] for _ in range(2)]
                        Tctmp = T()
                        s5o = [sbt(sw, "s5o%d" % i, [128, 2, TS], BF16) for i in range(2)]; Ts5o = [[T(), T()], [T(), T()]]
                        y5d = sbt(sw, "y5d", [128, 2, TB], F32); Ty5d = T()

                        xflat = xt1[:].rearrange("p a b -> p (a b)")
                        stage = [xflat[:, 1024 * j:1024 * (j + 1)] for j in range(4)]
                        Tstage2 = [T() for _ in range(4)]
                        wsrc = w_in_d[l].rearrange("(k p) n -> p k n", p=128)
                        if d == 1:
                            load_w(stage, Tstage2, Win, TWin, lambda k, c0, c1: wsrc[:, k, O_XBC + c0:O_XBC + c1], KT, 1296, nrm[:, 0, :], Tnrm)
                            if l > 0:
                                gsrc = ple_gate_d[l - 1].rearrange("(k p) n -> p k n", p=128)
                                load_w(stage, Tstage2, Wg, TWg, lambda k, c0, c1: gsrc[:, k, c0:c1], KT, D, nrmprev[:, 2, :], Tnrmprev)
                                psrc = ple_proj_d[l - 1].rearrange("(k p) n -> p k n", p=128)
                                load_w(stage, Tstage2, Wp, TWp, lambda k, c0, c1: psrc[:, k, c0:c1], 2, D)
                        else:
                            load_w(stage, Tstage2, Win[:, :, 0:512], TWin, lambda k, c0, c1: wsrc[:, k, O_Z + c0:O_Z + c1], KT, 512, nrm[:, 0, :], Tnrm)
                            load_w(stage, Tstage2, Win[:, :, 512:528], TWin, lambda k, c0, c1: wsrc[:, k, O_DT + c0:O_DT + c1], KT, 16, nrm[:, 0, :], Tnrm)
                            load_w(stage, Tstage2, Win[:, :, 528:1040], TWin, lambda k, c0, c1: wsrc[:, k, O_SGU + c0:O_SGU + c1], KT, 512, nrm[:, 0, :], Tnrm)
                            osrc = w_out_d[l].rearrange("(k p) n -> p k n", p=128)
                            load_w(stage, Tstage2, Wout, TWout, lambda k, c0, c1: osrc[:, k, c0:c1], KT, D, mixn, Tmixn)
                            gls = glu_w_d[l].rearrange("(k p) n -> p k n", p=128)
                            load_w(stage, Tstage2, gluw, Tgluw, lambda k, c0, c1: gls[:, k, c0:c1], 2, 256)
                            load_w(stage, Tstage2, sguw[:].rearrange("p a b -> p (a b)").unsqueeze(1), Tsguw,
                                   lambda k, c0, c1: sgu_wT_d[l].rearrange("p a b -> p (a b)")[:, c0:c1], 1, 512)

                        with contextlib.ExitStack() as dz:
                            CW = 256
                            lamR = sbt(dz, "lamR", [128, 3, CW], F32); TlamR = T()
                            wkR = gen[:, :].rearrange("p (a b) -> p a b", a=10); TwkR = T()
                            wki = sbt(dz, "wki", [128, CW], I32); Twki = T()
                            lamS = sbt(dz, "lamS", [128, 3, 8], F32); TlamS = T()
                            wkS = sbt(dz, "wkS", [128, 10, 8], F32); TwkS = T()
                            wkiS = sbt(dz, "wkiS", [128, 8], I32); TwkiS = T()
                            bspt = sbt(dz, "bspt", [128, 2, CW], F32); Tbspt = T()
                            dma(lamS[:], lamS_d[l, d], W=[TlamS])

                            def disc(lam, Tl, wk, Tw, wi, Twi):
                                lr, li, ls = lam[:, 0, :], lam[:, 1, :], lam[:, 2, :]
                                R0, W0 = [Tl, Tw], [Tw]
                                act(wk[:, 5, :], ls, AF.Exp, R0, W0)
                                tt("dve", wk[:, 0, :], lr, wk[:, 5, :], ALU.mult, R0, W0)
                                act(wk[:, 0, :], wk[:, 0, :], AF.Exp, R0, W0)
                                tt("dve", wk[:, 6, :], li, wk[:, 5, :], ALU.mult, R0, W0)
                                ts("dve", wk[:, 7, :], wk[:, 6, :], 1.0 / (2 * math.pi), ALU.mult, R0, W0)
                                cp("dve", wi, wk[:, 7, :], R0, [Twi])
                                cp("dve", wk[:, 7, :], wi, [Twi], W0)
                                ts("dve", wk[:, 6, :], wk[:, 6, :], 0.25, ALU.mult, R0, W0)
                                stt(wk[:, 6, :], wk[:, 7, :], -math.pi / 2, wk[:, 6, :], ALU.mult, ALU.add, R0, W0)
                                ts("dve", wk[:, 7, :], wk[:, 6, :], math.pi / 2, ALU.add, R0, W0)
                                act(wk[:, 2, :], wk[:, 6, :], AF.Sin, R0, W0)
                                act(wk[:, 1, :], wk[:, 7, :], AF.Sin, R0, W0)
                                for _ in range(2):
                                    tt("dve", wk[:, 6, :], wk[:, 1, :], wk[:, 1, :], ALU.mult, R0, W0)
                                    tt("dve", wk[:, 7, :], wk[:, 2, :], wk[:, 2, :], ALU.mult, R0, W0)
                                    tt("dve", wk[:, 2, :], wk[:, 2, :], wk[:, 1, :], ALU.mult, R0, W0)
                                    ts("dve", wk[:, 2, :], wk[:, 2, :], 2.0, ALU.mult, R0, W0)
                                    tt("dve", wk[:, 1, :], wk[:, 6, :], wk[:, 7, :], ALU.subtract, R0, W0)
                                tt("dve", wk[:, 6, :], wk[:, 0, :], wk[:, 1, :], ALU.mult, R0, W0)
                                ts("dve", wk[:, 6, :], wk[:, 6, :], -1.0, ALU.add, R0, W0)
                                tt("dve", wk[:, 7, :], wk[:, 0, :], wk[:, 2, :], ALU.mult, R0, W0)
                                tt("dve", wk[:, 8, :], lr, lr, ALU.mult, R0, W0)
                                tt("dve", wk[:, 9, :], li, li, ALU.mult, R0, W0)
                                tt("dve", wk[:, 8, :], wk[:, 8, :], wk[:, 9, :], ALU.add, R0, W0)
                                op("dve", lambda e: e.reciprocal(out=wk[:, 8, :], in_=wk[:, 8, :]), R0, W0)
                                tt("dve", wk[:, 3, :], wk[:, 6, :], lr, ALU.mult, R0, W0)
                                tt("dve", wk[:, 9, :], wk[:, 7, :], li, ALU.mult, R0, W0)
                                tt("dve", wk[:, 3, :], wk[:, 3, :], wk[:, 9, :], ALU.add, R0, W0)
                                tt("dve", wk[:, 3, :], wk[:, 3, :], wk[:, 8, :], ALU.mult, R0, W0)
                                tt("dve", wk[:, 4, :], wk[:, 7, :], lr, ALU.mult, R0, W0)
                                tt("dve", wk[:, 9, :], wk[:, 6, :], li, ALU.mult, R0, W0)
                                tt("dve", wk[:, 4, :], wk[:, 4, :], wk[:, 9, :], ALU.subtract, R0, W0)
                                tt("dve", wk[:, 4, :], wk[:, 4, :], wk[:, 8, :], ALU.mult, R0, W0)

                            disc(lamS, TlamS, wkS, TwkS, wkiS[:], TwkiS)
                            BBf = [BB[:, 0].rearrange("p a b -> p (a b)"), BB[:, 1].rearrange("p a b -> p (a b)")]
                            CCf = [CC[:, 0].rearrange("p a b -> p (a b)"), CC[:, 1].rearrange("p a b -> p (a b)")]
                            for cc in range(1024 // CW):
                                csl = slice(cc * CW, (cc + 1) * CW)
                                dma(lamR[:], lamR_d[l, d:d + 1, :, csl].partition_broadcast(128), R=[TwkR], W=[TlamR])
                                dma(bspt[:, 0, :], bsp_d[l, 0].rearrange("p a b -> p (a b)")[:, csl], R=[TwkR], W=[Tbspt])
                                dma(bspt[:, 1, :], bsp_d[l, 1].rearrange("p a b -> p (a b)")[:, csl], R=[TwkR], W=[Tbspt])
                                disc(lamR, TlamR, wkR, TwkR, wki[:], Twki)
                                R1 = [TwkR, Tbspt]
                                tt("dve", wkR[:, 5, :], wkR[:, 3, :], bspt[:, 0, :], ALU.mult, R1, [TwkR])
                                tt("dve", wkR[:, 6, :], wkR[:, 4, :], bspt[:, 1, :], ALU.mult, R1, [TwkR])
                                tt("dve", BBf[0][:, csl], wkR[:, 5, :], wkR[:, 6, :], ALU.subtract, [TwkR], [TBB])
                                tt("dve", wkR[:, 5, :], wkR[:, 3, :], bspt[:, 1, :], ALU.mult, R1, [TwkR])
                                tt("dve", wkR[:, 6, :], wkR[:, 4, :], bspt[:, 0, :], ALU.mult, R1, [TwkR])
                                tt("dve", BBf[1][:, csl], wkR[:, 5, :], wkR[:, 6, :], ALU.add, [TwkR], [TBB])
                                dma(bspt[:, 0, :], csp_d[l, 0].rearrange("p a b -> p (a b)")[:, csl], R=[TwkR, TBB], W=[Tbspt])
                                dma(bspt[:, 1, :], csp_d[l, 1].rearrange("p a b -> p (a b)")[:, csl], R=[TwkR, TBB], W=[Tbspt])
                                cp("dve", CCf[0][:, csl], bspt[:, 0, :], [Tbspt], [TCC])
                                ts("dve", CCf[1][:, csl], bspt[:, 1, :], -1.0, ALU.mult, [Tbspt], [TCC])
                            cp("dve", s5s[:, 0, :], wkS[:, 0, :], [TwkS], [Ts5s])
                            cp("dve", s5s[:, 1, :], wkS[:, 1, :], [TwkS], [Ts5s])
                            cp("dve", s5s[:, 2, :], wkS[:, 2, :], [TwkS], [Ts5s])
                            cosT, sinT = tab[:, 0], tab[:, 1]
                            p0 = 0 if d == 0 else TS - 1
                            mset("dve", cosT[:, :, p0:p0 + 1], 1.0, [Ttab])
                            mset("dve", sinT[:, :, p0:p0 + 1], 0.0, [Ttab])
                            tA = s5w1
                            tB = y5d[:].rearrange("p a b -> p (a b)").rearrange("p (a b) -> p a b", a=8)
                            n = 1
                            while n < TS:
                                if d == 0:
                                    src = slice(0, n); dst = slice(n, 2 * n)
                                else:
                                    src = slice(TS - n, TS); dst = slice(TS - 2 * n, TS - n)
                                cn = s5s[:, 1, :].unsqueeze(2).broadcast_to([128, 8, n])
                                sn = s5s[:, 2, :].unsqueeze(2).broadcast_to([128, 8, n])
                                Rt = [Ttab, Ts5s, Ts5w1, Ty5d]
                                tt("dve", tA[:, :, 0:n], cosT[:, :, src], cn, ALU.mult, Rt, [Ts5w[0]])
                                tt("dve", tB[:, :, 0:n], sinT[:, :, src], sn, ALU.mult, Rt, [Ty5d])
                                tt("dve", cosT[:, :, dst], tA[:, :, 0:n], tB[:, :, 0:n], ALU.subtract, Rt, [Ttab])
                                tt("dve", tA[:, :, 0:n], sinT[:, :, src], cn, ALU.mult, Rt, [Ts5w[0]])
                                tt("dve", tB[:, :, 0:n], cosT[:, :, src], sn, ALU.mult, Rt, [Ty5d])
                                tt("dve", sinT[:, :, dst], tA[:, :, 0:n], tB[:, :, 0:n], ALU.add, Rt, [Ttab])
                                Rs = [Ts5s]
                                tt("dve", s5s[:, 3, :], s5s[:, 1, :], s5s[:, 1, :], ALU.mult, Rs, Rs)
                                tt("dve", s5s[:, 4, :], s5s[:, 2, :], s5s[:, 2, :], ALU.mult, Rs, Rs)
                                tt("dve", s5s[:, 2, :], s5s[:, 2, :], s5s[:, 1, :], ALU.mult, Rs, Rs)
                                ts("dve", s5s[:, 2, :], s5s[:, 2, :], 2.0, ALU.mult, Rs, Rs)
                                tt("dve", s5s[:, 1, :], s5s[:, 3, :], s5s[:, 4, :], ALU.subtract, Rs, Rs)
                                n *= 2
                        fw.barrier()
                        if l == 1 and fw.dry:
                            print("sbuf remaining sweep d=%d:" % d, nc.sbuf_bytes_remaining)
                        mset("dve", carry[:], 0.0, [Tcarry])
                        mset("dve", Hst[:], 0.0, [THst])
                        mset("pool", Hbf[:], 0.0, [THbf])

                        order = list(range(NTB - 1, -1, -1)) if d == 1 else list(range(NTB))
                        xsrc, Txs = (xT, TxIn) if l == 0 else (xA, TxA)

                        def stageA(i):
                            t0 = i * TB
                            xt, Txt = xt2[i % 2], Txt2[i % 2]
                            h, Th = hT[i % 3], ThT[i % 3]
                            dma(xt[:], xsrc[:, i], R=xtr(Txs, t0, TB), W=[Txt])
                            if d == 1 and l > 0:
                                norm_stage(xt[:], Txt, KT, sq, Tsq, h[:, :, 1:TB + 1], Th, TB, D, rtmp, Trt)
                                dma(pt[:], pT[l - 1][:, i], W=[Tpt])
                                cp("pool", ptb[:], pt[:], [Tpt], [Tptb])
                                for m in range(KT):
                                    pg, pe_ = 2 * (m % 2), 2 * (m % 2) + 1
                                    for k in range(KT):
                                        mm(P[pg][:, 0:TB], Wg[:, k, m * 128:(m + 1) * 128], h[:, k, 1:TB + 1], k == 0, k == KT - 1, [TWg, Th], [TP[pg]])
                                    for k in range(2):
                                        mm(P[pe_][:, 0:TB], Wp[:, k, m * 128:(m + 1) * 128], ptb[:, k, :], k == 0, k == 1, [TWp, Tptb], [TP[pe_]])
                                    act(gsb[:], P[pg][:, 0:TB], AF.Sigmoid, [TP[pg]], [Tgsb])
                                    tt("dve", gsb[:], gsb[:], P[pe_][:, 0:TB], ALU.mult, [Tgsb, TP[pe_]], [Tgsb])
                                    tt("pool", xt[:, m, :], xt[:, m, :], gsb[:], ALU.add, [Txt, Tgsb], [Txt])
                                dma(xA[:, i], xt[:], R=[Txt], W=xtr(TxA, t0, TB))
                            norm_stage(xt[:], Txt, KT, sq, Tsq, h[:, :, 1:TB + 1], Th, TB, D, rtmp, Trt)

                        def halos(i):
                            h, Th = hT[i % 3], ThT[i % 3]
                            t0 = i * TB
                            for side in (0, 1):
                                j = i - 1 if side == 0 else i + 1
                                col = 0 if side == 0 else TB + 1
                                if j < 0 or j >= NTB:
                                    mset("pool", h[:, :, col:col + 1], 0.0, [Th])
                                    continue
                                hn, Tn = hT[j % 3], ThT[j % 3]
                                scol = TB if side == 0 else 1
                                bnd = t0 if side == 0 else t0 + TB
                                if bnd % SEG == 0:
                                    ts("pool", h[:, :, col:col + 1], hn[:, :, scol:scol + 1], cm[:, bnd // SEG:bnd // SEG + 1], ALU.mult, [Tn, Tcm], [Th])
                                else:
                                    cp("pool", h[:, :, col:col + 1], hn[:, :, scol:scol + 1], [Tn], [Th])

                        def dt_tile(h, Th):
                            tri = triB if d == 1 else triF
                            dcol = cDT + 8 * d
                            for c in range(4):
                                hs = slice(1 + c * 128, 1 + (c + 1) * 128)
                                for k in range(KT):
                                    mm(P[4][:, 8 * c:8 * c + 8], h[:, k, hs], Win[:, k, dcol:dcol + 8], k == 0, k == KT - 1, [Th, TWin], [TP[4]])
                            dtb, ab, e1, dt, adt, rcs, d1, ecs, dec, etot, dtd, dsk = [sm[:, j, :] for j in range(12)]
                            S = [Tsm]
                            v3 = lambda ap: ap.rearrange("p (a b) -> p a b", a=4)
                            tt("dve", v3(dtb), v3(P[4][:, 0:32]), dtrow[:, 0, 8 * d:8 * d + 8].unsqueeze(1).broadcast_to([128, 4, 8]), ALU.add, [TP[4], Tdtrow, Tsm], S)
                            act(ab, dtb, AF.Abs, S, S)
                            act(e1, ab, AF.Exp, S, S, scale=-1.0)
                            act(e1, e1, AF.Ln, [Tsm, Teps], S, bias=epsb[:, 2:3])
                            stt(dt, dtb, 0.0, e1, ALU.max, ALU.add, S, S)
                            tt("dve", v3(adt), v3(dt), arow[:, 8 * d:8 * d + 8].unsqueeze(1).broadcast_to([128, 4, 8]), ALU.mult, [Tsm, Tdtrow], S)
                            for c in range(4):
                                mm(P[4][:, 32 + 8 * c:40 + 8 * c], tri, adt[:, 8 * c:8 * c + 8], True, True, [Tcst, Tsm], [TP[4]])
                                mm(P[4][:, 64 + 8 * c:72 + 8 * c], ones_f, adt[:, 8 * c:8 * c + 8], True, True, [Tcst, Tsm], [TP[4]])
                            cp("act", rcs, P[4][:, 32:64], [TP[4], Tsm], S)
                            tt("dve", d1, P[4][:, 64:96], rcs, ALU.subtract, [TP[4], Tsm], S)
                            act(ecs, rcs, AF.Exp, S, S)
                            act(dec, d1, AF.Exp, S, S)
                            act(etot, P[4][:, 64:96], AF.Exp, [TP[4], Tsm], S)
                            tt("dve", dtd, dt, dec, ALU.mult, S, S)

                        def ssd_chunk(i, c, h, Th):
                            t0 = i * TB + c * 128
                            cs = slice(c * 128, (c + 1) * 128)
                            hs = slice(1 + c * 128, 1 + (c + 1) * 128)
                            tri = triB if d == 1 else triF
                            Um = UB if d == 1 else UF
                            maskb = maskB_b if d == 1 else maskF_b
                            bnd = (t0 + 128) if d == 1 else t0
                            if bnd % SEG == 0 and 0 < bnd < NT:
                                ts("pool", Hst[:], Hst[:], cm[:, bnd // SEG:bnd // SEG + 1], ALU.mult, [THst, Tcm], [THst])
                                cp("pool", Hbf[:], Hst[:], [THst], [THbf])
                            dtb, ab, e1, dt, adt, rcs, d1, ecs, dec, etot, dtd, dsk = [sm[:, j, 8 * c:8 * c + 8] for j in range(12)]
                            for j in range(4):
                                tr(PT[:, 128 * j:128 * j + 128], xbc[:, j, cs], ident_b, [Txbcl, Tcstb], [TPT])
                            for g in range(2):
                                tr(PT[:, 512 + 128 * g:640 + 128 * g], xbc[:, 4 + g, cs], ident_b, [Txbcl, Tcstb], [TPT])
                            x3v = PT[:, 0:512].rearrange("p (a b) -> p a b", a=8)
                            tt("dve", xr[:].rearrange("p (a b) -> p a b", a=8), x3v, dt.unsqueeze(2).broadcast_to([128, 8, 64]), ALU.mult, [TPT, Tsm], [Txr])
                            tt("dve", xrd[:].rearrange("p (a b) -> p a b", a=8), x3v, dtd.unsqueeze(2).broadcast_to([128, 8, 64]), ALU.mult, [TPT, Tsm], [Txrd])
                            cp("act", Btok[:], PT[:, 512:768], [TPT], [TBtok])
                            if d == 0:
                                tt("dve", ytmp[:].rearrange("p (a b) -> p a b", a=8), x3v, dtrow[:, 2, 0:8].unsqueeze(2).broadcast_to([128, 8, 64]), ALU.mult, [TPT, Tdtrow], [Tytmp])
                            def grp(g):
                                CBm, TCBm, Lt, TLt, Mt, TMt = CBm2[g], TCBm2[g], Lt2[g], TLt2[g], Mt2[g], TMt2[g]
                                mm(P[4][:, 128 + 128 * g:256 + 128 * g], xbc[:, 4 + g, cs], xbc[:, 6 + g, cs], True, True, [Txbcl], [TP[4]])
                                tt("dve", CBm[:], P[4][:, 128 + 128 * g:256 + 128 * g], maskb, ALU.mult, [TP[4], Tcstb], [TCBm])
                                for hb in range(2):
                                    for h2 in range(2):
                                        hh = 2 * hb + h2
                                        sg, Tsg = seg[2 * g + h2], Tseg[2 * g + h2]
                                        act(sg[:], Um, AF.Identity, [Tcst, Tsm], [Tsg], scale=adt[:, 4 * g + hh:4 * g + hh + 1])
                                        mm(P[6][:, 256 * g + 128 * h2:256 * g + 128 * h2 + 128], sg[:], tri, True, True, [Tsg, Tcst], [TP[6]])
                                    act(Lt[:, 256 * hb:256 * hb + 256], P[6][:, 256 * g:256 * g + 256], AF.Exp, [TP[6]], [TLt])
                                tt("dve", Mt[:].rearrange("p (a b) -> p a b", a=4), Lt[:].rearrange("p (a b) -> p a b", a=4),
                                   CBm[:].unsqueeze(1).broadcast_to([128, 4, 128]), ALU.mult, [TLt, TCBm], [TMt])
                                for hh in range(4):
                                    hd = 4 * g + hh
                                    mm(P[0][:, 64 * hd:64 * hd + 64], Mt[:, 128 * hh:128 * hh + 128], xr[:, 64 * hd:64 * hd + 64], True, True, [TMt, Txr], [TP[0]])
                                mm(P[1][:, 256 * g:256 * g + 256], xbc[:, 6 + g, cs], Hbf[:, 256 * g:256 * g + 256], True, True, [Txbcl, THbf], [TP[1]])
                                mm(P[2][:, 256 * g:256 * g + 256], Btok[:, 128 * g:128 * g + 128], xrd[:, 256 * g:256 * g + 256], True, True, [TBtok, Txrd], [TP[2]])
                            fw.emit_merged(fw.record(lambda: grp(0)), fw.record(lambda: grp(1)))
                            tt("dve", ych[:].rearrange("p (a b) -> p a b", a=8), P[1][:, :].rearrange("p (a b) -> p a b", a=8),
                               ecs.unsqueeze(2).broadcast_to([128, 8, 64]), ALU.mult, [TP[1], Tsm], [Tych])
                            tt("dve", ych[:], ych[:], P[0][:, :], ALU.add, [Tych, TP[0]], [Tych])
                            tt("pool", Hst[:].rearrange("p (a b) -> p a b", a=8), Hst[:].rearrange("p (a b) -> p a b", a=8),
                               etot.unsqueeze(2).broadcast_to([128, 8, 64]), ALU.mult, [THst, Tsm], [THst])
                            tt("dve", Hst[:], Hst[:], P[2][:, :], ALU.add, [THst, TP[2]], [THst])
                            cp("pool", Hbf[:], Hst[:], [THst], [THbf])

                        def s5_step(i, hf, q):
                            t0 = i * TB + hf * TS
                            hsl = slice(hf * TS, (hf + 1) * TS)
                            if q == 0:
                                bnd = (t0 + TS) if d == 1 else t0
                                if bnd % SEG == 0 and 0 < bnd < NT:
                                    ts("dve", carry[:].rearrange("p a b -> p (a b)"), carry[:].rearrange("p a b -> p (a b)"), cm[:, bnd // SEG:bnd // SEG + 1], ALU.mult, [Tcarry, Tcm], [Tcarry])
                            wA, TwA = s5w[q % 2], Ts5w[q % 2]
                            so = s5o[q % 2]
                            Tso0, Tso1 = Ts5o[q % 2]
                            mm(P[3][:, 0:TS], BB[:, 0, q, :], uT[:, q // 4, hsl], True, True, [TBB, TuT], [TP[3]])
                            mm(P[3][:, TS:2 * TS], BB[:, 1, q, :], uT[:, q // 4, hsl], True, True, [TBB, TuT], [TP[3]])
                            bre, bim, t1, t2, mre, mim, t3, t4 = [wA[:, j, :] for j in range(8)]
                            Tbre, Tbim, Tt1, Tt2, Tmre, Tmim, Tt3, Tt4 = Tslot[q % 2]
                            wre, wim = mre, mim
                            cq, sq_ = tab[:, 0, q, :], tab[:, 1, q, :]
                            act(wA[:, 0:2, :], P[3][:, 0:2 * TS].rearrange("p (a b) -> p a b", a=2), AF.Copy, [TP[3]], [Tbre, Tbim])
                            tt("pool", t1, bre, cq, ALU.mult, [Tbre, Ttab], [Tt1])
                            tt("pool", t2, bim, sq_, ALU.mult, [Tbim, Ttab], [Tt2])
                            tt("pool", mre, t1, t2, ALU.add, [Tt1, Tt2], [Tmre])
                            tt("dve", t3, bim, cq, ALU.mult, [Tbim, Ttab], [Tt3])
                            tt("dve", t4, bre, sq_, ALU.mult, [Tbre, Ttab], [Tt4])
                            tt("dve", mim, t3, t4, ALU.subtract, [Tt3, Tt4], [Tmim])
                            rb = s5s[:, 0, q:q + 1].broadcast_to([128, TS])
                            if d == 0:
                                op("dve", lambda e: e.tensor_tensor_scan(out=wre, data0=rb, data1=mre, initial=carry[:, 0, q:q + 1], op0=ALU.mult, op1=ALU.add), [Tmre, Ts5s, Tcarry], [Tmre])
                                op("dve", lambda e: e.tensor_tensor_scan(out=wim, data0=rb, data1=mim, initial=carry[:, 1, q:q + 1], op0=ALU.mult, op1=ALU.add), [Tmim, Ts5s, Tcarry], [Tmim])
                                lastc = TS - 1
                            else:
                                op("dve", lambda e: e.tensor_tensor_scan(out=wre[:, ::-1], data0=rb, data1=mre[:, ::-1], initial=carry[:, 0, q:q + 1], op0=ALU.mult, op1=ALU.add), [Tmre, Ts5s, Tcarry], [Tmre])
                                op("dve", lambda e: e.tensor_tensor_scan(out=wim[:, ::-1], data0=rb, data1=mim[:, ::-1], initial=carry[:, 1, q:q + 1], op0=ALU.mult, op1=ALU.add), [Tmim, Ts5s, Tcarry], [Tmim])
                                lastc = 0
                            c5, s5_ = s5s[:, 1, q:q + 1], s5s[:, 2, q:q + 1]
                            Rc = [Tmre, Tmim, Ts5s, Tcarry]
                            tt("pool", s5s[:, 5, q:q + 1], wre[:, lastc:lastc + 1], c5, ALU.mult, Rc, [Tctmp])
                            tt("pool", s5s[:, 6, q:q + 1], wim[:, lastc:lastc + 1], s5_, ALU.mult, Rc, [Tctmp])
                            tt("pool", carry[:, 0, q:q + 1], s5s[:, 5, q:q + 1], s5s[:, 6, q:q + 1], ALU.subtract, [Tctmp], [Tcarry])
                            tt("pool", s5s[:, 5, q:q + 1], wre[:, lastc:lastc + 1], s5_, ALU.mult, Rc, [Tctmp])
                            tt("pool", s5s[:, 6, q:q + 1], wim[:, lastc:lastc + 1], c5, ALU.mult, Rc, [Tctmp])
                            tt("pool", carry[:, 1, q:q + 1], s5s[:, 5, q:q + 1], s5s[:, 6, q:q + 1], ALU.add, [Tctmp], [Tcarry])
                            tt("pool", t1, wre, cq, ALU.mult, [Tmre, Ttab], [Tt1])
                            tt("pool", t2, wim, sq_, ALU.mult, [Tmim, Ttab], [Tt2])
                            tt("pool", so[:, 0, :], t1, t2, ALU.subtract, [Tt1, Tt2], [Tso0])
                            tt("dve", t3, wre, sq_, ALU.mult, [Tmre, Ttab], [Tt3])
                            tt("dve", t4, wim, cq, ALU.mult, [Tmim, Ttab], [Tt4])
                            tt("dve", so[:, 1, :], t3, t4, ALU.add, [Tt3, Tt4], [Tso1])
                            ut = q // 4
                            pcs = slice(ut * TS, (ut + 1) * TS)
                            mm(P[5][:, pcs], CC[:, 0, q, :], so[:, 0, :], q % 4 == 0, False, [TCC, Tso0], [TP[5]])
                            mm(P[5][:, pcs], CC[:, 1, q, :], so[:, 1, :], False, q % 4 == 3, [TCC, Tso1], [TP[5]])
                            if q % 4 == 3:
                                cp("act", y5d[:, ut, hsl], P[5][:, pcs], [TP[5]], [Ty5d])

                        def s5_steps(i):
                            halves = (1, 0) if d == 1 else (0, 1)
                            return [(hf, q) for hf in halves for q in range(8)]

                        if d == 1:
                            stageA(order[0])
                        for oi, i in enumerate(order):
                            t0 = i * TB
                            if d == 1:
                                if oi + 1 < len(order):
                                    stageA(order[oi + 1])
                                halos(i)
                            else:
                                stageA(i)
                            h, Th = hT[i % 3], ThT[i % 3]
                            xt, Txt = xt2[i % 2], Txt2[i % 2]
                            if d == 1:
                                def convtile(j):
                                    pm = j % 3
                                    ph = 3 + (j % 2)
                                    a = acc[j % 2]; Ta = Tacc[j % 2]
                                    for k in range(KT):
                                        mm(P[pm][:, 0:TB], Win[:, k, cXBC + 128 * j:cXBC + 128 * j + 128], h[:, k, 1:TB + 1], k == 0, k == KT - 1, [TWin, Th], [TP[pm]])
                                    for k in range(KT):
                                        mm(P[ph][:, 2 * j:2 * j + 2], Win[:, k, cXBC + 128 * j:cXBC + 128 * j + 128], h[:, k, 0:TB + 2:TB + 1], k == 0, k == KT - 1, [TWin, Th], [TP[ph]])
                                    w0, w1, w2, bb = [ssdcw[:, j, z:z + 1] for z in range(4)]
                                    act(a[:], P[pm][:, 0:TB], AF.Identity, [TP[pm], Tssdcw], [Ta], bias=bb, scale=w1)
                                    stt(a[:, 1:TB], P[pm][:, 0:TB - 1], w0, a[:, 1:TB], ALU.mult, ALU.add, [TP[pm], Ta, Tssdcw], [Ta])
                                    stt(a[:, 0:TB - 1], P[pm][:, 1:TB], w2, a[:, 0:TB - 1], ALU.mult, ALU.add, [TP[pm], Ta, Tssdcw], [Ta])
                                    stt(a[:, 0:1], P[ph][:, 2 * j:2 * j + 1], w0, a[:, 0:1], ALU.mult, ALU.add, [TP[ph], Ta, Tssdcw], [Ta])
                                    stt(a[:, TB - 1:TB], P[ph][:, 2 * j + 1:2 * j + 2], w2, a[:, TB - 1:TB], ALU.mult, ALU.add, [TP[ph], Ta, Tssdcw], [Ta])
                                    act(xbc[:, j, :], a[:], AF.Silu, [Ta], [Txbcl])
                                for j in range(0, 8, 2):
                                    fw.emit_merged(fw.record(lambda: convtile(j)), fw.record(lambda: convtile(j + 1)))
                                for j in range(2):
                                    pm = j % 3
                                    for k in range(KT):
                                        mm(P[pm][:, 0:TB], Win[:, k, cS5 + 128 * j:cS5 + 128 * j + 128], h[:, k, 1:TB + 1], k == 0, k == KT - 1, [TWin, Th], [TP[pm]])
                                    cp("act", uT[:, j, :], P[pm][:, 0:TB], [TP[pm]], [TuT])
                                dma(xbcS[:, i], xbc[:], R=[Txbcl], W=[Txbc[i]])
                                dma(uS[:, i], uT[:], R=[TuT], W=[Tu[i]])
                                steps = s5_steps(i)
                                dt_tile(h, Th)
                                for ci, c in enumerate((3, 2, 1, 0)):
                                    def chainA(c=c):
                                        ssd_chunk(i, c, h, Th)
                                        dma(ybS[t0 + c * 128:t0 + (c + 1) * 128, :], ych[:], R=[Tych], W=[Tyb[(t0 + c * 128) // 128]])

                                    def chainB(ci=ci):
                                        for (hf, q) in steps[4 * ci:4 * ci + 4]:
                                            s5_step(i, hf, q)
                                    fw.emit_merged(fw.record(chainA), fw.record(chainB))
                                dma(yb5S[:, i], y5d[:], R=[Ty5d], W=[Tyb5[i]])
                            else:
                                dma(xbc[:], xbcS[:, i], R=[Txbc[i]], W=[Txbcl])
                                dma(uT[:], uS[:, i], R=[Tu[i]], W=[TuT])
                                dma(yb5l[:], yb5S[:, i], R=[Tyb5[i]], W=[Tyb5l])
                                for j in range(4):
                                    pm = j % 3
                                    for k in range(KT):
                                        mm(P[pm][:, 0:TB], Win[:, k, cSGU + 128 * j:cSGU + 128 * j + 128], h[:, k, 1:TB + 1], k == 0, k == KT - 1, [TWin, Th], [TP[pm]])
                                    act(uvg[:, j, :], P[pm][:, 0:TB], AF.Gelu_apprx_tanh, [TP[pm]], [Tuvg])
                                act(sq[:, 0:2, :], uvg[:, 2:4, :], AF.Square, [Tuvg], [Tsq])
                                cp("pool", sq[:, 2:4, :], uvg[:, 2:4, :], [Tuvg], [Tsq])
                                for k in range(2):
                                    mm(P[3][:, 0:TB], ones_b, sq[:, 2 + k, :], k == 0, k == 1, [Tcstb, Tsq], [TP[3]])
                                for k in range(2):
                                    mm(P[6][:, 0:TB], ones_b, sq[:, k, :], k == 0, k == 1, [Tcstb, Tsq], [TP[6]])
                                a0, a1 = acc[0], acc[1]
                                ts("dve", a0[:], P[3][:, 0:TB], 1.0 / 256, ALU.mult, [TP[3]], [Tacc[0]])
                                tt("dve", a1[:], a0[:], a0[:], ALU.mult, [Tacc[0]], [Tacc[1]])
                                stt(a1[:], P[6][:, 0:TB], 1.0 / 256, a1[:], ALU.mult, ALU.subtract, [TP[6], Tacc[1]], [Tacc[1]])
                                act(a1[:], a1[:], AF.Sqrt, [Tacc[1], Teps], [Tacc[1]], bias=epsb[:, 1:2], scale=1.0)
                                op("dve", lambda e: e.reciprocal(out=a1[:], in_=a1[:]), [Tacc[1]], [Tacc[1]])
                                for k in range(2):
                                    tt("dve", uvg[:, 2 + k, :], uvg[:, 2 + k, :], a0[:], ALU.subtract, [Tuvg, Tacc[0]], [Tuvg])
                                    tt("dve", uvg[:, 2 + k, :], uvg[:, 2 + k, :], a1[:], ALU.mult, [Tuvg, Tacc[1]], [Tuvg])
                                    act(sq[:, 4 + k, :], uvg[:, 2 + k, :], AF.Identity, [Tuvg, Ts5vec], [Tsq], bias=s5vec[:, k, 3:4], scale=s5vec[:, k, 2:3])
                                steps = s5_steps(i)
                                dt_tile(h, Th)
                                for c in range(4):
                                    def chainA(c=c):
                                        cs = slice(c * 128, (c + 1) * 128)
                                        hs = slice(1 + c * 128, 1 + (c + 1) * 128)
                                        if "chunk" in SKIP:
                                            return
                                        dma(ybl[:], ybS[t0 + c * 128:t0 + (c + 1) * 128, :], R=[Tyb[(t0 + c * 128) // 128]], W=[Tybl])
                                        ssd_chunk(i, c, h, Th)
                                        tt("pool", ych[:], ych[:], ybl[:], ALU.add, [Tych, Tybl], [Tych])
                                        tt("pool", ych[:], ych[:], ytmp[:], ALU.add, [Tych, Tytmp], [Tych])
                                        for k in range(KT):
                                            mm(P[1][:, :], h[:, k, hs], Win[:, k, cZ:cZ + 512], k == 0, k == KT - 1, [Th, TWin], [TP[1]])
                                        act(ztmp[:], P[1][:, :], AF.Silu, [TP[1]], [Tztmp])
                                        tt("dve", ych[:], ych[:], ztmp[:], ALU.mult, [Tych, Tztmp], [Tych])
                                        mset("dve", ssq[:, 0:1], 0.0, [Tssq])
                                        act(ztmp[:], ych[:], AF.Square, [Tych, Tssq], [Tztmp, Tssq], accum=ssq[:, 0:1])
                                        act(ssq[:, 1:2], ssq[:, 0:1], AF.Sqrt, [Tssq, Teps], [Tssq], bias=epsb[:, 0:1], scale=1.0 / 512)
                                        op("dve", lambda e: e.reciprocal(out=ssq[:, 2:3], in_=ssq[:, 1:2]), [Tssq], [Tssq])
                                        act(ytok[:], ych[:], AF.Identity, [Tych, Tssq], [Tytok], scale=ssq[:, 2:3])
                                        for j in range(4):
                                            tr(PT[:, 128 * j:128 * j + 128], ytok[:, 128 * j:128 * j + 128], ident_b, [Tytok, Tcstb], [TPT])
                                        for j in range(4):
                                            cp("act", mixT[:, j, cs], PT[:, 128 * j:128 * j + 128], [TPT], [TmixT])
                                        if "sgu" in SKIP:
                                            return
                                        for j in range(2 if "notr" not in SKIP else 0):
                                            if "sgx" in SKIP:
                                                tr(PT[:, 512 + 128 * j:640 + 128 * j], xbc[:, 4 + j, cs], ident_b, [Txbcl, Tcstb], [TPT])
                                            else:
                                                tr(PT[:, 512 + 128 * j:640 + 128 * j], sq[:, 4 + j, cs], ident_b, [Tsq, Tcstb], [TPT])
                                        if "nocp" not in SKIP:
                                            cp("dve", vtok[:], PT[:, 512:768], [TPT], [Tvtok])
                                        if "sg1" in SKIP:
                                            return
                                        for hh in range(4):
                                            mm(P[6][:, 64 * hh:64 * hh + 64], sguw[:, hh, :], vtok[:, 64 * hh:64 * hh + 64], True, True, [Tsguw, Tvtok], [TP[6]])
                                        tt("dve", mixtok[:].rearrange("p (a b) -> p a b", a=4), P[6][:, 0:256].rearrange("p (a b) -> p a b", a=4),
                                           sgub[:].unsqueeze(2).broadcast_to([128, 4, 64]), ALU.add, [TP[6], Tsgub], [Tmixtok])
                                        if "sg2" in SKIP:
                                            return
                                        for j in range(2):
                                            tr(PT[:, 768 + 128 * j:896 + 128 * j], mixtok[:, 128 * j:128 * j + 128], ident_b, [Tmixtok, Tcstb], [TPT])
                                        if "sg3" in SKIP:
                                            return
                                        for j in range(2):
                                            tt("dve", uvg[:, j, cs], uvg[:, j, cs], PT[:, 768 + 128 * j:896 + 128 * j], ALU.mult, [Tuvg, TPT], [Tuvg])

                                    def chainB(c=c):
                                        for (hf, q) in steps[4 * c:4 * c + 4]:
                                            s5_step(i, hf, q)
                                    fw.emit_merged(fw.record(chainA), fw.record(chainB))
                                norm_stage(uvg[:, 0:2, :], Tuvg, 2, sq, Tsq, mixT[:, 6:8, :], TmixT, TB, 256, rtmp, Trt, pbank=3, eng="pool")
                                y5, Ty5 = y5d, Ty5d
                                for k in range(2):
                                    tt("pool", y5[:, k, :], y5d[:, k, :], yb5l[:, k, :], ALU.add, [Ty5d, Tyb5l], [Ty5])
                                    cp("pool", acc[k][:], uT[:, k, :], [TuT], [Tacc[k]])
                                    stt(y5[:, k, :], acc[k][:], s5vec[:, k, 0:1], y5[:, k, :], ALU.mult, ALU.add, [Tacc[k], Ts5vec, Ty5], [Ty5])
                                    act(y5[:, k, :], y5[:, k, :], AF.Gelu_apprx_tanh, [Ty5], [Ty5])
                                cp("pool", y5g[:], y5[:], [Ty5], [Ty5g])
                                for m in range(2):
                                    for k in range(2):
                                        mm(P[m][:, 0:TB], gluw[:, k, 128 * m:128 * m + 128], y5g[:, k, :], k == 0, k == 1, [Tgluw, Ty5g], [TP[m]])
                                    act(acc[m][:], P[m][:, 0:TB], AF.Sigmoid, [TP[m], Ts5vec], [Tacc[m]], bias=s5vec[:, m, 1:2], scale=1.0)
                                    tt("dve", y5[:, m, :], y5[:, m, :], acc[m][:], ALU.mult, [Ty5, Tacc[m]], [Ty5])
                                norm_stage(y5[:], Ty5, 2, sq, Tsq, mixT[:, 4:6, :], TmixT, TB, 256, rtmp, Trt, pbank=3, eng="pool")
                                for m in range(KT if "out" not in SKIP else 0):
                                    pm = m % 3
                                    for k in range(KT):
                                        mm(P[pm][:, 0:TB], Wout[:, k, 128 * m:128 * m + 128], mixT[:, k, :], k == 0, k == KT - 1, [TWout, TmixT], [TP[pm]])
                                    tt("dve", xt[:, m, :], xt[:, m, :], P[pm][:, 0:TB], ALU.add, [Txt, TP[pm]], [Txt])
                                dma(xB[:, i], xt[:], R=[Txt], W=xtr(TxB, t0, TB))
                        fw.new_phase()
                with contextlib.ExitStack() as sw:
                    ffncw = sbt(sw, "ffncw", [128, NFF, 4], F32); Tffncw = T()
                    dma(ffncw[:], ffncw_d[l], W=[Tffncw])
                    Wup = sbt(sw, "Wup", [128, KT, 2 * DFF], BF16); TWup = T()
                    Wdn = sbt(sw, "Wdn", [128, 22, D], BF16); TWdn = T()
                    xt2 = [sbt(sw, "fx%d" % i, [128, KT, TF], F32) for i in range(2)]; Txt2 = [T(), T()]
                    sq = sbt(sw, "fsq", [128, KT, TF], BF16); Tsq = T()
                    hT = [sbt(sw, "fh%d" % i, [128, KT, TF + 2], BF16) for i in range(3)]; ThT = [T(), T(), T()]
                    rtmp = sbt(sw, "frt", [128, TF], F32); Trt = T()
                    actb = sbt(sw, "actb", [128, 22, TF], BF16); Tactb = T()
                    acc = [sbt(sw, "facc%d" % i, [128, TF], F32) for i in range(6)]; Tacc = [T() for _ in range(6)]
                    stg = [sbt(sw, "stg%d" % i, [128, 1024], F32) for i in range(4)]; Tstg = [T() for _ in range(4)]
                    usrc = w_up_d[l].rearrange("(k p) n -> p k n", p=128)
                    load_w(stg, Tstg, Wup, TWup, lambda k, c0, c1: usrc[:, k, c0:c1], KT, 2 * DFF, nrm[:, 1, :], Tnrm)
                    dsrc = w_down_d[l].rearrange("(k p) n -> p k n", p=128)
                    load_w(stg, Tstg, Wdn, TWdn, lambda k, c0, c1: dsrc[:, k, c0:c1], 22, D)
                    if l == 1 and fw.dry:
                        print("sbuf remaining ffn:", nc.sbuf_bytes_remaining)

                    def stageAF(i):
                        t0 = i * TF
                        xt, Txt = xt2[i % 2], Txt2[i % 2]
                        for k in range(KT):
                            dma(xt[:, k, :], xB[:, t0 // 512, k, t0 % 512:t0 % 512 + TF], R=xtr(TxB, t0, TF), W=[Txt])
                        norm_stage(xt[:], Txt, KT, sq, Tsq, hT[i % 3][:, :, 1:TF + 1], ThT[i % 3], TF, D, rtmp, Trt)

                    def halosF(i):
                        h, Th = hT[i % 3], ThT[i % 3]
                        t0 = i * TF
                        for side in (0, 1):
                            j = i - 1 if side == 0 else i + 1
                            col = 0 if side == 0 else TF + 1
                            if j < 0 or j >= NTF:
                                mset("pool", h[:, :, col:col + 1], 0.0, [Th])
                                continue
                            hn, Tn = hT[j % 3], ThT[j % 3]
                            scol = TF if side == 0 else 1
                            bnd = t0 if side == 0 else t0 + TF
                            if bnd % SEG == 0:
                                ts("pool", h[:, :, col:col + 1], hn[:, :, scol:scol + 1], cm[:, bnd // SEG:bnd // SEG + 1], ALU.mult, [Tn, Tcm], [Th])
                            else:
                                cp("pool", h[:, :, col:col + 1], hn[:, :, scol:scol + 1], [Tn], [Th])

                    stageAF(0)
                    for i in range(NTF):
                        t0 = i * TF
                        if i + 1 < NTF:
                            stageAF(i + 1)
                        halosF(i)
                        h, Th = hT[i % 3], ThT[i % 3]
                        xt, Txt = xt2[i % 2], Txt2[i % 2]
                        for jj in range(22):
                            accs = [None, None]

                            def halfops(half, jj=jj):
                                j = jj + 22 * half
                                pm = (0, 1, 2, 3, 4, 5, 6)[(2 * jj + half) % 7]
                                a = acc[2 * (jj % 3) + half]; Ta = Tacc[2 * (jj % 3) + half]
                                for k in range(KT):
                                    mm(P[pm][:, 0:TF + 2], Wup[:, k, 128 * j:128 * j + 128], h[:, k, 0:TF + 2], k == 0, k == KT - 1, [TWup, Th], [TP[pm]])
                                w0, w1, w2, bb = [ffncw[:, j, z:z + 1] for z in range(4)]
                                act(a[:], P[pm][:, 1:TF + 1], AF.Identity, [TP[pm], Tffncw], [Ta], bias=bb, scale=w1)
                                stt(a[:], P[pm][:, 0:TF], w0, a[:], ALU.mult, ALU.add, [TP[pm], Ta, Tffncw], [Ta])
                                stt(a[:], P[pm][:, 2:TF + 2], w2, a[:], ALU.mult, ALU.add, [TP[pm], Ta, Tffncw], [Ta])
                                accs[half] = (a, Ta)
                            fw.emit_merged(fw.record(lambda: halfops(0)), fw.record(lambda: halfops(1)))
                            (ag, Tag), (av, Tav) = accs
                            act(ag[:], ag[:], AF.Silu, [Tag], [Tag])
                            tt("pool", actb[:, jj, :], ag[:], av[:], ALU.mult, [Tag, Tav], [Tactb])
                        for m in range(KT):
                            pm = 5 + (m % 2)
                            for k in range(22):
                                mm(P[pm][:, 0:TF], Wdn[:, k, 128 * m:128 * m + 128], actb[:, k, :], k == 0, k == 21, [TWdn, Tactb], [TP[pm]])
                            tt("dve", xt[:, m, :], xt[:, m, :], P[pm][:, 0:TF], ALU.add, [Txt, TP[pm]], [Txt])
                            dma(xA[:, t0 // 512, m, t0 % 512:t0 % 512 + TF], xt[:, m, :], R=[Txt], W=xtr(TxA, t0, TF))
                    fw.new_phase()
                nrmprev_keep = None
            if not last_layer:
                nrmprev = sbt(top, "nrmprev", [128, 4, 8], F32); Tnrmprev = T()
                dma(nrmprev[:], nrm_d[l], W=[Tnrmprev])

        with contextlib.ExitStack() as sw:
            TE = 512
            nrm = sbt(sw, "enrm", [128, 4, 8], F32); Tnrm = T()
            dma(nrm[:], nrm_d[L - 1], W=[Tnrm])
            Wg = sbt(sw, "eWg", [128, KT, D], BF16); TWg = T()
            Wp = sbt(sw, "eWp", [128, 2, D], BF16); TWp = T()
            stg = [sbt(sw, "estg%d" % i, [128, 1024], F32) for i in range(4)]; Tstg = [T() for _ in range(4)]
            gsrc = ple_gate_d[L - 1].rearrange("(k p) n -> p k n", p=128)
            load_w(stg, Tstg, Wg, TWg, lambda k, c0, c1: gsrc[:, k, c0:c1], KT, D, nrm[:, 2, :], Tnrm)
            psrc = ple_proj_d[L - 1].rearrange("(k p) n -> p k n", p=128)
            load_w(stg, Tstg, Wp, TWp, lambda k, c0, c1: psrc[:, k, c0:c1], 2, D)
            xt2 = [sbt(sw, "ex%d" % i, [128, KT, TE], F32) for i in range(2)]; Txt2 = [T(), T()]
            sq = sbt(sw, "esq", [128, KT, TE], BF16); Tsq = T()
            hb = [sbt(sw, "eh%d" % i, [128, KT, TE], BF16) for i in range(2)]; Thb = [T(), T()]
            rtmp = sbt(sw, "ert", [128, TE], F32); Trt = T()
            pt = sbt(sw, "ept", [128, 2, TE], F32); Tpt = T()
            ptb = sbt(sw, "eptb", [128, 2, TE], BF16); Tptb = T()
            gsb = sbt(sw, "egsb", [128, TE], F32); Tgsb = T()
            ot = [sbt(sw, "eo%d" % i, [128, KT, TE], F32) for i in range(2)]; Tot = [T(), T()]
            for i in range(NT // TE):
                t0 = i * TE
                xt, Txt = xt2[i % 2], Txt2[i % 2]
                h, Th = hb[i % 2], Thb[i % 2]
                o, To = ot[i % 2], Tot[i % 2]
                dma(xt[:], xA[:, i], R=xtr(TxA, t0, TE), W=[Txt])
                norm_stage(xt[:], Txt, KT, sq, Tsq, h[:], Th, TE, D, rtmp, Trt)
                dma(pt[:], pT[L - 1][:, i], W=[Tpt])
                cp("pool", ptb[:], pt[:], [Tpt], [Tptb])
                for m in range(KT):
                    pg, pe_ = 2 * (m % 2), 2 * (m % 2) + 1
                    for k in range(KT):
                        mm(P[pg][:, 0:TE], Wg[:, k, m * 128:(m + 1) * 128], h[:, k, :], k == 0, k == KT - 1, [TWg, Th], [TP[pg]])
                    for k in range(2):
                        mm(P[pe_][:, 0:TE], Wp[:, k, m * 128:(m + 1) * 128], ptb[:, k, :], k == 0, k == 1, [TWp, Tptb], [TP[pe_]])
                    act(gsb[:], P[pg][:, 0:TE], AF.Sigmoid, [TP[pg]], [Tgsb])
                    tt("dve", gsb[:], gsb[:], P[pe_][:, 0:TE], ALU.mult, [Tgsb, TP[pe_]], [Tgsb])
                    tt("pool", xt[:, m, :], xt[:, m, :], gsb[:], ALU.add, [Txt, Tgsb], [Txt])
                act(sq[:], xt[:], AF.Square, [Txt], [Tsq])
                for k in range(KT):
                    mm(P[6][:, 0:TE], ones_b, sq[:, k, :], k == 0, k == KT - 1, [Tcstb, Tsq], [TP[6]])
                act(rtmp[:], P[6][:, 0:TE], AF.Sqrt, [TP[6], Teps], [Trt], bias=epsb[:, 0:1], scale=1.0 / D)
                op("dve", lambda e: e.reciprocal(out=rtmp[:], in_=rtmp[:]), [Trt], [Trt])
                for k in range(KT):
                    stt(o[:, k, :], xt[:, k, :], nrm[:, 3, k:k + 1], rtmp[:], ALU.mult, ALU.mult, [Txt, Tnrm, Trt], [To])
                dma(yT[:, i], o[:], R=[To])
        fw.barrier()
        print("FW idx", {k: v for k, v in fw.idx.items() if k[0] != "q"}, "dma", fw.ndma, "ep", fw.ep)
    return nc


def _prep_weights(inp, L):
    f = lambda a: np.ascontiguousarray(np.asarray(a, dtype=np.float32))
    w = {}
    w["w_in"] = f(inp["w_in"][:L]); w["w_out"] = f(inp["w_out"][:L]); w["w_up"] = f(inp["ffn_w_up"][:L])
    w["w_down"] = f(inp["ffn_w_down"][:L]); w["ple_proj"] = f(inp["ple_proj"][:L]); w["ple_gate"] = f(inp["ple_gate_w"][:L])
    w["glu_w"] = f(inp["s5_glu_w"][:L])
    w["sgu_wT"] = f(np.transpose(np.asarray(inp["sgu_w"][:L]), (0, 3, 1, 2)))
    fin = np.broadcast_to(np.asarray(inp["final_norm"])[None], (L, D))
    nr = np.stack([inp["norm_mix"][:L], inp["norm_ffn"][:L], inp["ple_norm"][:L], fin], 1)
    w["nrm"] = f(nr.reshape(L, 4, 8, 128).transpose(0, 3, 1, 2))
    mx = np.concatenate([inp["ssd_norm"][:L], inp["s5_out_norm"][:L], inp["sgu_out_norm"][:L]], 1)
    w["mixn"] = f(mx.reshape(L, 8, 128).transpose(0, 2, 1))
    cw = np.concatenate([np.asarray(inp["ssd_conv_w"][:L]), np.asarray(inp["ssd_conv_b"][:L])[:, None]], 1)
    w["ssdcw"] = f(cw.reshape(L, 4, 8, 128).transpose(0, 3, 2, 1))
    fc = np.concatenate([np.asarray(inp["ffn_conv_w"][:L]), np.asarray(inp["ffn_conv_b"][:L])[:, None]], 1)
    w["ffncw"] = f(fc.reshape(L, 4, NFF, 128).transpose(0, 3, 2, 1))
    dr = np.zeros((L, 3, 16), np.float32)
    dr[:, 0] = np.asarray(inp["ssd_dt_bias"][:L]).reshape(L, 16)
    dr[:, 1] = np.asarray(inp["ssd_a_log"][:L]).reshape(L, 16)
    dr[:, 2, :8] = np.asarray(inp["ssd_d"][:L])
    w["dtrow"] = dr
    sv = np.zeros((L, 6, 256), np.float32)
    sv[:, 0] = inp["s5_d"][:L]; sv[:, 1] = inp["s5_glu_b"][:L]; sv[:, 2] = inp["sgu_norm_w"][:L]; sv[:, 3] = inp["sgu_norm_b"][:L]
    w["s5vec"] = f(sv.reshape(L, 6, 2, 128).transpose(0, 3, 2, 1))
    w["sgub"] = f(np.transpose(np.asarray(inp["sgu_b"][:L]), (0, 2, 1)))
    lam = np.stack([np.asarray(inp["s5_lambda_re"][:L]), np.asarray(inp["s5_lambda_im"][:L]),
                    np.broadcast_to(np.asarray(inp["s5_log_step"][:L])[..., None], (L, 2, 16, 64))], 2)
    ls = lam.reshape(L, 2, 3, 8, 2, 64).transpose(0, 1, 4, 5, 2, 3).reshape(L, 2, 128, 3, 8)
    w["lamS"] = f(ls)
    w["lamR"] = f(lam.reshape(L, 2, 3, 1024))
    bsp = np.zeros((L, 2, 128, 8, 128), np.float32)
    csp = np.zeros((L, 2, 128, 8, 128), np.float32)
    for ri, (bn, cn) in enumerate((("s5_b_re", "s5_c_re"), ("s5_b_im", "s5_c_im"))):
        b = np.asarray(inp[bn][:L]); c = np.asarray(inp[cn][:L])
        for g in range(16):
            q, hh, r0 = g // 2, (g % 2) * 64, (g % 8) * 16
            bsp[:, ri, r0:r0 + 16, q, hh:hh + 64] = np.transpose(b[:, g], (0, 2, 1))
            csp[:, ri, hh:hh + 64, q, r0:r0 + 16] = np.transpose(c[:, g], (0, 2, 1))
    w["bsp"] = bsp; w["csp"] = csp
    cst = np.zeros((128, 6, 128), np.float32)
    s = np.arange(128)[:, None]; t = np.arange(128)[None, :]
    cst[:, 0] = (s == t); cst[:, 1] = (s <= t); cst[:, 2] = (s >= t); cst[:, 3] = (s > t); cst[:, 4] = (s < t); cst[:, 5] = 1.0
    w["consts"] = cst
    return w


_NC_CACHE = {}


def run_streams(inp, streams, L, NSEG, SEG):
    key = (L, NSEG, SEG)
    if key not in _NC_CACHE:
        _NC_CACHE[key] = build(L, NSEG, SEG)
    nc = _NC_CACHE[key]
    w = _prep_weights(inp, L)
    in_maps = []
    for s in streams:
        m = dict(w)
        NTs = s["x"].shape[0]
        m["xT"] = np.ascontiguousarray(s["x"].astype(np.float32).reshape(NTs // 512, 512, 8, 128).transpose(3, 0, 2, 1))
        m["pT"] = np.ascontiguousarray(s["p"].astype(np.float32).reshape(L, NTs // 512, 512, 2, 128).transpose(0, 4, 1, 3, 2))
        m["cm"] = np.ascontiguousarray(np.broadcast_to(np.asarray(s["cm"], np.float32)[None], (128, NSEG + 1)))
        in_maps.append(m)
    res = run_bass_kernel_spmd(nc, in_maps, core_ids=list(range(len(streams))))
    return [np.ascontiguousarray(np.transpose(r["yT"], (1, 3, 2, 0)).reshape(-1, D)) for r in res.results]


def kernel(**inp):
    L, NSEG, SEG = 4, 4, 2048
    NT = NSEG * SEG
    xp = np.asarray(inp["x_prompt"]); xs = np.asarray(inp["x_sample"])
    pp = np.asarray(inp["p_prompt"]); ps = np.asarray(inp["p_sample"])
    streams = []
    for c in range(8):
        if c < 4:
            x = xp[4 * c:4 * c + 4].reshape(NT, D)
            p = pp[:, 4 * c:4 * c + 4].reshape(L, NT, PLE)
            cmv = np.zeros(NSEG + 1, np.float32)
        elif c < 6:
            x = np.zeros((NT, D), np.float32); p = np.zeros((L, NT, PLE), np.float32)
            cmv = np.zeros(NSEG + 1, np.float32)
        else:
            x = xs[c - 6]; p = ps[:, c - 6]
            cmv = np.zeros(NSEG + 1, np.float32); cmv[1:NSEG] = 1.0
        streams.append(dict(x=x, p=p, cm=cmv))
    outs = run_streams(inp, streams, L, NSEG, SEG)
    y_prompt = np.stack([outs[c].reshape(4, SEG, D) for c in range(4)], 0).reshape(16, SEG, D).astype(np.float32)
    y_sample = np.stack([outs[6], outs[7]], 0).astype(np.float32)
    return (y_prompt, y_sample)
```

```python
import contextlib
import math
import numpy as np
import ml_dtypes
import concourse.bass as bass
import concourse.mybir as mybir
from concourse.bass_utils import run_bass_kernel_spmd

F32 = mybir.dt.float32
BF16 = mybir.dt.bfloat16
I32 = mybir.dt.int32
AF = mybir.ActivationFunctionType
ALU = mybir.AluOpType

D = 1024
KT = 8
PLE = 256
DFF = 2816
NFF = 44
INW = 2320
EPS = 1e-6
O_Z, O_XBC, O_DT, O_S5, O_SGU = 0, 512, 1536, 1552, 1808


class T:
    __slots__ = ("w", "r", "ps")

    def __init__(self, ps=False):
        self.w = None
        self.r = []
        self.ps = ps


NDMA = 24


import os
STOP_AFTER = int(os.environ["KSTOP"]) if "KSTOP" in os.environ else None
SKIP = os.environ.get("KSKIP", "")


class _Stop(Exception):
    pass


class FW:
    def __init__(self, nc, sem_phases, dsems, needed=None):
        self.nc = nc
        self.e = {"pe": nc.tensor, "act": nc.scalar, "dve": nc.vector, "pool": nc.gpsimd, "sp": nc.sync}
        self.sem_phases = sem_phases
        self.ep = 0
        self.sem = dict(sem_phases[0])
        for j, d in enumerate(dsems):
            self.sem["q%d" % j] = d
        self.idx = {k: 0 for k in self.sem}
        self.val = {k: 0 for k in self.sem}
        self.rank = {k: {} for k in self.sem}
        self.seen = {k: {} for k in self.e}
        self.ndma = 0
        self.rec = None
        self.dry = needed is None
        self.needed = set() if needed is None else needed
        self.ninc = 0

    def _deps(self, R, W, en=None):
        deps = {}
        ep = self.ep
        for t in R:
            if t.w is not None:
                k, c, e = t.w
                if (e == ep or k[0] == "q") and deps.get(k, 0) < c:
                    deps[k] = c
            if t.ps:
                for (k, c, e) in t.r:
                    if k != en and (e == ep or k[0] == "q") and deps.get(k, 0) < c:
                        deps[k] = c
        for t in W:
            if t.w is not None:
                k, c, e = t.w
                if (e == ep or k[0] == "q") and deps.get(k, 0) < c:
                    deps[k] = c
            for (k, c, e) in t.r:
                if (e == ep or k[0] == "q") and deps.get(k, 0) < c:
                    deps[k] = c
        return deps

    def _wait(self, en, k, c):
        if k[0] == "q":
            self.e[en].wait_ge(self.sem[k], c)
        elif self.dry:
            self.needed.add((k, self.ep, c))
            self.e[en].wait_ge(self.sem[k], c)
        else:
            self.e[en].wait_ge(self.sem[k], self.rank[k][c])
        self.seen[en][k] = c

    def op(self, en, fn, R=(), W=(), inc=1):
        if self.rec is not None:
            self.rec.append((en, fn, tuple(R), tuple(W), inc))
            return None
        deps = self._deps(R, W, en)
        sk = en
        if en == "sp":
            sk = "q%d" % (self.ndma % NDMA)
            self.ndma += 1
            c = self.idx[sk]
            if c > 0 and deps.get(sk, 0) < c:
                deps[sk] = c
        seen = self.seen[en]
        for k, c in deps.items():
            if k == "pe" and en == "pe":
                continue
            if seen.get(k, 0) < c:
                self._wait(en, k, c)
        ins = fn(self.e[en])
        if en == "sp":
            self.idx[sk] += inc
            ins.then_inc(self.sem[sk], inc)
        else:
            self.idx[sk] += 1
            i = self.idx[sk]
            if self.dry or (sk, self.ep, i) in self.needed:
                self.val[sk] += 1
                self.rank[sk][i] = self.val[sk]
                ins.then_inc(self.sem[sk], 1)
                self.ninc += 1
        self.last_ins = ins
        tok = (sk, self.idx[sk], self.ep)
        for t in R:
            t.r.append(tok)
            if len(t.r) > 12:
                m = {}
                for (k, c, e) in t.r:
                    if (e == self.ep or k[0] == "q") and m.get(k, (0, 0))[0] < c:
                        m[k] = (c, e)
                t.r = [(k, c, e) for k, (c, e) in m.items()]
        for t in W:
            t.w = tok
            t.r = []
        return ins

    def record(self, f):
        assert self.rec is None
        self.rec = []
        f()
        r, self.rec = self.rec, None
        return r

    def emit_merged(self, a, b):
        ia = ib = 0
        na, nb = len(a), len(b)
        while ia < na or ib < nb:
            if ib >= nb or (ia < na and ia * nb <= ib * na):
                self.op(*a[ia]); ia += 1
            else:
                self.op(*b[ib]); ib += 1

    def barrier(self):
        for en in self.e:
            for k in self.sem:
                if k == en:
                    continue
                c = self.idx[k]
                if c > 0 and self.seen[en].get(k, 0) < c:
                    self._wait(en, k, c)

    def new_phase(self):
        self.barrier()
        if STOP_AFTER is not None and self.ep + 1 >= STOP_AFTER:
            raise _Stop()
        self.ep += 1
        assert self.ep < len(self.sem_phases)
        for k, h in self.sem_phases[self.ep].items():
            self.sem[k] = h
            self.idx[k] = 0
            self.val[k] = 0
            self.rank[k] = {}
            for en in self.seen:
                self.seen[en].pop(k, None)


def build(L, NSEG, SEG):
    needed = None
    for _pass in range(2):
        hold = {}
        try:
            nc = _build(L, NSEG, SEG, hold, needed)
        except _Stop:
            hold["fw"].barrier()
            nc = hold["nc"]
        needed = hold["fw"].needed
        print("pass", _pass, "incs", hold["fw"].ninc)
    return nc


def _build(L, NSEG, SEG, hold, needed):
    NT = NSEG * SEG
    TB = 512
    TS = 256
    TF = 256
    NTB = NT // TB
    NTF = NT // TF
    nc = bass.Bass("TRN2", target_bir_lowering=False)

    def din(name, shape, dt=F32):
        return nc.dram_tensor(name, list(shape), dt, kind="ExternalInput").ap()

    xT = din("xT", [128, NT // 512, KT, 512])
    pT = din("pT", [L, 128, NT // 512, 2, 512])
    cm_d = din("cm", [128, NSEG + 1])
    consts_d = din("consts", [128, 6, 128])
    w_in_d = din("w_in", [L, D, INW])
    w_out_d = din("w_out", [L, D, D])
    w_up_d = din("w_up", [L, D, 2 * DFF])
    w_down_d = din("w_down", [L, DFF, D])
    ple_proj_d = din("ple_proj", [L, PLE, D])
    ple_gate_d = din("ple_gate", [L, D, D])
    glu_w_d = din("glu_w", [L, 256, 256])
    sgu_wT_d = din("sgu_wT", [L, 128, 4, 128])
    nrm_d = din("nrm", [L, 128, 4, 8])
    mixn_d = din("mixn", [L, 128, 8])
    ssdcw_d = din("ssdcw", [L, 128, 8, 4])
    ffncw_d = din("ffncw", [L, 128, NFF, 4])
    dtrow_d = din("dtrow", [L, 3, 16])
    s5vec_d = din("s5vec", [L, 128, 2, 6])
    sgub_d = din("sgub", [L, 128, 4])
    lamS_d = din("lamS", [L, 2, 128, 3, 8])
    lamR_d = din("lamR", [L, 2, 3, 1024])
    bsp_d = din("bsp", [L, 2, 128, 8, 128])
    csp_d = din("csp", [L, 2, 128, 8, 128])
    yT = nc.dram_tensor("yT", [128, NT // 512, KT, 512], F32, kind="ExternalOutput").ap()
    xA = nc.dram_tensor("xA", [128, NT // 512, KT, 512], F32).ap()
    xB = nc.dram_tensor("xB", [128, NT // 512, KT, 512], F32).ap()
    xbcS = nc.dram_tensor("xbcS", [128, NT // 512, KT, 512], BF16).ap()
    uS = nc.dram_tensor("uS", [128, NT // 512, 2, 512], BF16).ap()
    ybS = nc.dram_tensor("ybS", [NT, 512], F32).ap()
    yb5S = nc.dram_tensor("yb5S", [128, NT // 512, 2, 512], F32).ap()

    with contextlib.ExitStack() as top:
        nph = 3 * L + 2
        sem_phases = [{n: top.enter_context(nc.semaphore("%s_%d" % (n, ph))) for n in ("pe", "act", "dve", "pool")} for ph in range(nph)]
        dsems = [top.enter_context(nc.semaphore("dq%d" % j)) for j in range(NDMA)]
        fw = FW(nc, sem_phases, dsems, needed)
        hold["fw"] = fw
        hold["nc"] = nc
        op = fw.op

        _cnt = [0]

        def sbt(st, name, shape, dt):
            _cnt[0] += 1
            return st.enter_context(nc.sbuf_tensor("s%d_%s" % (_cnt[0], name), list(shape), dt))

        def dma(out, in_, R=(), W=()):
            return op("sp", lambda e: e.dma_start(out=out, in_=in_), R, W, inc=16)

        def mm(out, lhsT, rhs, start, stop, R, W):
            return op("pe", lambda e: e.matmul(out, lhsT=lhsT, rhs=rhs, start=start, stop=stop), R, W)

        def tr(out, in_, ident, R, W):
            return op("pe", lambda e: e.transpose(out, in_, ident), R, W)

        def act(out, in_, func, R, W, bias=None, scale=None, accum=None):
            kw = {}
            if bias is not None:
                kw["bias"] = bias
            if scale is not None:
                kw["scale"] = scale
            if accum is not None:
                kw["accum_out"] = accum
            return op("act", lambda e: e.activation(out=out, in_=in_, func=func, **kw), R, W)

        def tt(en, out, in0, in1, o, R, W):
            return op(en, lambda e: e.tensor_tensor(out=out, in0=in0, in1=in1, op=o), R, W)

        def ts(en, out, in0, s1, o0, R, W, s2=None, o1=None):
            if o1 is None:
                return op(en, lambda e: e.tensor_scalar(out=out, in0=in0, scalar1=s1, scalar2=None, op0=o0), R, W)
            return op(en, lambda e: e.tensor_scalar(out=out, in0=in0, scalar1=s1, scalar2=s2, op0=o0, op1=o1), R, W)

        def stt(out, in0, sc, in1, o0, o1, R, W):
            return op("dve", lambda e: e.scalar_tensor_tensor(out=out, in0=in0, scalar=sc, in1=in1, op0=o0, op1=o1), R, W)

        def cp(en, out, in_, R, W):
            if en == "act":
                return act(out, in_, AF.Copy, R, W)
            return op(en, lambda e: e.tensor_copy(out=out, in_=in_), R, W)

        def mset(en, ap, val, W):
            return op(en, lambda e: e.memset(ap, val), (), W)

        cst = sbt(top, "cst", [128, 6, 128], F32); Tcst = T()
        cstb = sbt(top, "cstb", [128, 6, 128], BF16); Tcstb = T()
        cm = sbt(top, "cm", [128, NSEG + 1], F32); Tcm = T()
        epsb = sbt(top, "epsb", [128, 3], F32); Teps = T()
        dma(cst[:], consts_d[:, :, :], W=[Tcst])
        dma(cm[:], cm_d[:, :], W=[Tcm])
        cp("dve", cstb[:], cst[:], [Tcst], [Tcstb])
        mset("dve", epsb[:, 0:1], EPS, [Teps])
        mset("dve", epsb[:, 1:2], 1e-5, [Teps])
        mset("dve", epsb[:, 2:3], 1.0, [Teps])
        ident_b = cstb[:, 0, :]
        triF = cst[:, 1, :]
        triB = cst[:, 2, :]
        UF = cst[:, 3, :]
        UB = cst[:, 4, :]
        ones_f = cst[:, 5, :]
        ones_b = cstb[:, 5, :]
        maskF_b = cstb[:, 1, :]
        maskB_b = cstb[:, 2, :]

        P = []
        TP = []
        for i in range(7):
            P.append(top.enter_context(nc.psum_tensor("P%d" % i, [128, 512], F32)))
            TP.append(T(ps=True))
        PT = top.enter_context(nc.psum_tensor("PT", [128, 1024], BF16)); TPT = T(ps=True)

        TxA = [T() for _ in range(NT // 256)]
        TxB = [T() for _ in range(NT // 256)]
        TxIn = [T() for _ in range(NT // 256)]

        def xtr(TL, t0, n):
            return [TL[b] for b in range(t0 // 256, (t0 + n + 255) // 256)]

        Txbc = [T() for _ in range(NTB)]
        Tu = [T() for _ in range(NTB)]
        Tyb = [T() for _ in range(NT // 128)]
        Tyb5 = [T() for _ in range(NTB)]

        def load_w(stage, Tstage, dst, Tdst, src_kpn, nk, cols, scale_tile=None, Tscale=None, chunk=1024):
            si = 0
            for k in range(nk):
                for c0 in range(0, cols, chunk):
                    c1 = min(cols, c0 + chunk)
                    s, Ts = stage[si % len(stage)], Tstage[si % len(stage)]
                    si += 1
                    dma(s[:, 0:c1 - c0], src_kpn(k, c0, c1), W=[Ts])
                    if scale_tile is None:
                        act(dst[:, k, c0:c1], s[:, 0:c1 - c0], AF.Copy, [Ts], [Tdst])
                    else:
                        act(dst[:, k, c0:c1], s[:, 0:c1 - c0], AF.Identity, [Ts, Tscale], [Tdst], scale=scale_tile[:, k:k + 1])

        def norm_stage(xt, Txt, nkt, sq, Tsq, hT_main, ThT, width, nfeat, rtmp, Trt, pbank=6, eng="pool"):
            act(sq[:, 0:nkt, 0:width], xt, AF.Square, [Txt], [Tsq])
            for k in range(nkt):
                mm(P[pbank][:, 0:width], ones_b, sq[:, k, 0:width], k == 0, k == nkt - 1, [Tcstb, Tsq], [TP[pbank]])
            act(rtmp[:, 0:width], P[pbank][:, 0:width], AF.Sqrt, [TP[pbank], Teps], [Trt], bias=epsb[:, 0:1], scale=1.0 / nfeat)
            op("dve", lambda e: e.reciprocal(out=rtmp[:, 0:width], in_=rtmp[:, 0:width]), [Trt], [Trt])
            tt(eng, hT_main, xt, rtmp[:, 0:width].unsqueeze(1).broadcast_to([128, nkt, width]), ALU.mult, [Txt, Trt], [ThT])

        for l in range(L):
            last_layer = (l == L - 1)
            with contextlib.ExitStack() as lay:
                nrm = sbt(lay, "nrm", [128, 4, 8], F32); Tnrm = T()
                mixn = sbt(lay, "mixn", [128, 8], F32); Tmixn = T()
                ssdcw = sbt(lay, "ssdcw", [128, 8, 4], F32); Tssdcw = T()
                dtrow = sbt(lay, "dtrow", [128, 3, 16], F32); Tdtrow = T()
                arow = sbt(lay, "arow", [128, 16], F32)
                s5vec = sbt(lay, "s5vec", [128, 2, 6], F32); Ts5vec = T()
                sgub = sbt(lay, "sgub", [128, 4], F32); Tsgub = T()
                carry = sbt(lay, "carry", [128, 2, 8], F32); Tcarry = T()
                Hst = sbt(lay, "Hst", [128, 512], F32); THst = T()
                Hbf = sbt(lay, "Hbf", [128, 512], BF16); THbf = T()
                dma(nrm[:], nrm_d[l], W=[Tnrm])
                dma(mixn[:], mixn_d[l], W=[Tmixn])
                dma(ssdcw[:], ssdcw_d[l], W=[Tssdcw])
                dma(dtrow[:].rearrange("p a b -> p (a b)"), dtrow_d[l:l + 1].rearrange("o a b -> o (a b)").partition_broadcast(128), W=[Tdtrow])
                dma(s5vec[:], s5vec_d[l], W=[Ts5vec])
                dma(sgub[:], sgub_d[l], W=[Tsgub])
                act(arow[:], dtrow[:, 1, :], AF.Exp, [Tdtrow], [Tdtrow])
                ts("dve", arow[:], arow[:], -1.0, ALU.mult, [Tdtrow], [Tdtrow])

                for d in (1, 0):
                    with contextlib.ExitStack() as sw:
                        xt1 = sbt(sw, "xt1", [128, KT, TB], F32); Txt1 = T()
                        xt2 = [xt1, xt1]; Txt2 = [Txt1, Txt1]
                        sq = sbt(sw, "sq", [128, KT, TB], BF16); Tsq = T()
                        hT = [sbt(sw, "hT%d" % i, [128, KT, TB + 2], BF16) for i in range(3)]; ThT = [T(), T(), T()]
                        rtmp = sbt(sw, "rtmp", [128, TB], F32); Trt = T()
                        if d == 1:
                            gen = sbt(sw, "gen", [128, 2560], F32)
                            ncol = 1296
                            Win = sbt(sw, "Win", [128, KT, ncol], BF16); TWin = T()
                            cXBC, cDT, cS5 = 0, 1024, 1040
                            if l > 0:
                                Wg = sbt(sw, "Wg", [128, KT, D], BF16); TWg = T()
                                Wp = sbt(sw, "Wp", [128, 2, D], BF16); TWp = T()
                                pt = sbt(sw, "pt", [128, 2, TB], F32); Tpt = T()
                                ptb = sbt(sw, "ptb", [128, 2, TB], BF16); Tptb = T()
                                gsb = sbt(sw, "gsb", [128, TB], F32); Tgsb = T()
                        else:
                            ncol = 1040
                            Win = sbt(sw, "Win", [128, KT, ncol], BF16); TWin = T()
                            cZ, cDT, cSGU = 0, 512, 528
                            Wout = sbt(sw, "Wout", [128, KT, D], BF16); TWout = T()
                            gluw = sbt(sw, "gluw", [128, 2, 256], BF16); Tgluw = T()
                            sguw = sbt(sw, "sguw", [128, 4, 128], BF16); Tsguw = T()
                            mixT = sbt(sw, "mixT", [128, KT, TB], BF16); TmixT = T()
                            gen = sbt(sw, "gen", [128, 2560], F32)
                            uvg = gen[:, 0:4 * TB].rearrange("p (a b) -> p a b", a=4); Tuvg = T()
                            vtok = sbt(sw, "vtok", [128, 256], BF16); Tvtok = T()
                            mixtok = sbt(sw, "mixtok", [128, 256], BF16); Tmixtok = T()
                            ybl = sbt(sw, "ybl", [128, 512], F32); Tybl = T()
                            yb5l = sbt(sw, "yb5l", [128, 2, TB], F32); Tyb5l = T()
                            y5g = sbt(sw, "y5g", [128, 2, TB], BF16); Ty5g = T()
                            ztmp = sbt(sw, "ztmp", [128, 512], F32); Tztmp = T()
                            ssq = sbt(sw, "ssq", [128, 4], F32); Tssq = T()
                            ytok = sbt(sw, "ytok", [128, 512], BF16); Tytok = T()
                        xbc = sbt(sw, "xbc", [128, KT, TB], BF16); Txbcl = T()
                        uT = sbt(sw, "uT", [128, 2, TB], BF16); TuT = T()
                        acc = [sbt(sw, "acc%d" % i, [128, TB], F32) for i in range(2)]; Tacc = [T(), T()]
                        sm = sbt(sw, "sm", [128, 12, 32], F32); Tsm = T()
                        seg = [sbt(sw, "seg%d" % i, [128, 128], F32) for i in range(2)]; Tseg = [T(), T()]
                        CBm = sbt(sw, "CBm", [128, 128], BF16); TCBm = T()
                        Lt = sbt(sw, "Lt", [128, 512], BF16); TLt = T()
                        Mt = sbt(sw, "Mt", [128, 512], BF16); TMt = T()
                        xr = sbt(sw, "xr", [128, 512], BF16); Txr = T()
                        xrd = sbt(sw, "xrd", [128, 512], BF16); Txrd = T()
                        Btok = sbt(sw, "Btok", [128, 256], BF16); TBtok = T()
                        ych = sbt(sw, "ych", [128, 512], F32); Tych = T()
                        ytmp = sbt(sw, "ytmp", [128, 512], F32); Tytmp = T()
                        BB = sbt(sw, "BB", [128, 2, 8, 128], BF16); TBB = T()
                        CC = sbt(sw, "CC", [128, 2, 8, 128], BF16); TCC = T()
                        tab = sbt(sw, "tab", [128, 2, 8, TS], F32); Ttab = T()
                        s5s = sbt(sw, "s5s", [128, 8, 8], F32); Ts5s = T()
                        s5w1 = sbt(sw, "s5w1", [128, 8, TS], F32); Ts5w1 = T()
                        s5w2 = sbt(sw, "s5w2", [128, 8, TS], F32); Ts5w2 = T()
                        s5w = [s5w1, s5w2]; Ts5w = [Ts5w1, Ts5w2]
                        Tslot = [[T() for _ in range(8)] for _ in range(2)]
                        Tctmp = T()
                        s5o = [sbt(sw, "s5o%d" % i, [128, 2, TS], BF16) for i in range(2)]; Ts5o = [[T(), T()], [T(), T()]]
                        y5d = sbt(sw, "y5d", [128, 2, TB], F32); Ty5d = T()

                        xflat = xt1[:].rearrange("p a b -> p (a b)")
                        stage = [xflat[:, 1024 * j:1024 * (j + 1)] for j in range(4)]
                        Tstage2 = [T() for _ in range(4)]
                        wsrc = w_in_d[l].rearrange("(k p) n -> p k n", p=128)
                        if d == 1:
                            load_w(stage, Tstage2, Win, TWin, lambda k, c0, c1: wsrc[:, k, O_XBC + c0:O_XBC + c1], KT, 1296, nrm[:, 0, :], Tnrm)
                            if l > 0:
                                gsrc = ple_gate_d[l - 1].rearrange("(k p) n -> p k n", p=128)
                                load_w(stage, Tstage2, Wg, TWg, lambda k, c0, c1: gsrc[:, k, c0:c1], KT, D, nrmprev[:, 2, :], Tnrmprev)
                                psrc = ple_proj_d[l - 1].rearrange("(k p) n -> p k n", p=128)
                                load_w(stage, Tstage2, Wp, TWp, lambda k, c0, c1: psrc[:, k, c0:c1], 2, D)
                        else:
                            load_w(stage, Tstage2, Win[:, :, 0:512], TWin, lambda k, c0, c1: wsrc[:, k, O_Z + c0:O_Z + c1], KT, 512, nrm[:, 0, :], Tnrm)
                            load_w(stage, Tstage2, Win[:, :, 512:528], TWin, lambda k, c0, c1: wsrc[:, k, O_DT + c0:O_DT + c1], KT, 16, nrm[:, 0, :], Tnrm)
                            load_w(stage, Tstage2, Win[:, :, 528:1040], TWin, lambda k, c0, c1: wsrc[:, k, O_SGU + c0:O_SGU + c1], KT, 512, nrm[:, 0, :], Tnrm)
                            osrc = w_out_d[l].rearrange("(k p) n -> p k n", p=128)
                            load_w(stage, Tstage2, Wout, TWout, lambda k, c0, c1: osrc[:, k, c0:c1], KT, D, mixn, Tmixn)
                            gls = glu_w_d[l].rearrange("(k p) n -> p k n", p=128)
                            load_w(stage, Tstage2, gluw, Tgluw, lambda k, c0, c1: gls[:, k, c0:c1], 2, 256)
                            load_w(stage, Tstage2, sguw[:].rearrange("p a b -> p (a b)").unsqueeze(1), Tsguw,
                                   lambda k, c0, c1: sgu_wT_d[l].rearrange("p a b -> p (a b)")[:, c0:c1], 1, 512)

                        with contextlib.ExitStack() as dz:
                            CW = 256
                            lamR = sbt(dz, "lamR", [128, 3, CW], F32); TlamR = T()
                            wkR = gen[:, :].rearrange("p (a b) -> p a b", a=10); TwkR = T()
                            wki = sbt(dz, "wki", [128, CW], I32); Twki = T()
                            lamS = sbt(dz, "lamS", [128, 3, 8], F32); TlamS = T()
                            wkS = sbt(dz, "wkS", [128, 10, 8], F32); TwkS = T()
                            wkiS = sbt(dz, "wkiS", [128, 8], I32); TwkiS = T()
                            bspt = sbt(dz, "bspt", [128, 2, CW], F32); Tbspt = T()
                            dma(lamS[:], lamS_d[l, d], W=[TlamS])

                            def disc(lam, Tl, wk, Tw, wi, Twi):
                                lr, li, ls = lam[:, 0, :], lam[:, 1, :], lam[:, 2, :]
                                R0, W0 = [Tl, Tw], [Tw]
                                act(wk[:, 5, :], ls, AF.Exp, R0, W0)
                                tt("dve", wk[:, 0, :], lr, wk[:, 5, :], ALU.mult, R0, W0)
                                act(wk[:, 0, :], wk[:, 0, :], AF.Exp, R0, W0)
                                tt("dve", wk[:, 6, :], li, wk[:, 5, :], ALU.mult, R0, W0)
                                ts("dve", wk[:, 7, :], wk[:, 6, :], 1.0 / (2 * math.pi), ALU.mult, R0, W0)
                                cp("dve", wi, wk[:, 7, :], R0, [Twi])
                                cp("dve", wk[:, 7, :], wi, [Twi], W0)
                                ts("dve", wk[:, 6, :], wk[:, 6, :], 0.25, ALU.mult, R0, W0)
                                stt(wk[:, 6, :], wk[:, 7, :], -math.pi / 2, wk[:, 6, :], ALU.mult, ALU.add, R0, W0)
                                ts("dve", wk[:, 7, :], wk[:, 6, :], math.pi / 2, ALU.add, R0, W0)
                                act(wk[:, 2, :], wk[:, 6, :], AF.Sin, R0, W0)
                                act(wk[:, 1, :], wk[:, 7, :], AF.Sin, R0, W0)
                                for _ in range(2):
                                    tt("dve", wk[:, 6, :], wk[:, 1, :], wk[:, 1, :], ALU.mult, R0, W0)
                                    tt("dve", wk[:, 7, :], wk[:, 2, :], wk[:, 2, :], ALU.mult, R0, W0)
                                    tt("dve", wk[:, 2, :], wk[:, 2, :], wk[:, 1, :], ALU.mult, R0, W0)
                                    ts("dve", wk[:, 2, :], wk[:, 2, :], 2.0, ALU.mult, R0, W0)
                                    tt("dve", wk[:, 1, :], wk[:, 6, :], wk[:, 7, :], ALU.subtract, R0, W0)
                                tt("dve", wk[:, 6, :], wk[:, 0, :], wk[:, 1, :], ALU.mult, R0, W0)
                                ts("dve", wk[:, 6, :], wk[:, 6, :], -1.0, ALU.add, R0, W0)
                                tt("dve", wk[:, 7, :], wk[:, 0, :], wk[:, 2, :], ALU.mult, R0, W0)
                                tt("dve", wk[:, 8, :], lr, lr, ALU.mult, R0, W0)
                                tt("dve", wk[:, 9, :], li, li, ALU.mult, R0, W0)
                                tt("dve", wk[:, 8, :], wk[:, 8, :], wk[:, 9, :], ALU.add, R0, W0)
                                op("dve", lambda e: e.reciprocal(out=wk[:, 8, :], in_=wk[:, 8, :]), R0, W0)
                                tt("dve", wk[:, 3, :], wk[:, 6, :], lr, ALU.mult, R0, W0)
                                tt("dve", wk[:, 9, :], wk[:, 7, :], li, ALU.mult, R0, W0)
                                tt("dve", wk[:, 3, :], wk[:, 3, :], wk[:, 9, :], ALU.add, R0, W0)
                                tt("dve", wk[:, 3, :], wk[:, 3, :], wk[:, 8, :], ALU.mult, R0, W0)
                                tt("dve", wk[:, 4, :], wk[:, 7, :], lr, ALU.mult, R0, W0)
                                tt("dve", wk[:, 9, :], wk[:, 6, :], li, ALU.mult, R0, W0)
                                tt("dve", wk[:, 4, :], wk[:, 4, :], wk[:, 9, :], ALU.subtract, R0, W0)
                                tt("dve", wk[:, 4, :], wk[:, 4, :], wk[:, 8, :], ALU.mult, R0, W0)

                            disc(lamS, TlamS, wkS, TwkS, wkiS[:], TwkiS)
                            BBf = [BB[:, 0].rearrange("p a b -> p (a b)"), BB[:, 1].rearrange("p a b -> p (a b)")]
                            CCf = [CC[:, 0].rearrange("p a b -> p (a b)"), CC[:, 1].rearrange("p a b -> p (a b)")]
                            for cc in range(1024 // CW):
                                csl = slice(cc * CW, (cc + 1) * CW)
                                dma(lamR[:], lamR_d[l, d:d + 1, :, csl].partition_broadcast(128), R=[TwkR], W=[TlamR])
                                dma(bspt[:, 0, :], bsp_d[l, 0].rearrange("p a b -> p (a b)")[:, csl], R=[TwkR], W=[Tbspt])
                                dma(bspt[:, 1, :], bsp_d[l, 1].rearrange("p a b -> p (a b)")[:, csl], R=[TwkR], W=[Tbspt])
                                disc(lamR, TlamR, wkR, TwkR, wki[:], Twki)
                                R1 = [TwkR, Tbspt]
                                tt("dve", wkR[:, 5, :], wkR[:, 3, :], bspt[:, 0, :], ALU.mult, R1, [TwkR])
                                tt("dve", wkR[:, 6, :], wkR[:, 4, :], bspt[:, 1, :], ALU.mult, R1, [TwkR])
                                tt("dve", BBf[0][:, csl], wkR[:, 5, :], wkR[:, 6, :], ALU.subtract, [TwkR], [TBB])
                                tt("dve", wkR[:, 5, :], wkR[:, 3, :], bspt[:, 1, :], ALU.mult, R1, [TwkR])
                                tt("dve", wkR[:, 6, :], wkR[:, 4, :], bspt[:, 0, :], ALU.mult, R1, [TwkR])
                                tt("dve", BBf[1][:, csl], wkR[:, 5, :], wkR[:, 6, :], ALU.add, [TwkR], [TBB])
                                dma(bspt[:, 0, :], csp_d[l, 0].rearrange("p a b -> p (a b)")[:, csl], R=[TwkR, TBB], W=[Tbspt])
                                dma(bspt[:, 1, :], csp_d[l, 1].rearrange("p a b -> p (a b)")[:, csl], R=[TwkR, TBB], W=[Tbspt])
                                cp("dve", CCf[0][:, csl], bspt[:, 0, :], [Tbspt], [TCC])
                                ts("dve", CCf[1][:, csl], bspt[:, 1, :], -1.0, ALU.mult, [Tbspt], [TCC])
                            cp("dve", s5s[:, 0, :], wkS[:, 0, :], [TwkS], [Ts5s])
                            cp("dve", s5s[:, 1, :], wkS[:, 1, :], [TwkS], [Ts5s])
                            cp("dve", s5s[:, 2, :], wkS[:, 2, :], [TwkS], [Ts5s])
                            cosT, sinT = tab[:, 0], tab[:, 1]
                            p0 = 0 if d == 0 else TS - 1
                            mset("dve", cosT[:, :, p0:p0 + 1], 1.0, [Ttab])
                            mset("dve", sinT[:, :, p0:p0 + 1], 0.0, [Ttab])
                            tA = s5w1
                            tB = y5d[:].rearrange("p a b -> p (a b)").rearrange("p (a b) -> p a b", a=8)
                            n = 1
                            while n < TS:
                                if d == 0:
                                    src = slice(0, n); dst = slice(n, 2 * n)
                                else:
                                    src = slice(TS - n, TS); dst = slice(TS - 2 * n, TS - n)
                                cn = s5s[:, 1, :].unsqueeze(2).broadcast_to([128, 8, n])
                                sn = s5s[:, 2, :].unsqueeze(2).broadcast_to([128, 8, n])
                                Rt = [Ttab, Ts5s, Ts5w1, Ty5d]
                                tt("dve", tA[:, :, 0:n], cosT[:, :, src], cn, ALU.mult, Rt, [Ts5w[0]])
                                tt("dve", tB[:, :, 0:n], sinT[:, :, src], sn, ALU.mult, Rt, [Ty5d])
                                tt("dve", cosT[:, :, dst], tA[:, :, 0:n], tB[:, :, 0:n], ALU.subtract, Rt, [Ttab])
                                tt("dve", tA[:, :, 0:n], sinT[:, :, src], cn, ALU.mult, Rt, [Ts5w[0]])
                                tt("dve", tB[:, :, 0:n], cosT[:, :, src], sn, ALU.mult, Rt, [Ty5d])
                                tt("dve", sinT[:, :, dst], tA[:, :, 0:n], tB[:, :, 0:n], ALU.add, Rt, [Ttab])
                                Rs = [Ts5s]
                                tt("dve", s5s[:, 3, :], s5s[:, 1, :], s5s[:, 1, :], ALU.mult, Rs, Rs)
                                tt("dve", s5s[:, 4, :], s5s[:, 2, :], s5s[:, 2, :], ALU.mult, Rs, Rs)
                                tt("dve", s5s[:, 2, :], s5s[:, 2, :], s5s[:, 1, :], ALU.mult, Rs, Rs)
                                ts("dve", s5s[:, 2, :], s5s[:, 2, :], 2.0, ALU.mult, Rs, Rs)
                                tt("dve", s5s[:, 1, :], s5s[:, 3, :], s5s[:, 4, :], ALU.subtract, Rs, Rs)
                                n *= 2
                        fw.barrier()
                        if l == 1 and fw.dry:
                            print("sbuf remaining sweep d=%d:" % d, nc.sbuf_bytes_remaining)
                        mset("dve", carry[:], 0.0, [Tcarry])
                        mset("dve", Hst[:], 0.0, [THst])
                        mset("pool", Hbf[:], 0.0, [THbf])

                        order = list(range(NTB - 1, -1, -1)) if d == 1 else list(range(NTB))
                        xsrc, Txs = (xT, TxIn) if l == 0 else (xA, TxA)

                        def stageA(i):
                            t0 = i * TB
                            xt, Txt = xt2[i % 2], Txt2[i % 2]
                            h, Th = hT[i % 3], ThT[i % 3]
                            dma(xt[:], xsrc[:, i], R=xtr(Txs, t0, TB), W=[Txt])
                            if d == 1 and l > 0:
                                norm_stage(xt[:], Txt, KT, sq, Tsq, h[:, :, 1:TB + 1], Th, TB, D, rtmp, Trt)
                                dma(pt[:], pT[l - 1][:, i], W=[Tpt])
                                cp("pool", ptb[:], pt[:], [Tpt], [Tptb])
                                for m in range(KT):
                                    pg, pe_ = 2 * (m % 2), 2 * (m % 2) + 1
                                    for k in range(KT):
                                        mm(P[pg][:, 0:TB], Wg[:, k, m * 128:(m + 1) * 128], h[:, k, 1:TB + 1], k == 0, k == KT - 1, [TWg, Th], [TP[pg]])
                                    for k in range(2):
                                        mm(P[pe_][:, 0:TB], Wp[:, k, m * 128:(m + 1) * 128], ptb[:, k, :], k == 0, k == 1, [TWp, Tptb], [TP[pe_]])
                                    act(gsb[:], P[pg][:, 0:TB], AF.Sigmoid, [TP[pg]], [Tgsb])
                                    tt("dve", gsb[:], gsb[:], P[pe_][:, 0:TB], ALU.mult, [Tgsb, TP[pe_]], [Tgsb])
                                    tt("pool", xt[:, m, :], xt[:, m, :], gsb[:], ALU.add, [Txt, Tgsb], [Txt])
                                dma(xA[:, i], xt[:], R=[Txt], W=xtr(TxA, t0, TB))
                            norm_stage(xt[:], Txt, KT, sq, Tsq, h[:, :, 1:TB + 1], Th, TB, D, rtmp, Trt)

                        def halos(i):
                            h, Th = hT[i % 3], ThT[i % 3]
                            t0 = i * TB
                            for side in (0, 1):
                                j = i - 1 if side == 0 else i + 1
                                col = 0 if side == 0 else TB + 1
                                if j < 0 or j >= NTB:
                                    mset("pool", h[:, :, col:col + 1], 0.0, [Th])
                                    continue
                                hn, Tn = hT[j % 3], ThT[j % 3]
                                scol = TB if side == 0 else 1
                                bnd = t0 if side == 0 else t0 + TB
                                if bnd % SEG == 0:
                                    ts("pool", h[:, :, col:col + 1], hn[:, :, scol:scol + 1], cm[:, bnd // SEG:bnd // SEG + 1], ALU.mult, [Tn, Tcm], [Th])
                                else:
                                    cp("pool", h[:, :, col:col + 1], hn[:, :, scol:scol + 1], [Tn], [Th])

                        def dt_tile(h, Th):
                            tri = triB if d == 1 else triF
                            dcol = cDT + 8 * d
                            for c in range(4):
                                hs = slice(1 + c * 128, 1 + (c + 1) * 128)
                                for k in range(KT):
                                    mm(P[4][:, 8 * c:8 * c + 8], h[:, k, hs], Win[:, k, dcol:dcol + 8], k == 0, k == KT - 1, [Th, TWin], [TP[4]])
                            dtb, ab, e1, dt, adt, rcs, d1, ecs, dec, etot, dtd, dsk = [sm[:, j, :] for j in range(12)]
                            S = [Tsm]
                            v3 = lambda ap: ap.rearrange("p (a b) -> p a b", a=4)
                            tt("dve", v3(dtb), v3(P[4][:, 0:32]), dtrow[:, 0, 8 * d:8 * d + 8].unsqueeze(1).broadcast_to([128, 4, 8]), ALU.add, [TP[4], Tdtrow, Tsm], S)
                            act(ab, dtb, AF.Abs, S, S)
                            act(e1, ab, AF.Exp, S, S, scale=-1.0)
                            act(e1, e1, AF.Ln, [Tsm, Teps], S, bias=epsb[:, 2:3])
                            stt(dt, dtb, 0.0, e1, ALU.max, ALU.add, S, S)
                            tt("dve", v3(adt), v3(dt), arow[:, 8 * d:8 * d + 8].unsqueeze(1).broadcast_to([128, 4, 8]), ALU.mult, [Tsm, Tdtrow], S)
                            for c in range(4):
                                mm(P[4][:, 32 + 8 * c:40 + 8 * c], tri, adt[:, 8 * c:8 * c + 8], True, True, [Tcst, Tsm], [TP[4]])
                                mm(P[4][:, 64 + 8 * c:72 + 8 * c], ones_f, adt[:, 8 * c:8 * c + 8], True, True, [Tcst, Tsm], [TP[4]])
                            cp("act", rcs, P[4][:, 32:64], [TP[4], Tsm], S)
                            tt("dve", d1, P[4][:, 64:96], rcs, ALU.subtract, [TP[4], Tsm], S)
                            act(ecs, rcs, AF.Exp, S, S)
                            act(dec, d1, AF.Exp, S, S)
                            act(etot, P[4][:, 64:96], AF.Exp, [TP[4], Tsm], S)
                            tt("dve", dtd, dt, dec, ALU.mult, S, S)

                        def ssd_chunk(i, c, h, Th):
                            t0 = i * TB + c * 128
                            cs = slice(c * 128, (c + 1) * 128)
                            hs = slice(1 + c * 128, 1 + (c + 1) * 128)
                            tri = triB if d == 1 else triF
                            Um = UB if d == 1 else UF
                            maskb = maskB_b if d == 1 else maskF_b
                            bnd = (t0 + 128) if d == 1 else t0
                            if bnd % SEG == 0 and 0 < bnd < NT:
                                ts("pool", Hst[:], Hst[:], cm[:, bnd // SEG:bnd // SEG + 1], ALU.mult, [THst, Tcm], [THst])
                                cp("pool", Hbf[:], Hst[:], [THst], [THbf])
                            dtb, ab, e1, dt, adt, rcs, d1, ecs, dec, etot, dtd, dsk = [sm[:, j, 8 * c:8 * c + 8] for j in range(12)]
                            for j in range(4):
                                tr(PT[:, 128 * j:128 * j + 128], xbc[:, j, cs], ident_b, [Txbcl, Tcstb], [TPT])
                            for g in range(2):
                                tr(PT[:, 512 + 128 * g:640 + 128 * g], xbc[:, 4 + g, cs], ident_b, [Txbcl, Tcstb], [TPT])
                            x3v = PT[:, 0:512].rearrange("p (a b) -> p a b", a=8)
                            tt("dve", xr[:].rearrange("p (a b) -> p a b", a=8), x3v, dt.unsqueeze(2).broadcast_to([128, 8, 64]), ALU.mult, [TPT, Tsm], [Txr])
                            tt("dve", xrd[:].rearrange("p (a b) -> p a b", a=8), x3v, dtd.unsqueeze(2).broadcast_to([128, 8, 64]), ALU.mult, [TPT, Tsm], [Txrd])
                            cp("act", Btok[:], PT[:, 512:768], [TPT], [TBtok])
                            if d == 0:
                                tt("dve", ytmp[:].rearrange("p (a b) -> p a b", a=8), x3v, dtrow[:, 2, 0:8].unsqueeze(2).broadcast_to([128, 8, 64]), ALU.mult, [TPT, Tdtrow], [Tytmp])
                            for g in range(2):
                                mm(P[4][:, 128 + 128 * g:256 + 128 * g], xbc[:, 4 + g, cs], xbc[:, 6 + g, cs], True, True, [Txbcl], [TP[4]])
                                tt("dve", CBm[:], P[4][:, 128 + 128 * g:256 + 128 * g], maskb, ALU.mult, [TP[4], Tcstb], [TCBm])
                                for hh in range(4):
                                    sg, Tsg = seg[hh % 2], Tseg[hh % 2]
                                    act(sg[:], Um, AF.Identity, [Tcst, Tsm], [Tsg], scale=adt[:, 4 * g + hh:4 * g + hh + 1])
                                    mm(P[6][:, 128 * hh:128 * hh + 128], sg[:], tri, True, True, [Tsg, Tcst], [TP[6]])
                                act(Lt[:], P[6][:, :], AF.Exp, [TP[6]], [TLt])
                                tt("dve", Mt[:].rearrange("p (a b) -> p a b", a=4), Lt[:].rearrange("p (a b) -> p a b", a=4),
                                   CBm[:].unsqueeze(1).broadcast_to([128, 4, 128]), ALU.mult, [TLt, TCBm], [TMt])
                                for hh in range(4):
                                    hd = 4 * g + hh
                                    mm(P[0][:, 64 * hd:64 * hd + 64], Mt[:, 128 * hh:128 * hh + 128], xr[:, 64 * hd:64 * hd + 64], True, True, [TMt, Txr], [TP[0]])
                                mm(P[1][:, 256 * g:256 * g + 256], xbc[:, 6 + g, cs], Hbf[:, 256 * g:256 * g + 256], True, True, [Txbcl, THbf], [TP[1]])
                                mm(P[2][:, 256 * g:256 * g + 256], Btok[:, 128 * g:128 * g + 128], xrd[:, 256 * g:256 * g + 256], True, True, [TBtok, Txrd], [TP[2]])
                            tt("dve", ych[:].rearrange("p (a b) -> p a b", a=8), P[1][:, :].rearrange("p (a b) -> p a b", a=8),
                               ecs.unsqueeze(2).broadcast_to([128, 8, 64]), ALU.mult, [TP[1], Tsm], [Tych])
                            tt("dve", ych[:], ych[:], P[0][:, :], ALU.add, [Tych, TP[0]], [Tych])
                            tt("pool", Hst[:].rearrange("p (a b) -> p a b", a=8), Hst[:].rearrange("p (a b) -> p a b", a=8),
                               etot.unsqueeze(2).broadcast_to([128, 8, 64]), ALU.mult, [THst, Tsm], [THst])
                            tt("dve", Hst[:], Hst[:], P[2][:, :], ALU.add, [THst, TP[2]], [THst])
                            cp("pool", Hbf[:], Hst[:], [THst], [THbf])

                        def s5_step(i, hf, q):
                            t0 = i * TB + hf * TS
                            hsl = slice(hf * TS, (hf + 1) * TS)
                            if q == 0:
                                bnd = (t0 + TS) if d == 1 else t0
                                if bnd % SEG == 0 and 0 < bnd < NT:
                                    ts("dve", carry[:].rearrange("p a b -> p (a b)"), carry[:].rearrange("p a b -> p (a b)"), cm[:, bnd // SEG:bnd // SEG + 1], ALU.mult, [Tcarry, Tcm], [Tcarry])
                            wA, TwA = s5w[q % 2], Ts5w[q % 2]
                            so = s5o[q % 2]
                            Tso0, Tso1 = Ts5o[q % 2]
                            mm(P[3][:, 0:TS], BB[:, 0, q, :], uT[:, q // 4, hsl], True, True, [TBB, TuT], [TP[3]])
                            mm(P[3][:, TS:2 * TS], BB[:, 1, q, :], uT[:, q // 4, hsl], True, True, [TBB, TuT], [TP[3]])
                            bre, bim, t1, t2, mre, mim, t3, t4 = [wA[:, j, :] for j in range(8)]
                            Tbre, Tbim, Tt1, Tt2, Tmre, Tmim, Tt3, Tt4 = Tslot[q % 2]
                            wre, wim = mre, mim
                            cq, sq_ = tab[:, 0, q, :], tab[:, 1, q, :]
                            act(wA[:, 0:2, :], P[3][:, 0:2 * TS].rearrange("p (a b) -> p a b", a=2), AF.Copy, [TP[3]], [Tbre, Tbim])
                            tt("pool", t1, bre, cq, ALU.mult, [Tbre, Ttab], [Tt1])
                            tt("pool", t2, bim, sq_, ALU.mult, [Tbim, Ttab], [Tt2])
                            tt("dve", t3, bim, cq, ALU.mult, [Tbim, Ttab], [Tt3])
                            tt("dve", t4, bre, sq_, ALU.mult, [Tbre, Ttab], [Tt4])
                            tt("dve", mim, t3, t4, ALU.subtract, [Tt3, Tt4], [Tmim])
                            tt("dve", mre, t1, t2, ALU.add, [Tt1, Tt2], [Tmre])
                            rb = s5s[:, 0, q:q + 1].broadcast_to([128, TS])
                            if d == 0:
                                op("dve", lambda e: e.tensor_tensor_scan(out=wre, data0=rb, data1=mre, initial=carry[:, 0, q:q + 1], op0=ALU.mult, op1=ALU.add), [Tmre, Ts5s, Tcarry], [Tmre])
                                op("dve", lambda e: e.tensor_tensor_scan(out=wim, data0=rb, data1=mim, initial=carry[:, 1, q:q + 1], op0=ALU.mult, op1=ALU.add), [Tmim, Ts5s, Tcarry], [Tmim])
                                lastc = TS - 1
                            else:
                                op("dve", lambda e: e.tensor_tensor_scan(out=wre[:, ::-1], data0=rb, data1=mre[:, ::-1], initial=carry[:, 0, q:q + 1], op0=ALU.mult, op1=ALU.add), [Tmre, Ts5s, Tcarry], [Tmre])
                                op("dve", lambda e: e.tensor_tensor_scan(out=wim[:, ::-1], data0=rb, data1=mim[:, ::-1], initial=carry[:, 1, q:q + 1], op0=ALU.mult, op1=ALU.add), [Tmim, Ts5s, Tcarry], [Tmim])
                                lastc = 0
                            c5, s5_ = s5s[:, 1, q:q + 1], s5s[:, 2, q:q + 1]
                            Rc = [Tmre, Tmim, Ts5s, Tcarry]
                            tt("pool", s5s[:, 5, q:q + 1], wre[:, lastc:lastc + 1], c5, ALU.mult, Rc, [Tctmp])
                            tt("pool", s5s[:, 6, q:q + 1], wim[:, lastc:lastc + 1], s5_, ALU.mult, Rc, [Tctmp])
                            tt("pool", carry[:, 0, q:q + 1], s5s[:, 5, q:q + 1], s5s[:, 6, q:q + 1], ALU.subtract, [Tctmp], [Tcarry])
                            tt("pool", s5s[:, 5, q:q + 1], wre[:, lastc:lastc + 1], s5_, ALU.mult, Rc, [Tctmp])
                            tt("pool", s5s[:, 6, q:q + 1], wim[:, lastc:lastc + 1], c5, ALU.mult, Rc, [Tctmp])
                            tt("pool", carry[:, 1, q:q + 1], s5s[:, 5, q:q + 1], s5s[:, 6, q:q + 1], ALU.add, [Tctmp], [Tcarry])
                            tt("pool", t1, wre, cq, ALU.mult, [Tmre, Ttab], [Tt1])
                            tt("pool", t2, wim, sq_, ALU.mult, [Tmim, Ttab], [Tt2])
                            tt("dve", t3, wre, sq_, ALU.mult, [Tmre, Ttab], [Tt3])
                            tt("dve", t4, wim, cq, ALU.mult, [Tmim, Ttab], [Tt4])
                            tt("dve", so[:, 1, :], t3, t4, ALU.add, [Tt3, Tt4], [Tso1])
                            tt("dve", so[:, 0, :], t1, t2, ALU.subtract, [Tt1, Tt2], [Tso0])
                            ut = q // 4
                            pcs = slice(ut * TS, (ut + 1) * TS)
                            mm(P[5][:, pcs], CC[:, 0, q, :], so[:, 0, :], q % 4 == 0, False, [TCC, Tso0], [TP[5]])
                            mm(P[5][:, pcs], CC[:, 1, q, :], so[:, 1, :], False, q % 4 == 3, [TCC, Tso1], [TP[5]])
                            if q % 4 == 3:
                                cp("act", y5d[:, ut, hsl], P[5][:, pcs], [TP[5]], [Ty5d])

                        def s5_steps(i):
                            halves = (1, 0) if d == 1 else (0, 1)
                            return [(hf, q) for hf in halves for q in range(8)]

                        if d == 1:
                            stageA(order[0])
                        for oi, i in enumerate(order):
                            t0 = i * TB
                            if d == 1:
                                if oi + 1 < len(order):
                                    stageA(order[oi + 1])
                                halos(i)
                            else:
                                stageA(i)
                            h, Th = hT[i % 3], ThT[i % 3]
                            xt, Txt = xt2[i % 2], Txt2[i % 2]
                            if d == 1:
                                def convtile(j):
                                    pm = j % 3
                                    ph = 3 + (j % 2)
                                    a = acc[j % 2]; Ta = Tacc[j % 2]
                                    for k in range(KT):
                                        mm(P[pm][:, 0:TB], Win[:, k, cXBC + 128 * j:cXBC + 128 * j + 128], h[:, k, 1:TB + 1], k == 0, k == KT - 1, [TWin, Th], [TP[pm]])
                                    for k in range(KT):
                                        mm(P[ph][:, 2 * j:2 * j + 2], Win[:, k, cXBC + 128 * j:cXBC + 128 * j + 128], h[:, k, 0:TB + 2:TB + 1], k == 0, k == KT - 1, [TWin, Th], [TP[ph]])
                                    w0, w1, w2, bb = [ssdcw[:, j, z:z + 1] for z in range(4)]
                                    act(a[:], P[pm][:, 0:TB], AF.Identity, [TP[pm], Tssdcw], [Ta], bias=bb, scale=w1)
                                    stt(a[:, 1:TB], P[pm][:, 0:TB - 1], w0, a[:, 1:TB], ALU.mult, ALU.add, [TP[pm], Ta, Tssdcw], [Ta])
                                    stt(a[:, 0:TB - 1], P[pm][:, 1:TB], w2, a[:, 0:TB - 1], ALU.mult, ALU.add, [TP[pm], Ta, Tssdcw], [Ta])
                                    stt(a[:, 0:1], P[ph][:, 2 * j:2 * j + 1], w0, a[:, 0:1], ALU.mult, ALU.add, [TP[ph], Ta, Tssdcw], [Ta])
                                    stt(a[:, TB - 1:TB], P[ph][:, 2 * j + 1:2 * j + 2], w2, a[:, TB - 1:TB], ALU.mult, ALU.add, [TP[ph], Ta, Tssdcw], [Ta])
                                    act(xbc[:, j, :], a[:], AF.Silu, [Ta], [Txbcl])
                                for j in range(0, 8, 2):
                                    fw.emit_merged(fw.record(lambda: convtile(j)), fw.record(lambda: convtile(j + 1)))
                                for j in range(2):
                                    pm = j % 3
                                    for k in range(KT):
                                        mm(P[pm][:, 0:TB], Win[:, k, cS5 + 128 * j:cS5 + 128 * j + 128], h[:, k, 1:TB + 1], k == 0, k == KT - 1, [TWin, Th], [TP[pm]])
                                    cp("act", uT[:, j, :], P[pm][:, 0:TB], [TP[pm]], [TuT])
                                dma(xbcS[:, i], xbc[:], R=[Txbcl], W=[Txbc[i]])
                                dma(uS[:, i], uT[:], R=[TuT], W=[Tu[i]])
                                steps = s5_steps(i)
                                dt_tile(h, Th)
                                for ci, c in enumerate((3, 2, 1, 0)):
                                    def chainA(c=c):
                                        ssd_chunk(i, c, h, Th)
                                        dma(ybS[t0 + c * 128:t0 + (c + 1) * 128, :], ych[:], R=[Tych], W=[Tyb[(t0 + c * 128) // 128]])

                                    def chainB(ci=ci):
                                        for (hf, q) in steps[4 * ci:4 * ci + 4]:
                                            s5_step(i, hf, q)
                                    fw.emit_merged(fw.record(chainA), fw.record(chainB))
                                dma(yb5S[:, i], y5d[:], R=[Ty5d], W=[Tyb5[i]])
                            else:
                                dma(xbc[:], xbcS[:, i], R=[Txbc[i]], W=[Txbcl])
                                dma(uT[:], uS[:, i], R=[Tu[i]], W=[TuT])
                                dma(yb5l[:], yb5S[:, i], R=[Tyb5[i]], W=[Tyb5l])
                                for j in range(4):
                                    pm = j % 3
                                    for k in range(KT):
                                        mm(P[pm][:, 0:TB], Win[:, k, cSGU + 128 * j:cSGU + 128 * j + 128], h[:, k, 1:TB + 1], k == 0, k == KT - 1, [TWin, Th], [TP[pm]])
                                    act(uvg[:, j, :], P[pm][:, 0:TB], AF.Gelu_apprx_tanh, [TP[pm]], [Tuvg])
                                act(sq[:, 0:2, :], uvg[:, 2:4, :], AF.Square, [Tuvg], [Tsq])
                                cp("pool", sq[:, 2:4, :], uvg[:, 2:4, :], [Tuvg], [Tsq])
                                for k in range(2):
                                    mm(P[3][:, 0:TB], ones_b, sq[:, 2 + k, :], k == 0, k == 1, [Tcstb, Tsq], [TP[3]])
                                for k in range(2):
                                    mm(P[6][:, 0:TB], ones_b, sq[:, k, :], k == 0, k == 1, [Tcstb, Tsq], [TP[6]])
                                a0, a1 = acc[0], acc[1]
                                ts("dve", a0[:], P[3][:, 0:TB], 1.0 / 256, ALU.mult, [TP[3]], [Tacc[0]])
                                tt("dve", a1[:], a0[:], a0[:], ALU.mult, [Tacc[0]], [Tacc[1]])
                                stt(a1[:], P[6][:, 0:TB], 1.0 / 256, a1[:], ALU.mult, ALU.subtract, [TP[6], Tacc[1]], [Tacc[1]])
                                act(a1[:], a1[:], AF.Sqrt, [Tacc[1], Teps], [Tacc[1]], bias=epsb[:, 1:2], scale=1.0)
                                op("dve", lambda e: e.reciprocal(out=a1[:], in_=a1[:]), [Tacc[1]], [Tacc[1]])
                                for k in range(2):
                                    tt("dve", uvg[:, 2 + k, :], uvg[:, 2 + k, :], a0[:], ALU.subtract, [Tuvg, Tacc[0]], [Tuvg])
                                    tt("dve", uvg[:, 2 + k, :], uvg[:, 2 + k, :], a1[:], ALU.mult, [Tuvg, Tacc[1]], [Tuvg])
                                    act(sq[:, 4 + k, :], uvg[:, 2 + k, :], AF.Identity, [Tuvg, Ts5vec], [Tsq], bias=s5vec[:, k, 3:4], scale=s5vec[:, k, 2:3])
                                steps = s5_steps(i)
                                dt_tile(h, Th)
                                for c in range(4):
                                    def chainA(c=c):
                                        cs = slice(c * 128, (c + 1) * 128)
                                        hs = slice(1 + c * 128, 1 + (c + 1) * 128)
                                        if "chunk" in SKIP:
                                            return
                                        dma(ybl[:], ybS[t0 + c * 128:t0 + (c + 1) * 128, :], R=[Tyb[(t0 + c * 128) // 128]], W=[Tybl])
                                        ssd_chunk(i, c, h, Th)
                                        tt("pool", ych[:], ych[:], ybl[:], ALU.add, [Tych, Tybl], [Tych])
                                        tt("pool", ych[:], ych[:], ytmp[:], ALU.add, [Tych, Tytmp], [Tych])
                                        for k in range(KT):
                                            mm(P[1][:, :], h[:, k, hs], Win[:, k, cZ:cZ + 512], k == 0, k == KT - 1, [Th, TWin], [TP[1]])
                                        act(ztmp[:], P[1][:, :], AF.Silu, [TP[1]], [Tztmp])
                                        tt("dve", ych[:], ych[:], ztmp[:], ALU.mult, [Tych, Tztmp], [Tych])
                                        mset("dve", ssq[:, 0:1], 0.0, [Tssq])
                                        act(ztmp[:], ych[:], AF.Square, [Tych, Tssq], [Tztmp, Tssq], accum=ssq[:, 0:1])
                                        act(ssq[:, 1:2], ssq[:, 0:1], AF.Sqrt, [Tssq, Teps], [Tssq], bias=epsb[:, 0:1], scale=1.0 / 512)
                                        op("dve", lambda e: e.reciprocal(out=ssq[:, 2:3], in_=ssq[:, 1:2]), [Tssq], [Tssq])
                                        act(ytok[:], ych[:], AF.Identity, [Tych, Tssq], [Tytok], scale=ssq[:, 2:3])
                                        for j in range(4):
                                            tr(PT[:, 128 * j:128 * j + 128], ytok[:, 128 * j:128 * j + 128], ident_b, [Tytok, Tcstb], [TPT])
                                        for j in range(4):
                                            cp("act", mixT[:, j, cs], PT[:, 128 * j:128 * j + 128], [TPT], [TmixT])
                                        if "sgu" in SKIP:
                                            return
                                        for j in range(2 if "notr" not in SKIP else 0):
                                            if "sgx" in SKIP:
                                                tr(PT[:, 512 + 128 * j:640 + 128 * j], xbc[:, 4 + j, cs], ident_b, [Txbcl, Tcstb], [TPT])
                                            else:
                                                tr(PT[:, 512 + 128 * j:640 + 128 * j], sq[:, 4 + j, cs], ident_b, [Tsq, Tcstb], [TPT])
                                        if "nocp" not in SKIP:
                                            cp("dve", vtok[:], PT[:, 512:768], [TPT], [Tvtok])
                                        if "sg1" in SKIP:
                                            return
                                        for hh in range(4):
                                            mm(P[6][:, 64 * hh:64 * hh + 64], sguw[:, hh, :], vtok[:, 64 * hh:64 * hh + 64], True, True, [Tsguw, Tvtok], [TP[6]])
                                        tt("dve", mixtok[:].rearrange("p (a b) -> p a b", a=4), P[6][:, 0:256].rearrange("p (a b) -> p a b", a=4),
                                           sgub[:].unsqueeze(2).broadcast_to([128, 4, 64]), ALU.add, [TP[6], Tsgub], [Tmixtok])
                                        if "sg2" in SKIP:
                                            return
                                        for j in range(2):
                                            tr(PT[:, 768 + 128 * j:896 + 128 * j], mixtok[:, 128 * j:128 * j + 128], ident_b, [Tmixtok, Tcstb], [TPT])
                                        if "sg3" in SKIP:
                                            return
                                        for j in range(2):
                                            tt("dve", uvg[:, j, cs], uvg[:, j, cs], PT[:, 768 + 128 * j:896 + 128 * j], ALU.mult, [Tuvg, TPT], [Tuvg])

                                    def chainB(c=c):
                                        for (hf, q) in steps[4 * c:4 * c + 4]:
                                            s5_step(i, hf, q)
                                    fw.emit_merged(fw.record(chainA), fw.record(chainB))
                                norm_stage(uvg[:, 0:2, :], Tuvg, 2, sq, Tsq, mixT[:, 6:8, :], TmixT, TB, 256, rtmp, Trt, pbank=3, eng="pool")
                                y5, Ty5 = y5d, Ty5d
                                for k in range(2):
                                    tt("pool", y5[:, k, :], y5d[:, k, :], yb5l[:, k, :], ALU.add, [Ty5d, Tyb5l], [Ty5])
                                    cp("pool", acc[k][:], uT[:, k, :], [TuT], [Tacc[k]])
                                    stt(y5[:, k, :], acc[k][:], s5vec[:, k, 0:1], y5[:, k, :], ALU.mult, ALU.add, [Tacc[k], Ts5vec, Ty5], [Ty5])
                                    act(y5[:, k, :], y5[:, k, :], AF.Gelu_apprx_tanh, [Ty5], [Ty5])
                                cp("pool", y5g[:], y5[:], [Ty5], [Ty5g])
                                for m in range(2):
                                    for k in range(2):
                                        mm(P[m][:, 0:TB], gluw[:, k, 128 * m:128 * m + 128], y5g[:, k, :], k == 0, k == 1, [Tgluw, Ty5g], [TP[m]])
                                    act(acc[m][:], P[m][:, 0:TB], AF.Sigmoid, [TP[m], Ts5vec], [Tacc[m]], bias=s5vec[:, m, 1:2], scale=1.0)
                                    tt("dve", y5[:, m, :], y5[:, m, :], acc[m][:], ALU.mult, [Ty5, Tacc[m]], [Ty5])
                                norm_stage(y5[:], Ty5, 2, sq, Tsq, mixT[:, 4:6, :], TmixT, TB, 256, rtmp, Trt, pbank=3, eng="pool")
                                for m in range(KT if "out" not in SKIP else 0):
                                    pm = m % 3
                                    for k in range(KT):
                                        mm(P[pm][:, 0:TB], Wout[:, k, 128 * m:128 * m + 128], mixT[:, k, :], k == 0, k == KT - 1, [TWout, TmixT], [TP[pm]])
                                    tt("dve", xt[:, m, :], xt[:, m, :], P[pm][:, 0:TB], ALU.add, [Txt, TP[pm]], [Txt])
                                dma(xB[:, i], xt[:], R=[Txt], W=xtr(TxB, t0, TB))
                        fw.new_phase()
                with contextlib.ExitStack() as sw:
                    ffncw = sbt(sw, "ffncw", [128, NFF, 4], F32); Tffncw = T()
                    dma(ffncw[:], ffncw_d[l], W=[Tffncw])
                    Wup = sbt(sw, "Wup", [128, KT, 2 * DFF], BF16); TWup = T()
                    Wdn = sbt(sw, "Wdn", [128, 22, D], BF16); TWdn = T()
                    xt2 = [sbt(sw, "fx%d" % i, [128, KT, TF], F32) for i in range(2)]; Txt2 = [T(), T()]
                    sq = sbt(sw, "fsq", [128, KT, TF], BF16); Tsq = T()
                    hT = [sbt(sw, "fh%d" % i, [128, KT, TF + 2], BF16) for i in range(3)]; ThT = [T(), T(), T()]
                    rtmp = sbt(sw, "frt", [128, TF], F32); Trt = T()
                    actb = sbt(sw, "actb", [128, 22, TF], BF16); Tactb = T()
                    acc = [sbt(sw, "facc%d" % i, [128, TF], F32) for i in range(6)]; Tacc = [T() for _ in range(6)]
                    stg = [sbt(sw, "stg%d" % i, [128, 1024], F32) for i in range(4)]; Tstg = [T() for _ in range(4)]
                    usrc = w_up_d[l].rearrange("(k p) n -> p k n", p=128)
                    load_w(stg, Tstg, Wup, TWup, lambda k, c0, c1: usrc[:, k, c0:c1], KT, 2 * DFF, nrm[:, 1, :], Tnrm)
                    dsrc = w_down_d[l].rearrange("(k p) n -> p k n", p=128)
                    load_w(stg, Tstg, Wdn, TWdn, lambda k, c0, c1: dsrc[:, k, c0:c1], 22, D)
                    if l == 1 and fw.dry:
                        print("sbuf remaining ffn:", nc.sbuf_bytes_remaining)

                    def stageAF(i):
                        t0 = i * TF
                        xt, Txt = xt2[i % 2], Txt2[i % 2]
                        for k in range(KT):
                            dma(xt[:, k, :], xB[:, t0 // 512, k, t0 % 512:t0 % 512 + TF], R=xtr(TxB, t0, TF), W=[Txt])
                        norm_stage(xt[:], Txt, KT, sq, Tsq, hT[i % 3][:, :, 1:TF + 1], ThT[i % 3], TF, D, rtmp, Trt)

                    def halosF(i):
                        h, Th = hT[i % 3], ThT[i % 3]
                        t0 = i * TF
                        for side in (0, 1):
                            j = i - 1 if side == 0 else i + 1
                            col = 0 if side == 0 else TF + 1
                            if j < 0 or j >= NTF:
                                mset("pool", h[:, :, col:col + 1], 0.0, [Th])
                                continue
                            hn, Tn = hT[j % 3], ThT[j % 3]
                            scol = TF if side == 0 else 1
                            bnd = t0 if side == 0 else t0 + TF
                            if bnd % SEG == 0:
                                ts("pool", h[:, :, col:col + 1], hn[:, :, scol:scol + 1], cm[:, bnd // SEG:bnd // SEG + 1], ALU.mult, [Tn, Tcm], [Th])
                            else:
                                cp("pool", h[:, :, col:col + 1], hn[:, :, scol:scol + 1], [Tn], [Th])

                    stageAF(0)
                    for i in range(NTF):
                        t0 = i * TF
                        if i + 1 < NTF:
                            stageAF(i + 1)
                        halosF(i)
                        h, Th = hT[i % 3], ThT[i % 3]
                        xt, Txt = xt2[i % 2], Txt2[i % 2]
                        for jj in range(22):
                            accs = [None, None]

                            def halfops(half, jj=jj):
                                j = jj + 22 * half
                                pm = (0, 1, 2, 3, 4, 5, 6)[(2 * jj + half) % 7]
                                a = acc[2 * (jj % 3) + half]; Ta = Tacc[2 * (jj % 3) + half]
                                for k in range(KT):
                                    mm(P[pm][:, 0:TF + 2], Wup[:, k, 128 * j:128 * j + 128], h[:, k, 0:TF + 2], k == 0, k == KT - 1, [TWup, Th], [TP[pm]])
                                w0, w1, w2, bb = [ffncw[:, j, z:z + 1] for z in range(4)]
                                act(a[:], P[pm][:, 1:TF + 1], AF.Identity, [TP[pm], Tffncw], [Ta], bias=bb, scale=w1)
                                stt(a[:], P[pm][:, 0:TF], w0, a[:], ALU.mult, ALU.add, [TP[pm], Ta, Tffncw], [Ta])
                                stt(a[:], P[pm][:, 2:TF + 2], w2, a[:], ALU.mult, ALU.add, [TP[pm], Ta, Tffncw], [Ta])
                                accs[half] = (a, Ta)
                            fw.emit_merged(fw.record(lambda: halfops(0)), fw.record(lambda: halfops(1)))
                            (ag, Tag), (av, Tav) = accs
                            act(ag[:], ag[:], AF.Silu, [Tag], [Tag])
                            tt("pool", actb[:, jj, :], ag[:], av[:], ALU.mult, [Tag, Tav], [Tactb])
                        for m in range(KT):
                            pm = 5 + (m % 2)
                            for k in range(22):
                                mm(P[pm][:, 0:TF], Wdn[:, k, 128 * m:128 * m + 128], actb[:, k, :], k == 0, k == 21, [TWdn, Tactb], [TP[pm]])
                            tt("dve", xt[:, m, :], xt[:, m, :], P[pm][:, 0:TF], ALU.add, [Txt, TP[pm]], [Txt])
                            dma(xA[:, t0 // 512, m, t0 % 512:t0 % 512 + TF], xt[:, m, :], R=[Txt], W=xtr(TxA, t0, TF))
                    fw.new_phase()
                nrmprev_keep = None
            if not last_layer:
                nrmprev = sbt(top, "nrmprev", [128, 4, 8], F32); Tnrmprev = T()
                dma(nrmprev[:], nrm_d[l], W=[Tnrmprev])

        with contextlib.ExitStack() as sw:
            TE = 512
            nrm = sbt(sw, "enrm", [128, 4, 8], F32); Tnrm = T()
            dma(nrm[:], nrm_d[L - 1], W=[Tnrm])
            Wg = sbt(sw, "eWg", [128, KT, D], BF16); TWg = T()
            Wp = sbt(sw, "eWp", [128, 2, D], BF16); TWp = T()
            stg = [sbt(sw, "estg%d" % i, [128, 1024], F32) for i in range(4)]; Tstg = [T() for _ in range(4)]
            gsrc = ple_gate_d[L - 1].rearrange("(k p) n -> p k n", p=128)
            load_w(stg, Tstg, Wg, TWg, lambda k, c0, c1: gsrc[:, k, c0:c1], KT, D, nrm[:, 2, :], Tnrm)
            psrc = ple_proj_d[L - 1].rearrange("(k p) n -> p k n", p=128)
            load_w(stg, Tstg, Wp, TWp, lambda k, c0, c1: psrc[:, k, c0:c1], 2, D)
            xt2 = [sbt(sw, "ex%d" % i, [128, KT, TE], F32) for i in range(2)]; Txt2 = [T(), T()]
            sq = sbt(sw, "esq", [128, KT, TE], BF16); Tsq = T()
            hb = [sbt(sw, "eh%d" % i, [128, KT, TE], BF16) for i in range(2)]; Thb = [T(), T()]
            rtmp = sbt(sw, "ert", [128, TE], F32); Trt = T()
            pt = sbt(sw, "ept", [128, 2, TE], F32); Tpt = T()
            ptb = sbt(sw, "eptb", [128, 2, TE], BF16); Tptb = T()
            gsb = sbt(sw, "egsb", [128, TE], F32); Tgsb = T()
            ot = [sbt(sw, "eo%d" % i, [128, KT, TE], F32) for i in range(2)]; Tot = [T(), T()]
            for i in range(NT // TE):
                t0 = i * TE
                xt, Txt = xt2[i % 2], Txt2[i % 2]
                h, Th = hb[i % 2], Thb[i % 2]
                o, To = ot[i % 2], Tot[i % 2]
                dma(xt[:], xA[:, i], R=xtr(TxA, t0, TE), W=[Txt])
                norm_stage(xt[:], Txt, KT, sq, Tsq, h[:], Th, TE, D, rtmp, Trt)
                dma(pt[:], pT[L - 1][:, i], W=[Tpt])
                cp("pool", ptb[:], pt[:], [Tpt], [Tptb])
                for m in range(KT):
                    pg, pe_ = 2 * (m % 2), 2 * (m % 2) + 1
                    for k in range(KT):
                        mm(P[pg][:, 0:TE], Wg[:, k, m * 128:(m + 1) * 128], h[:, k, :], k == 0, k == KT - 1, [TWg, Th], [TP[pg]])
                    for k in range(2):
                        mm(P[pe_][:, 0:TE], Wp[:, k, m * 128:(m + 1) * 128], ptb[:, k, :], k == 0, k == 1, [TWp, Tptb], [TP[pe_]])
                    act(gsb[:], P[pg][:, 0:TE], AF.Sigmoid, [TP[pg]], [Tgsb])
                    tt("dve", gsb[:], gsb[:], P[pe_][:, 0:TE], ALU.mult, [Tgsb, TP[pe_]], [Tgsb])
                    tt("pool", xt[:, m, :], xt[:, m, :], gsb[:], ALU.add, [Txt, Tgsb], [Txt])
                act(sq[:], xt[:], AF.Square, [Txt], [Tsq])
                for k in range(KT):
                    mm(P[6][:, 0:TE], ones_b, sq[:, k, :], k == 0, k == KT - 1, [Tcstb, Tsq], [TP[6]])
                act(rtmp[:], P[6][:, 0:TE], AF.Sqrt, [TP[6], Teps], [Trt], bias=epsb[:, 0:1], scale=1.0 / D)
                op("dve", lambda e: e.reciprocal(out=rtmp[:], in_=rtmp[:]), [Trt], [Trt])
                for k in range(KT):
                    stt(o[:, k, :], xt[:, k, :], nrm[:, 3, k:k + 1], rtmp[:], ALU.mult, ALU.mult, [Txt, Tnrm, Trt], [To])
                dma(yT[:, i], o[:], R=[To])
        fw.barrier()
        print("FW idx", {k: v for k, v in fw.idx.items() if k[0] != "q"}, "dma", fw.ndma, "ep", fw.ep)
    return nc


def _prep_weights(inp, L):
    f = lambda a: np.ascontiguousarray(np.asarray(a, dtype=np.float32))
    w = {}
    w["w_in"] = f(inp["w_in"][:L]); w["w_out"] = f(inp["w_out"][:L]); w["w_up"] = f(inp["ffn_w_up"][:L])
    w["w_down"] = f(inp["ffn_w_down"][:L]); w["ple_proj"] = f(inp["ple_proj"][:L]); w["ple_gate"] = f(inp["ple_gate_w"][:L])
    w["glu_w"] = f(inp["s5_glu_w"][:L])
    w["sgu_wT"] = f(np.transpose(np.asarray(inp["sgu_w"][:L]), (0, 3, 1, 2)))
    fin = np.broadcast_to(np.asarray(inp["final_norm"])[None], (L, D))
    nr = np.stack([inp["norm_mix"][:L], inp["norm_ffn"][:L], inp["ple_norm"][:L], fin], 1)
    w["nrm"] = f(nr.reshape(L, 4, 8, 128).transpose(0, 3, 1, 2))
    mx = np.concatenate([inp["ssd_norm"][:L], inp["s5_out_norm"][:L], inp["sgu_out_norm"][:L]], 1)
    w["mixn"] = f(mx.reshape(L, 8, 128).transpose(0, 2, 1))
    cw = np.concatenate([np.asarray(inp["ssd_conv_w"][:L]), np.asarray(inp["ssd_conv_b"][:L])[:, None]], 1)
    w["ssdcw"] = f(cw.reshape(L, 4, 8, 128).transpose(0, 3, 2, 1))
    fc = np.concatenate([np.asarray(inp["ffn_conv_w"][:L]), np.asarray(inp["ffn_conv_b"][:L])[:, None]], 1)
    w["ffncw"] = f(fc.reshape(L, 4, NFF, 128).transpose(0, 3, 2, 1))
    dr = np.zeros((L, 3, 16), np.float32)
    dr[:, 0] = np.asarray(inp["ssd_dt_bias"][:L]).reshape(L, 16)
    dr[:, 1] = np.asarray(inp["ssd_a_log"][:L]).reshape(L, 16)
    dr[:, 2, :8] = np.asarray(inp["ssd_d"][:L])
    w["dtrow"] = dr
    sv = np.zeros((L, 6, 256), np.float32)
    sv[:, 0] = inp["s5_d"][:L]; sv[:, 1] = inp["s5_glu_b"][:L]; sv[:, 2] = inp["sgu_norm_w"][:L]; sv[:, 3] = inp["sgu_norm_b"][:L]
    w["s5vec"] = f(sv.reshape(L, 6, 2, 128).transpose(0, 3, 2, 1))
    w["sgub"] = f(np.transpose(np.asarray(inp["sgu_b"][:L]), (0, 2, 1)))
    lam = np.stack([np.asarray(inp["s5_lambda_re"][:L]), np.asarray(inp["s5_lambda_im"][:L]),
                    np.broadcast_to(np.asarray(inp["s5_log_step"][:L])[..., None], (L, 2, 16, 64))], 2)
    ls = lam.reshape(L, 2, 3, 8, 2, 64).transpose(0, 1, 4, 5, 2, 3).reshape(L, 2, 128, 3, 8)
    w["lamS"] = f(ls)
    w["lamR"] = f(lam.reshape(L, 2, 3, 1024))
    bsp = np.zeros((L, 2, 128, 8, 128), np.float32)
    csp = np.zeros((L, 2, 128, 8, 128), np.float32)
    for ri, (bn, cn) in enumerate((("s5_b_re", "s5_c_re"), ("s5_b_im", "s5_c_im"))):
        b = np.asarray(inp[bn][:L]); c = np.asarray(inp[cn][:L])
        for g in range(16):
            q, hh, r0 = g // 2, (g % 2) * 64, (g % 8) * 16
            bsp[:, ri, r0:r0 + 16, q, hh:hh + 64] = np.transpose(b[:, g], (0, 2, 1))
            csp[:, ri, hh:hh + 64, q, r0:r0 + 16] = np.transpose(c[:, g], (0, 2, 1))
    w["bsp"] = bsp; w["csp"] = csp
    cst = np.zeros((128, 6, 128), np.float32)
    s = np.arange(128)[:, None]; t = np.arange(128)[None, :]
    cst[:, 0] = (s == t); cst[:, 1] = (s <= t); cst[:, 2] = (s >= t); cst[:, 3] = (s > t); cst[:, 4] = (s < t); cst[:, 5] = 1.0
    w["consts"] = cst
    return w


_NC_CACHE = {}


def run_streams(inp, streams, L, NSEG, SEG):
    key = (L, NSEG, SEG)
    if key not in _NC_CACHE:
        _NC_CACHE[key] = build(L, NSEG, SEG)
    nc = _NC_CACHE[key]
    w = _prep_weights(inp, L)
    in_maps = []
    for s in streams:
        m = dict(w)
        NTs = s["x"].shape[0]
        m["xT"] = np.ascontiguousarray(s["x"].astype(np.float32).reshape(NTs // 512, 512, 8, 128).transpose(3, 0, 2, 1))
        m["pT"] = np.ascontiguousarray(s["p"].astype(np.float32).reshape(L, NTs // 512, 512, 2, 128).transpose(0, 4, 1, 3, 2))
        m["cm"] = np.ascontiguousarray(np.broadcast_to(np.asarray(s["cm"], np.float32)[None], (128, NSEG + 1)))
        in_maps.append(m)
    res = run_bass_kernel_spmd(nc, in_maps, core_ids=list(range(len(streams))))
    return [np.ascontiguousarray(np.transpose(r["yT"], (1, 3, 2, 0)).reshape(-1, D)) for r in res.results]


def kernel(**inp):
    L, NSEG, SEG = 4, 4, 2048
    NT = NSEG * SEG
    xp = np.asarray(inp["x_prompt"]); xs = np.asarray(inp["x_sample"])
    pp = np.asarray(inp["p_prompt"]); ps = np.asarray(inp["p_sample"])
    streams = []
    for c in range(8):
        if c < 4:
            x = xp[4 * c:4 * c + 4].reshape(NT, D)
            p = pp[:, 4 * c:4 * c + 4].reshape(L, NT, PLE)
            cmv = np.zeros(NSEG + 1, np.float32)
        elif c < 6:
            x = np.zeros((NT, D), np.float32); p = np.zeros((L, NT, PLE), np.float32)
            cmv = np.zeros(NSEG + 1, np.float32)
        else:
            x = xs[c - 6]; p = ps[:, c - 6]
            cmv = np.zeros(NSEG + 1, np.float32); cmv[1:NSEG] = 1.0
        streams.append(dict(x=x, p=p, cm=cmv))
    outs = run_streams(inp, streams, L, NSEG, SEG)
    y_prompt = np.stack([outs[c].reshape(4, SEG, D) for c in range(4)], 0).reshape(16, SEG, D).astype(np.float32)
    y_sample = np.stack([outs[6], outs[7]], 0).astype(np.float32)
    return (y_prompt, y_sample)
```

```python
import contextlib
import math
import numpy as np
import ml_dtypes
import concourse.bass as bass
import concourse.mybir as mybir
from concourse.bass_utils import run_bass_kernel_spmd

F32 = mybir.dt.float32
BF16 = mybir.dt.bfloat16
I32 = mybir.dt.int32
AF = mybir.ActivationFunctionType
ALU = mybir.AluOpType

D = 1024
KT = 8
PLE = 256
DFF = 2816
NFF = 44
INW = 2320
EPS = 1e-6
O_Z, O_XBC, O_DT, O_S5, O_SGU = 0, 512, 1536, 1552, 1808


class T:
    __slots__ = ("w", "r", "ps")

    def __init__(self, ps=False):
        self.w = None
        self.r = []
        self.ps = ps


NDMA = 24


import os
STOP_AFTER = int(os.environ["KSTOP"]) if "KSTOP" in os.environ else None
SKIP = os.environ.get("KSKIP", "")


class _Stop(Exception):
    pass


class FW:
    def __init__(self, nc, sem_phases, dsems, needed=None):
        self.nc = nc
        self.e = {"pe": nc.tensor, "act": nc.scalar, "dve": nc.vector, "pool": nc.gpsimd, "sp": nc.sync}
        self.sem_phases = sem_phases
        self.ep = 0
        self.sem = dict(sem_phases[0])
        for j, d in enumerate(dsems):
            self.sem["q%d" % j] = d
        self.idx = {k: 0 for k in self.sem}
        self.val = {k: 0 for k in self.sem}
        self.rank = {k: {} for k in self.sem}
        self.seen = {k: {} for k in self.e}
        self.ndma = 0
        self.rec = None
        self.dry = needed is None
        self.needed = set() if needed is None else needed
        self.ninc = 0

    def _deps(self, R, W, en=None):
        deps = {}
        ep = self.ep
        for t in R:
            if t.w is not None:
                k, c, e = t.w
                if (e == ep or k[0] == "q") and deps.get(k, 0) < c:
                    deps[k] = c
            if t.ps:
                for (k, c, e) in t.r:
                    if k != en and (e == ep or k[0] == "q") and deps.get(k, 0) < c:
                        deps[k] = c
        for t in W:
            if t.w is not None:
                k, c, e = t.w
                if (e == ep or k[0] == "q") and deps.get(k, 0) < c:
                    deps[k] = c
            for (k, c, e) in t.r:
                if (e == ep or k[0] == "q") and deps.get(k, 0) < c:
                    deps[k] = c
        return deps

    def _wait(self, en, k, c):
        if k[0] == "q":
            self.e[en].wait_ge(self.sem[k], c)
        elif self.dry:
            self.needed.add((k, self.ep, c))
            self.e[en].wait_ge(self.sem[k], c)
        else:
            self.e[en].wait_ge(self.sem[k], self.rank[k][c])
        self.seen[en][k] = c

    def op(self, en, fn, R=(), W=(), inc=1):
        if self.rec is not None:
            self.rec.append((en, fn, tuple(R), tuple(W), inc))
            return None
        deps = self._deps(R, W, en)
        sk = en
        if en == "sp":
            sk = "q%d" % (self.ndma % NDMA)
            self.ndma += 1
            c = self.idx[sk]
            if c > 0 and deps.get(sk, 0) < c:
                deps[sk] = c
        seen = self.seen[en]
        for k, c in deps.items():
            if k == "pe" and en == "pe":
                continue
            if seen.get(k, 0) < c:
                self._wait(en, k, c)
        ins = fn(self.e[en])
        if en == "sp":
            self.idx[sk] += inc
            ins.then_inc(self.sem[sk], inc)
        else:
            self.idx[sk] += 1
            i = self.idx[sk]
            if self.dry or (sk, self.ep, i) in self.needed:
                self.val[sk] += 1
                self.rank[sk][i] = self.val[sk]
                ins.then_inc(self.sem[sk], 1)
                self.ninc += 1
        self.last_ins = ins
        tok = (sk, self.idx[sk], self.ep)
        for t in R:
            t.r.append(tok)
            if len(t.r) > 12:
                m = {}
                for (k, c, e) in t.r:
                    if (e == self.ep or k[0] == "q") and m.get(k, (0, 0))[0] < c:
                        m[k] = (c, e)
                t.r = [(k, c, e) for k, (c, e) in m.items()]
        for t in W:
            t.w = tok
            t.r = []
        return ins

    def record(self, f):
        assert self.rec is None
        self.rec = []
        f()
        r, self.rec = self.rec, None
        return r

    def emit_merged(self, a, b):
        ia = ib = 0
        na, nb = len(a), len(b)
        while ia < na or ib < nb:
            if ib >= nb or (ia < na and ia * nb <= ib * na):
                self.op(*a[ia]); ia += 1
            else:
                self.op(*b[ib]); ib += 1

    def barrier(self):
        for en in self.e:
            for k in self.sem:
                if k == en:
                    continue
                c = self.idx[k]
                if c > 0 and self.seen[en].get(k, 0) < c:
                    self._wait(en, k, c)

    def new_phase(self):
        self.barrier()
        if STOP_AFTER is not None and self.ep + 1 >= STOP_AFTER:
            raise _Stop()
        self.ep += 1
        assert self.ep < len(self.sem_phases)
        for k, h in self.sem_phases[self.ep].items():
            self.sem[k] = h
            self.idx[k] = 0
            self.val[k] = 0
            self.rank[k] = {}
            for en in self.seen:
                self.seen[en].pop(k, None)


def build(L, NSEG, SEG):
    needed = None
    for _pass in range(2):
        hold = {}
        try:
            nc = _build(L, NSEG, SEG, hold, needed)
        except _Stop:
            hold["fw"].barrier()
            nc = hold["nc"]
        needed = hold["fw"].needed
        print("pass", _pass, "incs", hold["fw"].ninc)
    return nc


def _build(L, NSEG, SEG, hold, needed):
    NT = NSEG * SEG
    TB = 512
    TS = 256
    TF = 256
    NTB = NT // TB
    NTF = NT // TF
    nc = bass.Bass("TRN2", target_bir_lowering=False)

    def din(name, shape, dt=F32):
        return nc.dram_tensor(name, list(shape), dt, kind="ExternalInput").ap()

    xT = din("xT", [128, NT // 512, KT, 512])
    pT = din("pT", [L, 128, NT // 512, 2, 512])
    cm_d = din("cm", [128, NSEG + 1])
    consts_d = din("consts", [128, 6, 128])
    w_in_d = din("w_in", [L, D, INW])
    w_out_d = din("w_out", [L, D, D])
    w_up_d = din("w_up", [L, D, 2 * DFF])
    w_down_d = din("w_down", [L, DFF, D])
    ple_proj_d = din("ple_proj", [L, PLE, D])
    ple_gate_d = din("ple_gate", [L, D, D])
    glu_w_d = din("glu_w", [L, 256, 256])
    sgu_wT_d = din("sgu_wT", [L, 128, 4, 128])
    nrm_d = din("nrm", [L, 128, 4, 8])
    mixn_d = din("mixn", [L, 128, 8])
    ssdcw_d = din("ssdcw", [L, 128, 8, 4])
    ffncw_d = din("ffncw", [L, 128, NFF, 4])
    dtrow_d = din("dtrow", [L, 3, 16])
    s5vec_d = din("s5vec", [L, 128, 2, 6])
    sgub_d = din("sgub", [L, 128, 4])
    lamS_d = din("lamS", [L, 2, 128, 3, 8])
    lamR_d = din("lamR", [L, 2, 3, 1024])
    bsp_d = din("bsp", [L, 2, 128, 8, 128])
    csp_d = din("csp", [L, 2, 128, 8, 128])
    yT = nc.dram_tensor("yT", [128, NT // 512, KT, 512], F32, kind="ExternalOutput").ap()
    xA = nc.dram_tensor("xA", [128, NT // 512, KT, 512], F32).ap()
    xB = nc.dram_tensor("xB", [128, NT // 512, KT, 512], F32).ap()
    xbcS = nc.dram_tensor("xbcS", [128, NT // 512, KT, 512], BF16).ap()
    uS = nc.dram_tensor("uS", [128, NT // 512, 2, 512], BF16).ap()
    ybS = nc.dram_tensor("ybS", [NT, 512], F32).ap()
    yb5S = nc.dram_tensor("yb5S", [128, NT // 512, 2, 512], F32).ap()

    with contextlib.ExitStack() as top:
        nph = 3 * L + 2
        sem_phases = [{n: top.enter_context(nc.semaphore("%s_%d" % (n, ph))) for n in ("pe", "act", "dve", "pool")} for ph in range(nph)]
        dsems = [top.enter_context(nc.semaphore("dq%d" % j)) for j in range(NDMA)]
        fw = FW(nc, sem_phases, dsems, needed)
        hold["fw"] = fw
        hold["nc"] = nc
        op = fw.op

        _cnt = [0]

        def sbt(st, name, shape, dt):
            _cnt[0] += 1
            return st.enter_context(nc.sbuf_tensor("s%d_%s" % (_cnt[0], name), list(shape), dt))

        def dma(out, in_, R=(), W=()):
            return op("sp", lambda e: e.dma_start(out=out, in_=in_), R, W, inc=16)

        def mm(out, lhsT, rhs, start, stop, R, W):
            return op("pe", lambda e: e.matmul(out, lhsT=lhsT, rhs=rhs, start=start, stop=stop), R, W)

        def tr(out, in_, ident, R, W):
            return op("pe", lambda e: e.transpose(out, in_, ident), R, W)

        def act(out, in_, func, R, W, bias=None, scale=None, accum=None):
            kw = {}
            if bias is not None:
                kw["bias"] = bias
            if scale is not None:
                kw["scale"] = scale
            if accum is not None:
                kw["accum_out"] = accum
            return op("act", lambda e: e.activation(out=out, in_=in_, func=func, **kw), R, W)

        def tt(en, out, in0, in1, o, R, W):
            return op(en, lambda e: e.tensor_tensor(out=out, in0=in0, in1=in1, op=o), R, W)

        def ts(en, out, in0, s1, o0, R, W, s2=None, o1=None):
            if o1 is None:
                return op(en, lambda e: e.tensor_scalar(out=out, in0=in0, scalar1=s1, scalar2=None, op0=o0), R, W)
            return op(en, lambda e: e.tensor_scalar(out=out, in0=in0, scalar1=s1, scalar2=s2, op0=o0, op1=o1), R, W)

        def stt(out, in0, sc, in1, o0, o1, R, W):
            return op("dve", lambda e: e.scalar_tensor_tensor(out=out, in0=in0, scalar=sc, in1=in1, op0=o0, op1=o1), R, W)

        def cp(en, out, in_, R, W):
            if en == "act":
                return act(out, in_, AF.Copy, R, W)
            return op(en, lambda e: e.tensor_copy(out=out, in_=in_), R, W)

        def mset(en, ap, val, W):
            return op(en, lambda e: e.memset(ap, val), (), W)

        cst = sbt(top, "cst", [128, 6, 128], F32); Tcst = T()
        cstb = sbt(top, "cstb", [128, 6, 128], BF16); Tcstb = T()
        cm = sbt(top, "cm", [128, NSEG + 1], F32); Tcm = T()
        epsb = sbt(top, "epsb", [128, 3], F32); Teps = T()
        dma(cst[:], consts_d[:, :, :], W=[Tcst])
        dma(cm[:], cm_d[:, :], W=[Tcm])
        cp("dve", cstb[:], cst[:], [Tcst], [Tcstb])
        mset("dve", epsb[:, 0:1], EPS, [Teps])
        mset("dve", epsb[:, 1:2], 1e-5, [Teps])
        mset("dve", epsb[:, 2:3], 1.0, [Teps])
        ident_b = cstb[:, 0, :]
        triF = cst[:, 1, :]
        triB = cst[:, 2, :]
        UF = cst[:, 3, :]
        UB = cst[:, 4, :]
        ones_f = cst[:, 5, :]
        ones_b = cstb[:, 5, :]
        maskF_b = cstb[:, 1, :]
        maskB_b = cstb[:, 2, :]

        P = []
        TP = []
        for i in range(7):
            P.append(top.enter_context(nc.psum_tensor("P%d" % i, [128, 512], F32)))
            TP.append(T(ps=True))
        PT = top.enter_context(nc.psum_tensor("PT", [128, 1024], BF16)); TPT = T(ps=True)

        TxA = [T() for _ in range(NT // 256)]
        TxB = [T() for _ in range(NT // 256)]
        TxIn = [T() for _ in range(NT // 256)]

        def xtr(TL, t0, n):
            return [TL[b] for b in range(t0 // 256, (t0 + n + 255) // 256)]

        Txbc = [T() for _ in range(NTB)]
        Tu = [T() for _ in range(NTB)]
        Tyb = [T() for _ in range(NT // 128)]
        Tyb5 = [T() for _ in range(NTB)]

        def load_w(stage, Tstage, dst, Tdst, src_kpn, nk, cols, scale_tile=None, Tscale=None, chunk=1024):
            si = 0
            for k in range(nk):
                for c0 in range(0, cols, chunk):
                    c1 = min(cols, c0 + chunk)
                    s, Ts = stage[si % len(stage)], Tstage[si % len(stage)]
                    si += 1
                    dma(s[:, 0:c1 - c0], src_kpn(k, c0, c1), W=[Ts])
                    if scale_tile is None:
                        act(dst[:, k, c0:c1], s[:, 0:c1 - c0], AF.Copy, [Ts], [Tdst])
                    else:
                        act(dst[:, k, c0:c1], s[:, 0:c1 - c0], AF.Identity, [Ts, Tscale], [Tdst], scale=scale_tile[:, k:k + 1])

        def norm_stage(xt, Txt, nkt, sq, Tsq, hT_main, ThT, width, nfeat, rtmp, Trt, pbank=6, eng="pool"):
            act(sq[:, 0:nkt, 0:width], xt, AF.Square, [Txt], [Tsq])
            for k in range(nkt):
                mm(P[pbank][:, 0:width], ones_b, sq[:, k, 0:width], k == 0, k == nkt - 1, [Tcstb, Tsq], [TP[pbank]])
            act(rtmp[:, 0:width], P[pbank][:, 0:width], AF.Ln, [TP[pbank], Teps], [Trt], bias=epsb[:, 0:1], scale=1.0 / nfeat)
            act(rtmp[:, 0:width], rtmp[:, 0:width], AF.Exp, [Trt], [Trt], scale=-0.5)
            tt(eng, hT_main, xt, rtmp[:, 0:width].unsqueeze(1).broadcast_to([128, nkt, width]), ALU.mult, [Txt, Trt], [ThT])

        for l in range(L):
            last_layer = (l == L - 1)
            with contextlib.ExitStack() as lay:
                nrm = sbt(lay, "nrm", [128, 4, 8], F32); Tnrm = T()
                mixn = sbt(lay, "mixn", [128, 8], F32); Tmixn = T()
                ssdcw = sbt(lay, "ssdcw", [128, 8, 4], F32); Tssdcw = T()
                dtrow = sbt(lay, "dtrow", [128, 3, 16], F32); Tdtrow = T()
                arow = sbt(lay, "arow", [128, 16], F32)
                s5vec = sbt(lay, "s5vec", [128, 2, 6], F32); Ts5vec = T()
                sgub = sbt(lay, "sgub", [128, 4], F32); Tsgub = T()
                carry = sbt(lay, "carry", [128, 2, 8], F32); Tcarry = T()
                Hst = sbt(lay, "Hst", [128, 512], F32); THst = T()
                Hbf = sbt(lay, "Hbf", [128, 512], BF16); THbf = T()
                dma(nrm[:], nrm_d[l], W=[Tnrm])
                dma(mixn[:], mixn_d[l], W=[Tmixn])
                dma(ssdcw[:], ssdcw_d[l], W=[Tssdcw])
                dma(dtrow[:].rearrange("p a b -> p (a b)"), dtrow_d[l:l + 1].rearrange("o a b -> o (a b)").partition_broadcast(128), W=[Tdtrow])
                dma(s5vec[:], s5vec_d[l], W=[Ts5vec])
                dma(sgub[:], sgub_d[l], W=[Tsgub])
                act(arow[:], dtrow[:, 1, :], AF.Exp, [Tdtrow], [Tdtrow])
                ts("dve", arow[:], arow[:], -1.0, ALU.mult, [Tdtrow], [Tdtrow])

                for d in (1, 0):
                    with contextlib.ExitStack() as sw:
                        xt1 = sbt(sw, "xt1", [128, KT, TB], F32); Txt1 = T()
                        xt2 = [xt1, xt1]; Txt2 = [Txt1, Txt1]
                        sq = sbt(sw, "sq", [128, KT, TB], BF16); Tsq = T()
                        hT = [sbt(sw, "hT%d" % i, [128, KT, TB + 2], BF16) for i in range(3)]; ThT = [T(), T(), T()]
                        rtmp = sbt(sw, "rtmp", [128, TB], F32); Trt = T()
                        if d == 1:
                            gen = sbt(sw, "gen", [128, 2560], F32)
                            ncol = 1296
                            Win = sbt(sw, "Win", [128, KT, ncol], BF16); TWin = T()
                            cXBC, cDT, cS5 = 0, 1024, 1040
                            if l > 0:
                                Wg = sbt(sw, "Wg", [128, KT, D], BF16); TWg = T()
                                Wp = sbt(sw, "Wp", [128, 2, D], BF16); TWp = T()
                                pt = sbt(sw, "pt", [128, 2, TB], F32); Tpt = T()
                                ptb = sbt(sw, "ptb", [128, 2, TB], BF16); Tptb = T()
                                gsb = sbt(sw, "gsb", [128, TB], F32); Tgsb = T()
                        else:
                            ncol = 1040
                            Win = sbt(sw, "Win", [128, KT, ncol], BF16); TWin = T()
                            cZ, cDT, cSGU = 0, 512, 528
                            Wout = sbt(sw, "Wout", [128, KT, D], BF16); TWout = T()
                            gluw = sbt(sw, "gluw", [128, 2, 256], BF16); Tgluw = T()
                            sguw = sbt(sw, "sguw", [128, 4, 128], BF16); Tsguw = T()
                            mixT = sbt(sw, "mixT", [128, KT, TB], BF16); TmixT = T()
                            gen = sbt(sw, "gen", [128, 2560], F32)
                            uvg = gen[:, 0:4 * TB].rearrange("p (a b) -> p a b", a=4); Tuvg = T()
                            vtok = sbt(sw, "vtok", [128, 256], BF16); Tvtok = T()
                            mixtok = sbt(sw, "mixtok", [128, 256], BF16); Tmixtok = T()
                            ybl = sbt(sw, "ybl", [128, 512], F32); Tybl = T()
                            yb5l = sbt(sw, "yb5l", [128, 2, TB], F32); Tyb5l = T()
                            y5g = sbt(sw, "y5g", [128, 2, TB], BF16); Ty5g = T()
                            ztmp = sbt(sw, "ztmp", [128, 512], F32); Tztmp = T()
                            ssq = sbt(sw, "ssq", [128, 4], F32); Tssq = T()
                            ytok = sbt(sw, "ytok", [128, 512], BF16); Tytok = T()
                        xbc = sbt(sw, "xbc", [128, KT, TB], BF16); Txbcl = T()
                        uT = sbt(sw, "uT", [128, 2, TB], BF16); TuT = T()
                        acc = [sbt(sw, "acc%d" % i, [128, TB], F32) for i in range(2)]; Tacc = [T(), T()]
                        sm = sbt(sw, "sm", [128, 12, 32], F32); Tsm = T()
                        seg = [sbt(sw, "seg%d" % i, [128, 128], F32) for i in range(2)]; Tseg = [T(), T()]
                        CBm = sbt(sw, "CBm", [128, 128], BF16); TCBm = T()
                        Lt = sbt(sw, "Lt", [128, 512], BF16); TLt = T()
                        Mt = sbt(sw, "Mt", [128, 512], BF16); TMt = T()
                        xr = sbt(sw, "xr", [128, 512], BF16); Txr = T()
                        xrd = sbt(sw, "xrd", [128, 512], BF16); Txrd = T()
                        Btok = sbt(sw, "Btok", [128, 256], BF16); TBtok = T()
                        ych = sbt(sw, "ych", [128, 512], F32); Tych = T()
                        ytmp = sbt(sw, "ytmp", [128, 512], F32); Tytmp = T()
                        BB = sbt(sw, "BB", [128, 2, 8, 128], BF16); TBB = T()
                        CC = sbt(sw, "CC", [128, 2, 8, 128], BF16); TCC = T()
                        tab = sbt(sw, "tab", [128, 2, 8, TS], F32); Ttab = T()
                        s5s = sbt(sw, "s5s", [128, 8, 8], F32); Ts5s = T()
                        s5w1 = sbt(sw, "s5w1", [128, 8, TS], F32); Ts5w1 = T()
                        s5w2 = sbt(sw, "s5w2", [128, 8, TS], F32); Ts5w2 = T()
                        s5w = [s5w1, s5w2]; Ts5w = [Ts5w1, Ts5w2]
                        Tslot = [[T() for _ in range(8)] for _ in range(2)]
                        Tctmp = T()
                        s5o = [sbt(sw, "s5o%d" % i, [128, 2, TS], BF16) for i in range(2)]; Ts5o = [[T(), T()], [T(), T()]]
                        y5d = sbt(sw, "y5d", [128, 2, TB], F32); Ty5d = T()

                        xflat = xt1[:].rearrange("p a b -> p (a b)")
                        stage = [xflat[:, 1024 * j:1024 * (j + 1)] for j in range(4)]
                        Tstage2 = [T() for _ in range(4)]
                        wsrc = w_in_d[l].rearrange("(k p) n -> p k n", p=128)
                        if d == 1:
                            load_w(stage, Tstage2, Win, TWin, lambda k, c0, c1: wsrc[:, k, O_XBC + c0:O_XBC + c1], KT, 1296, nrm[:, 0, :], Tnrm)
                            if l > 0:
                                gsrc = ple_gate_d[l - 1].rearrange("(k p) n -> p k n", p=128)
                                load_w(stage, Tstage2, Wg, TWg, lambda k, c0, c1: gsrc[:, k, c0:c1], KT, D, nrmprev[:, 2, :], Tnrmprev)
                                psrc = ple_proj_d[l - 1].rearrange("(k p) n -> p k n", p=128)
                                load_w(stage, Tstage2, Wp, TWp, lambda k, c0, c1: psrc[:, k, c0:c1], 2, D)
                        else:
                            load_w(stage, Tstage2, Win[:, :, 0:512], TWin, lambda k, c0, c1: wsrc[:, k, O_Z + c0:O_Z + c1], KT, 512, nrm[:, 0, :], Tnrm)
                            load_w(stage, Tstage2, Win[:, :, 512:528], TWin, lambda k, c0, c1: wsrc[:, k, O_DT + c0:O_DT + c1], KT, 16, nrm[:, 0, :], Tnrm)
                            load_w(stage, Tstage2, Win[:, :, 528:1040], TWin, lambda k, c0, c1: wsrc[:, k, O_SGU + c0:O_SGU + c1], KT, 512, nrm[:, 0, :], Tnrm)
                            osrc = w_out_d[l].rearrange("(k p) n -> p k n", p=128)
                            load_w(stage, Tstage2, Wout, TWout, lambda k, c0, c1: osrc[:, k, c0:c1], KT, D, mixn, Tmixn)
                            gls = glu_w_d[l].rearrange("(k p) n -> p k n", p=128)
                            load_w(stage, Tstage2, gluw, Tgluw, lambda k, c0, c1: gls[:, k, c0:c1], 2, 256)
                            load_w(stage, Tstage2, sguw[:].rearrange("p a b -> p (a b)").unsqueeze(1), Tsguw,
                                   lambda k, c0, c1: sgu_wT_d[l].rearrange("p a b -> p (a b)")[:, c0:c1], 1, 512)

                        with contextlib.ExitStack() as dz:
                            CW = 256
                            lamR = sbt(dz, "lamR", [128, 3, CW], F32); TlamR = T()
                            wkR = gen[:, :].rearrange("p (a b) -> p a b", a=10); TwkR = T()
                            wki = sbt(dz, "wki", [128, CW], I32); Twki = T()
                            lamS = sbt(dz, "lamS", [128, 3, 8], F32); TlamS = T()
                            wkS = sbt(dz, "wkS", [128, 10, 8], F32); TwkS = T()
                            wkiS = sbt(dz, "wkiS", [128, 8], I32); TwkiS = T()
                            bspt = sbt(dz, "bspt", [128, 2, CW], F32); Tbspt = T()
                            dma(lamS[:], lamS_d[l, d], W=[TlamS])

                            def disc(lam, Tl, wk, Tw, wi, Twi):
                                lr, li, ls = lam[:, 0, :], lam[:, 1, :], lam[:, 2, :]
                                R0, W0 = [Tl, Tw], [Tw]
                                act(wk[:, 5, :], ls, AF.Exp, R0, W0)
                                tt("dve", wk[:, 0, :], lr, wk[:, 5, :], ALU.mult, R0, W0)
                                act(wk[:, 0, :], wk[:, 0, :], AF.Exp, R0, W0)
                                tt("dve", wk[:, 6, :], li, wk[:, 5, :], ALU.mult, R0, W0)
                                ts("dve", wk[:, 7, :], wk[:, 6, :], 1.0 / (2 * math.pi), ALU.mult, R0, W0)
                                cp("dve", wi, wk[:, 7, :], R0, [Twi])
                                cp("dve", wk[:, 7, :], wi, [Twi], W0)
                                ts("dve", wk[:, 6, :], wk[:, 6, :], 0.25, ALU.mult, R0, W0)
                                stt(wk[:, 6, :], wk[:, 7, :], -math.pi / 2, wk[:, 6, :], ALU.mult, ALU.add, R0, W0)
                                ts("dve", wk[:, 7, :], wk[:, 6, :], math.pi / 2, ALU.add, R0, W0)
                                act(wk[:, 2, :], wk[:, 6, :], AF.Sin, R0, W0)
                                act(wk[:, 1, :], wk[:, 7, :], AF.Sin, R0, W0)
                                for _ in range(2):
                                    tt("dve", wk[:, 6, :], wk[:, 1, :], wk[:, 1, :], ALU.mult, R0, W0)
                                    tt("dve", wk[:, 7, :], wk[:, 2, :], wk[:, 2, :], ALU.mult, R0, W0)
                                    tt("dve", wk[:, 2, :], wk[:, 2, :], wk[:, 1, :], ALU.mult, R0, W0)
                                    ts("dve", wk[:, 2, :], wk[:, 2, :], 2.0, ALU.mult, R0, W0)
                                    tt("dve", wk[:, 1, :], wk[:, 6, :], wk[:, 7, :], ALU.subtract, R0, W0)
                                tt("dve", wk[:, 6, :], wk[:, 0, :], wk[:, 1, :], ALU.mult, R0, W0)
                                ts("dve", wk[:, 6, :], wk[:, 6, :], -1.0, ALU.add, R0, W0)
                                tt("dve", wk[:, 7, :], wk[:, 0, :], wk[:, 2, :], ALU.mult, R0, W0)
                                tt("dve", wk[:, 8, :], lr, lr, ALU.mult, R0, W0)
                                tt("dve", wk[:, 9, :], li, li, ALU.mult, R0, W0)
                                tt("dve", wk[:, 8, :], wk[:, 8, :], wk[:, 9, :], ALU.add, R0, W0)
                                op("dve", lambda e: e.reciprocal(out=wk[:, 8, :], in_=wk[:, 8, :]), R0, W0)
                                tt("dve", wk[:, 3, :], wk[:, 6, :], lr, ALU.mult, R0, W0)
                                tt("dve", wk[:, 9, :], wk[:, 7, :], li, ALU.mult, R0, W0)
                                tt("dve", wk[:, 3, :], wk[:, 3, :], wk[:, 9, :], ALU.add, R0, W0)
                                tt("dve", wk[:, 3, :], wk[:, 3, :], wk[:, 8, :], ALU.mult, R0, W0)
                                tt("dve", wk[:, 4, :], wk[:, 7, :], lr, ALU.mult, R0, W0)
                                tt("dve", wk[:, 9, :], wk[:, 6, :], li, ALU.mult, R0, W0)
                                tt("dve", wk[:, 4, :], wk[:, 4, :], wk[:, 9, :], ALU.subtract, R0, W0)
                                tt("dve", wk[:, 4, :], wk[:, 4, :], wk[:, 8, :], ALU.mult, R0, W0)

                            disc(lamS, TlamS, wkS, TwkS, wkiS[:], TwkiS)
                            BBf = [BB[:, 0].rearrange("p a b -> p (a b)"), BB[:, 1].rearrange("p a b -> p (a b)")]
                            CCf = [CC[:, 0].rearrange("p a b -> p (a b)"), CC[:, 1].rearrange("p a b -> p (a b)")]
                            for cc in range(1024 // CW):
                                csl = slice(cc * CW, (cc + 1) * CW)
                                dma(lamR[:], lamR_d[l, d:d + 1, :, csl].partition_broadcast(128), R=[TwkR], W=[TlamR])
                                dma(bspt[:, 0, :], bsp_d[l, 0].rearrange("p a b -> p (a b)")[:, csl], R=[TwkR], W=[Tbspt])
                                dma(bspt[:, 1, :], bsp_d[l, 1].rearrange("p a b -> p (a b)")[:, csl], R=[TwkR], W=[Tbspt])
                                disc(lamR, TlamR, wkR, TwkR, wki[:], Twki)
                                R1 = [TwkR, Tbspt]
                                tt("dve", wkR[:, 5, :], wkR[:, 3, :], bspt[:, 0, :], ALU.mult, R1, [TwkR])
                                tt("dve", wkR[:, 6, :], wkR[:, 4, :], bspt[:, 1, :], ALU.mult, R1, [TwkR])
                                tt("dve", BBf[0][:, csl], wkR[:, 5, :], wkR[:, 6, :], ALU.subtract, [TwkR], [TBB])
                                tt("dve", wkR[:, 5, :], wkR[:, 3, :], bspt[:, 1, :], ALU.mult, R1, [TwkR])
                                tt("dve", wkR[:, 6, :], wkR[:, 4, :], bspt[:, 0, :], ALU.mult, R1, [TwkR])
                                tt("dve", BBf[1][:, csl], wkR[:, 5, :], wkR[:, 6, :], ALU.add, [TwkR], [TBB])
                                dma(bspt[:, 0, :], csp_d[l, 0].rearrange("p a b -> p (a b)")[:, csl], R=[TwkR, TBB], W=[Tbspt])
                                dma(bspt[:, 1, :], csp_d[l, 1].rearrange("p a b -> p (a b)")[:, csl], R=[TwkR, TBB], W=[Tbspt])
                                cp("dve", CCf[0][:, csl], bspt[:, 0, :], [Tbspt], [TCC])
                                ts("dve", CCf[1][:, csl], bspt[:, 1, :], -1.0, ALU.mult, [Tbspt], [TCC])
                            cp("dve", s5s[:, 0, :], wkS[:, 0, :], [TwkS], [Ts5s])
                            cp("dve", s5s[:, 1, :], wkS[:, 1, :], [TwkS], [Ts5s])
                            cp("dve", s5s[:, 2, :], wkS[:, 2, :], [TwkS], [Ts5s])
                            cosT, sinT = tab[:, 0], tab[:, 1]
                            p0 = 0 if d == 0 else TS - 1
                            mset("dve", cosT[:, :, p0:p0 + 1], 1.0, [Ttab])
                            mset("dve", sinT[:, :, p0:p0 + 1], 0.0, [Ttab])
                            tA = s5w1
                            tB = y5d[:].rearrange("p a b -> p (a b)").rearrange("p (a b) -> p a b", a=8)
                            n = 1
                            while n < TS:
                                if d == 0:
                                    src = slice(0, n); dst = slice(n, 2 * n)
                                else:
                                    src = slice(TS - n, TS); dst = slice(TS - 2 * n, TS - n)
                                cn = s5s[:, 1, :].unsqueeze(2).broadcast_to([128, 8, n])
                                sn = s5s[:, 2, :].unsqueeze(2).broadcast_to([128, 8, n])
                                Rt = [Ttab, Ts5s, Ts5w1, Ty5d]
                                tt("dve", tA[:, :, 0:n], cosT[:, :, src], cn, ALU.mult, Rt, [Ts5w[0]])
                                tt("dve", tB[:, :, 0:n], sinT[:, :, src], sn, ALU.mult, Rt, [Ty5d])
                                tt("dve", cosT[:, :, dst], tA[:, :, 0:n], tB[:, :, 0:n], ALU.subtract, Rt, [Ttab])
                                tt("dve", tA[:, :, 0:n], sinT[:, :, src], cn, ALU.mult, Rt, [Ts5w[0]])
                                tt("dve", tB[:, :, 0:n], cosT[:, :, src], sn, ALU.mult, Rt, [Ty5d])
                                tt("dve", sinT[:, :, dst], tA[:, :, 0:n], tB[:, :, 0:n], ALU.add, Rt, [Ttab])
                                Rs = [Ts5s]
                                tt("dve", s5s[:, 3, :], s5s[:, 1, :], s5s[:, 1, :], ALU.mult, Rs, Rs)
                                tt("dve", s5s[:, 4, :], s5s[:, 2, :], s5s[:, 2, :], ALU.mult, Rs, Rs)
                                tt("dve", s5s[:, 2, :], s5s[:, 2, :], s5s[:, 1, :], ALU.mult, Rs, Rs)
                                ts("dve", s5s[:, 2, :], s5s[:, 2, :], 2.0, ALU.mult, Rs, Rs)
                                tt("dve", s5s[:, 1, :], s5s[:, 3, :], s5s[:, 4, :], ALU.subtract, Rs, Rs)
                                n *= 2
                        fw.barrier()
                        if l == 1 and fw.dry:
                            print("sbuf remaining sweep d=%d:" % d, nc.sbuf_bytes_remaining)
                        mset("dve", carry[:], 0.0, [Tcarry])
                        mset("dve", Hst[:], 0.0, [THst])
                        mset("pool", Hbf[:], 0.0, [THbf])

                        order = list(range(NTB - 1, -1, -1)) if d == 1 else list(range(NTB))
                        xsrc, Txs = (xT, TxIn) if l == 0 else (xA, TxA)

                        def stageA(i):
                            t0 = i * TB
                            xt, Txt = xt2[i % 2], Txt2[i % 2]
                            h, Th = hT[i % 3], ThT[i % 3]
                            dma(xt[:], xsrc[:, i], R=xtr(Txs, t0, TB), W=[Txt])
                            if d == 1 and l > 0:
                                norm_stage(xt[:], Txt, KT, sq, Tsq, h[:, :, 1:TB + 1], Th, TB, D, rtmp, Trt)
                                dma(pt[:], pT[l - 1][:, i], W=[Tpt])
                                cp("pool", ptb[:], pt[:], [Tpt], [Tptb])
                                for m in range(KT):
                                    pg, pe_ = 2 * (m % 2), 2 * (m % 2) + 1
                                    for k in range(KT):
                                        mm(P[pg][:, 0:TB], Wg[:, k, m * 128:(m + 1) * 128], h[:, k, 1:TB + 1], k == 0, k == KT - 1, [TWg, Th], [TP[pg]])
                                    for k in range(2):
                                        mm(P[pe_][:, 0:TB], Wp[:, k, m * 128:(m + 1) * 128], ptb[:, k, :], k == 0, k == 1, [TWp, Tptb], [TP[pe_]])
                                    act(gsb[:], P[pg][:, 0:TB], AF.Sigmoid, [TP[pg]], [Tgsb])
                                    tt("dve", gsb[:], gsb[:], P[pe_][:, 0:TB], ALU.mult, [Tgsb, TP[pe_]], [Tgsb])
                                    tt("pool", xt[:, m, :], xt[:, m, :], gsb[:], ALU.add, [Txt, Tgsb], [Txt])
                                dma(xA[:, i], xt[:], R=[Txt], W=xtr(TxA, t0, TB))
                            norm_stage(xt[:], Txt, KT, sq, Tsq, h[:, :, 1:TB + 1], Th, TB, D, rtmp, Trt)

                        def halos(i):
                            h, Th = hT[i % 3], ThT[i % 3]
                            t0 = i * TB
                            for side in (0, 1):
                                j = i - 1 if side == 0 else i + 1
                                col = 0 if side == 0 else TB + 1
                                if j < 0 or j >= NTB:
                                    mset("pool", h[:, :, col:col + 1], 0.0, [Th])
                                    continue
                                hn, Tn = hT[j % 3], ThT[j % 3]
                                scol = TB if side == 0 else 1
                                bnd = t0 if side == 0 else t0 + TB
                                if bnd % SEG == 0:
                                    ts("pool", h[:, :, col:col + 1], hn[:, :, scol:scol + 1], cm[:, bnd // SEG:bnd // SEG + 1], ALU.mult, [Tn, Tcm], [Th])
                                else:
                                    cp("pool", h[:, :, col:col + 1], hn[:, :, scol:scol + 1], [Tn], [Th])

                        def dt_tile(h, Th):
                            tri = triB if d == 1 else triF
                            dcol = cDT + 8 * d
                            for c in range(4):
                                hs = slice(1 + c * 128, 1 + (c + 1) * 128)
                                for k in range(KT):
                                    mm(P[4][:, 8 * c:8 * c + 8], h[:, k, hs], Win[:, k, dcol:dcol + 8], k == 0, k == KT - 1, [Th, TWin], [TP[4]])
                            dtb, ab, e1, dt, adt, rcs, d1, ecs, dec, etot, dtd, dsk = [sm[:, j, :] for j in range(12)]
                            S = [Tsm]
                            v3 = lambda ap: ap.rearrange("p (a b) -> p a b", a=4)
                            tt("dve", v3(dtb), v3(P[4][:, 0:32]), dtrow[:, 0, 8 * d:8 * d + 8].unsqueeze(1).broadcast_to([128, 4, 8]), ALU.add, [TP[4], Tdtrow, Tsm], S)
                            act(ab, dtb, AF.Abs, S, S)
                            act(e1, ab, AF.Exp, S, S, scale=-1.0)
                            act(e1, e1, AF.Ln, [Tsm, Teps], S, bias=epsb[:, 2:3])
                            stt(dt, dtb, 0.0, e1, ALU.max, ALU.add, S, S)
                            tt("dve", v3(adt), v3(dt), arow[:, 8 * d:8 * d + 8].unsqueeze(1).broadcast_to([128, 4, 8]), ALU.mult, [Tsm, Tdtrow], S)
                            for c in range(4):
                                mm(P[4][:, 32 + 8 * c:40 + 8 * c], tri, adt[:, 8 * c:8 * c + 8], True, True, [Tcst, Tsm], [TP[4]])
                                mm(P[4][:, 64 + 8 * c:72 + 8 * c], ones_f, adt[:, 8 * c:8 * c + 8], True, True, [Tcst, Tsm], [TP[4]])
                            cp("act", rcs, P[4][:, 32:64], [TP[4], Tsm], S)
                            tt("dve", d1, P[4][:, 64:96], rcs, ALU.subtract, [TP[4], Tsm], S)
                            act(ecs, rcs, AF.Exp, S, S)
                            act(dec, d1, AF.Exp, S, S)
                            act(etot, P[4][:, 64:96], AF.Exp, [TP[4], Tsm], S)
                            tt("dve", dtd, dt, dec, ALU.mult, S, S)

                        def ssd_chunk(i, c, h, Th):
                            t0 = i * TB + c * 128
                            cs = slice(c * 128, (c + 1) * 128)
                            hs = slice(1 + c * 128, 1 + (c + 1) * 128)
                            tri = triB if d == 1 else triF
                            Um = UB if d == 1 else UF
                            maskb = maskB_b if d == 1 else maskF_b
                            bnd = (t0 + 128) if d == 1 else t0
                            if bnd % SEG == 0 and 0 < bnd < NT:
                                ts("pool", Hst[:], Hst[:], cm[:, bnd // SEG:bnd // SEG + 1], ALU.mult, [THst, Tcm], [THst])
                                cp("pool", Hbf[:], Hst[:], [THst], [THbf])
                            dtb, ab, e1, dt, adt, rcs, d1, ecs, dec, etot, dtd, dsk = [sm[:, j, 8 * c:8 * c + 8] for j in range(12)]
                            for j in range(4):
                                tr(PT[:, 128 * j:128 * j + 128], xbc[:, j, cs], ident_b, [Txbcl, Tcstb], [TPT])
                            for g in range(2):
                                tr(PT[:, 512 + 128 * g:640 + 128 * g], xbc[:, 4 + g, cs], ident_b, [Txbcl, Tcstb], [TPT])
                            x3v = PT[:, 0:512].rearrange("p (a b) -> p a b", a=8)
                            tt("dve", xr[:].rearrange("p (a b) -> p a b", a=8), x3v, dt.unsqueeze(2).broadcast_to([128, 8, 64]), ALU.mult, [TPT, Tsm], [Txr])
                            tt("dve", xrd[:].rearrange("p (a b) -> p a b", a=8), x3v, dtd.unsqueeze(2).broadcast_to([128, 8, 64]), ALU.mult, [TPT, Tsm], [Txrd])
                            cp("act", Btok[:], PT[:, 512:768], [TPT], [TBtok])
                            if d == 0:
                                tt("dve", ytmp[:].rearrange("p (a b) -> p a b", a=8), x3v, dtrow[:, 2, 0:8].unsqueeze(2).broadcast_to([128, 8, 64]), ALU.mult, [TPT, Tdtrow], [Tytmp])
                            for g in range(2):
                                mm(P[4][:, 128 + 128 * g:256 + 128 * g], xbc[:, 4 + g, cs], xbc[:, 6 + g, cs], True, True, [Txbcl], [TP[4]])
                                tt("dve", CBm[:], P[4][:, 128 + 128 * g:256 + 128 * g], maskb, ALU.mult, [TP[4], Tcstb], [TCBm])
                                for hh in range(4):
                                    sg, Tsg = seg[hh % 2], Tseg[hh % 2]
                                    act(sg[:], Um, AF.Identity, [Tcst, Tsm], [Tsg], scale=adt[:, 4 * g + hh:4 * g + hh + 1])
                                    mm(P[6][:, 128 * hh:128 * hh + 128], sg[:], tri, True, True, [Tsg, Tcst], [TP[6]])
                                act(Lt[:], P[6][:, :], AF.Exp, [TP[6]], [TLt])
                                tt("dve", Mt[:].rearrange("p (a b) -> p a b", a=4), Lt[:].rearrange("p (a b) -> p a b", a=4),
                                   CBm[:].unsqueeze(1).broadcast_to([128, 4, 128]), ALU.mult, [TLt, TCBm], [TMt])
                                for hh in range(4):
                                    hd = 4 * g + hh
                                    mm(P[0][:, 64 * hd:64 * hd + 64], Mt[:, 128 * hh:128 * hh + 128], xr[:, 64 * hd:64 * hd + 64], True, True, [TMt, Txr], [TP[0]])
                                mm(P[1][:, 256 * g:256 * g + 256], xbc[:, 6 + g, cs], Hbf[:, 256 * g:256 * g + 256], True, True, [Txbcl, THbf], [TP[1]])
                                mm(P[2][:, 256 * g:256 * g + 256], Btok[:, 128 * g:128 * g + 128], xrd[:, 256 * g:256 * g + 256], True, True, [TBtok, Txrd], [TP[2]])
                            tt("dve", ych[:].rearrange("p (a b) -> p a b", a=8), P[1][:, :].rearrange("p (a b) -> p a b", a=8),
                               ecs.unsqueeze(2).broadcast_to([128, 8, 64]), ALU.mult, [TP[1], Tsm], [Tych])
                            tt("dve", ych[:], ych[:], P[0][:, :], ALU.add, [Tych, TP[0]], [Tych])
                            tt("pool", Hst[:].rearrange("p (a b) -> p a b", a=8), Hst[:].rearrange("p (a b) -> p a b", a=8),
                               etot.unsqueeze(2).broadcast_to([128, 8, 64]), ALU.mult, [THst, Tsm], [THst])
                            tt("dve", Hst[:], Hst[:], P[2][:, :], ALU.add, [THst, TP[2]], [THst])
                            cp("pool", Hbf[:], Hst[:], [THst], [THbf])

                        def s5_step(i, hf, q):
                            t0 = i * TB + hf * TS
                            hsl = slice(hf * TS, (hf + 1) * TS)
                            if q == 0:
                                bnd = (t0 + TS) if d == 1 else t0
                                if bnd % SEG == 0 and 0 < bnd < NT:
                                    ts("dve", carry[:].rearrange("p a b -> p (a b)"), carry[:].rearrange("p a b -> p (a b)"), cm[:, bnd // SEG:bnd // SEG + 1], ALU.mult, [Tcarry, Tcm], [Tcarry])
                            wA, TwA = s5w[q % 2], Ts5w[q % 2]
                            so = s5o[q % 2]
                            Tso0, Tso1 = Ts5o[q % 2]
                            mm(P[3][:, 0:TS], BB[:, 0, q, :], uT[:, q // 4, hsl], True, True, [TBB, TuT], [TP[3]])
                            mm(P[3][:, TS:2 * TS], BB[:, 1, q, :], uT[:, q // 4, hsl], True, True, [TBB, TuT], [TP[3]])
                            bre, bim, t1, t2, mre, mim, t3, t4 = [wA[:, j, :] for j in range(8)]
                            Tbre, Tbim, Tt1, Tt2, Tmre, Tmim, Tt3, Tt4 = Tslot[q % 2]
                            wre, wim = mre, mim
                            cq, sq_ = tab[:, 0, q, :], tab[:, 1, q, :]
                            act(wA[:, 0:2, :], P[3][:, 0:2 * TS].rearrange("p (a b) -> p a b", a=2), AF.Copy, [TP[3]], [Tbre, Tbim])
                            tt("pool", t1, bre, cq, ALU.mult, [Tbre, Ttab], [Tt1])
                            tt("pool", t2, bim, sq_, ALU.mult, [Tbim, Ttab], [Tt2])
                            tt("dve", t3, bim, cq, ALU.mult, [Tbim, Ttab], [Tt3])
                            tt("dve", t4, bre, sq_, ALU.mult, [Tbre, Ttab], [Tt4])
                            tt("dve", mim, t3, t4, ALU.subtract, [Tt3, Tt4], [Tmim])
                            tt("dve", mre, t1, t2, ALU.add, [Tt1, Tt2], [Tmre])
                            rb = s5s[:, 0, q:q + 1].broadcast_to([128, TS])
                            if d == 0:
                                op("dve", lambda e: e.tensor_tensor_scan(out=wre, data0=rb, data1=mre, initial=carry[:, 0, q:q + 1], op0=ALU.mult, op1=ALU.add), [Tmre, Ts5s, Tcarry], [Tmre])
                                op("dve", lambda e: e.tensor_tensor_scan(out=wim, data0=rb, data1=mim, initial=carry[:, 1, q:q + 1], op0=ALU.mult, op1=ALU.add), [Tmim, Ts5s, Tcarry], [Tmim])
                                lastc = TS - 1
                            else:
                                op("dve", lambda e: e.tensor_tensor_scan(out=wre[:, ::-1], data0=rb, data1=mre[:, ::-1], initial=carry[:, 0, q:q + 1], op0=ALU.mult, op1=ALU.add), [Tmre, Ts5s, Tcarry], [Tmre])
                                op("dve", lambda e: e.tensor_tensor_scan(out=wim[:, ::-1], data0=rb, data1=mim[:, ::-1], initial=carry[:, 1, q:q + 1], op0=ALU.mult, op1=ALU.add), [Tmim, Ts5s, Tcarry], [Tmim])
                                lastc = 0
                            c5, s5_ = s5s[:, 1, q:q + 1], s5s[:, 2, q:q + 1]
                            Rc = [Tmre, Tmim, Ts5s, Tcarry]
                            tt("pool", s5s[:, 5, q:q + 1], wre[:, lastc:lastc + 1], c5, ALU.mult, Rc, [Tctmp])
                            tt("pool", s5s[:, 6, q:q + 1], wim[:, lastc:lastc + 1], s5_, ALU.mult, Rc, [Tctmp])
                            tt("pool", carry[:, 0, q:q + 1], s5s[:, 5, q:q + 1], s5s[:, 6, q:q + 1], ALU.subtract, [Tctmp], [Tcarry])
                            tt("pool", s5s[:, 5, q:q + 1], wre[:, lastc:lastc + 1], s5_, ALU.mult, Rc, [Tctmp])
                            tt("pool", s5s[:, 6, q:q + 1], wim[:, lastc:lastc + 1], c5, ALU.mult, Rc, [Tctmp])
                            tt("pool", carry[:, 1, q:q + 1], s5s[:, 5, q:q + 1], s5s[:, 6, q:q + 1], ALU.add, [Tctmp], [Tcarry])
                            tt("pool", t1, wre, cq, ALU.mult, [Tmre, Ttab], [Tt1])
                            tt("pool", t2, wim, sq_, ALU.mult, [Tmim, Ttab], [Tt2])
                            tt("dve", t3, wre, sq_, ALU.mult, [Tmre, Ttab], [Tt3])
                            tt("dve", t4, wim, cq, ALU.mult, [Tmim, Ttab], [Tt4])
                            tt("dve", so[:, 1, :], t3, t4, ALU.add, [Tt3, Tt4], [Tso1])
                            tt("dve", so[:, 0, :], t1, t2, ALU.subtract, [Tt1, Tt2], [Tso0])
                            ut = q // 4
                            pcs = slice(ut * TS, (ut + 1) * TS)
                            mm(P[5][:, pcs], CC[:, 0, q, :], so[:, 0, :], q % 4 == 0, False, [TCC, Tso0], [TP[5]])
                            mm(P[5][:, pcs], CC[:, 1, q, :], so[:, 1, :], False, q % 4 == 3, [TCC, Tso1], [TP[5]])
                            if q % 4 == 3:
                                cp("act", y5d[:, ut, hsl], P[5][:, pcs], [TP[5]], [Ty5d])

                        def s5_steps(i):
                            halves = (1, 0) if d == 1 else (0, 1)
                            return [(hf, q) for hf in halves for q in range(8)]

                        if d == 1:
                            stageA(order[0])
                        for oi, i in enumerate(order):
                            t0 = i * TB
                            if d == 1:
                                if oi + 1 < len(order):
                                    stageA(order[oi + 1])
                                halos(i)
                            else:
                                stageA(i)
                            h, Th = hT[i % 3], ThT[i % 3]
                            xt, Txt = xt2[i % 2], Txt2[i % 2]
                            if d == 1:
                                def convtile(j):
                                    pm = j % 3
                                    ph = 3 + (j % 2)
                                    a = acc[j % 2]; Ta = Tacc[j % 2]
                                    for k in range(KT):
                                        mm(P[pm][:, 0:TB], Win[:, k, cXBC + 128 * j:cXBC + 128 * j + 128], h[:, k, 1:TB + 1], k == 0, k == KT - 1, [TWin, Th], [TP[pm]])
                                    for k in range(KT):
                                        mm(P[ph][:, 2 * j:2 * j + 2], Win[:, k, cXBC + 128 * j:cXBC + 128 * j + 128], h[:, k, 0:TB + 2:TB + 1], k == 0, k == KT - 1, [TWin, Th], [TP[ph]])
                                    w0, w1, w2, bb = [ssdcw[:, j, z:z + 1] for z in range(4)]
                                    act(a[:], P[pm][:, 0:TB], AF.Identity, [TP[pm], Tssdcw], [Ta], bias=bb, scale=w1)
                                    stt(a[:, 1:TB], P[pm][:, 0:TB - 1], w0, a[:, 1:TB], ALU.mult, ALU.add, [TP[pm], Ta, Tssdcw], [Ta])
                                    stt(a[:, 0:TB - 1], P[pm][:, 1:TB], w2, a[:, 0:TB - 1], ALU.mult, ALU.add, [TP[pm], Ta, Tssdcw], [Ta])
                                    stt(a[:, 0:1], P[ph][:, 2 * j:2 * j + 1], w0, a[:, 0:1], ALU.mult, ALU.add, [TP[ph], Ta, Tssdcw], [Ta])
                                    stt(a[:, TB - 1:TB], P[ph][:, 2 * j + 1:2 * j + 2], w2, a[:, TB - 1:TB], ALU.mult, ALU.add, [TP[ph], Ta, Tssdcw], [Ta])
                                    act(xbc[:, j, :], a[:], AF.Silu, [Ta], [Txbcl])
                                for j in range(0, 8, 2):
                                    fw.emit_merged(fw.record(lambda: convtile(j)), fw.record(lambda: convtile(j + 1)))
                                for j in range(2):
                                    pm = j % 3
                                    for k in range(KT):
                                        mm(P[pm][:, 0:TB], Win[:, k, cS5 + 128 * j:cS5 + 128 * j + 128], h[:, k, 1:TB + 1], k == 0, k == KT - 1, [TWin, Th], [TP[pm]])
                                    cp("act", uT[:, j, :], P[pm][:, 0:TB], [TP[pm]], [TuT])
                                dma(xbcS[:, i], xbc[:], R=[Txbcl], W=[Txbc[i]])
                                dma(uS[:, i], uT[:], R=[TuT], W=[Tu[i]])
                                steps = s5_steps(i)
                                dt_tile(h, Th)
                                for ci, c in enumerate((3, 2, 1, 0)):
                                    def chainA(c=c):
                                        ssd_chunk(i, c, h, Th)
                                        dma(ybS[t0 + c * 128:t0 + (c + 1) * 128, :], ych[:], R=[Tych], W=[Tyb[(t0 + c * 128) // 128]])

                                    def chainB(ci=ci):
                                        for (hf, q) in steps[4 * ci:4 * ci + 4]:
                                            s5_step(i, hf, q)
                                    fw.emit_merged(fw.record(chainA), fw.record(chainB))
                                dma(yb5S[:, i], y5d[:], R=[Ty5d], W=[Tyb5[i]])
                            else:
                                dma(xbc[:], xbcS[:, i], R=[Txbc[i]], W=[Txbcl])
                                dma(uT[:], uS[:, i], R=[Tu[i]], W=[TuT])
                                dma(yb5l[:], yb5S[:, i], R=[Tyb5[i]], W=[Tyb5l])
                                for j in range(4):
                                    pm = j % 3
                                    for k in range(KT):
                                        mm(P[pm][:, 0:TB], Win[:, k, cSGU + 128 * j:cSGU + 128 * j + 128], h[:, k, 1:TB + 1], k == 0, k == KT - 1, [TWin, Th], [TP[pm]])
                                    act(uvg[:, j, :], P[pm][:, 0:TB], AF.Gelu_apprx_tanh, [TP[pm]], [Tuvg])
                                act(sq[:, 0:2, :], uvg[:, 2:4, :], AF.Square, [Tuvg], [Tsq])
                                cp("pool", sq[:, 2:4, :], uvg[:, 2:4, :], [Tuvg], [Tsq])
                                for k in range(2):
                                    mm(P[3][:, 0:TB], ones_b, sq[:, 2 + k, :], k == 0, k == 1, [Tcstb, Tsq], [TP[3]])
                                for k in range(2):
                                    mm(P[6][:, 0:TB], ones_b, sq[:, k, :], k == 0, k == 1, [Tcstb, Tsq], [TP[6]])
                                a0, a1 = acc[0], acc[1]
                                ts("dve", a0[:], P[3][:, 0:TB], 1.0 / 256, ALU.mult, [TP[3]], [Tacc[0]])
                                tt("dve", a1[:], a0[:], a0[:], ALU.mult, [Tacc[0]], [Tacc[1]])
                                stt(a1[:], P[6][:, 0:TB], 1.0 / 256, a1[:], ALU.mult, ALU.subtract, [TP[6], Tacc[1]], [Tacc[1]])
                                act(a1[:], a1[:], AF.Ln, [Tacc[1], Teps], [Tacc[1]], bias=epsb[:, 1:2], scale=1.0)
                                act(a1[:], a1[:], AF.Exp, [Tacc[1]], [Tacc[1]], scale=-0.5)
                                for k in range(2):
                                    tt("dve", uvg[:, 2 + k, :], uvg[:, 2 + k, :], a0[:], ALU.subtract, [Tuvg, Tacc[0]], [Tuvg])
                                    tt("dve", uvg[:, 2 + k, :], uvg[:, 2 + k, :], a1[:], ALU.mult, [Tuvg, Tacc[1]], [Tuvg])
                                    act(sq[:, 4 + k, :], uvg[:, 2 + k, :], AF.Identity, [Tuvg, Ts5vec], [Tsq], bias=s5vec[:, k, 3:4], scale=s5vec[:, k, 2:3])
                                steps = s5_steps(i)
                                dt_tile(h, Th)
                                for c in range(4):
                                    def chainA(c=c):
                                        cs = slice(c * 128, (c + 1) * 128)
                                        hs = slice(1 + c * 128, 1 + (c + 1) * 128)
                                        if "chunk" in SKIP:
                                            return
                                        dma(ybl[:], ybS[t0 + c * 128:t0 + (c + 1) * 128, :], R=[Tyb[(t0 + c * 128) // 128]], W=[Tybl])
                                        ssd_chunk(i, c, h, Th)
                                        tt("pool", ych[:], ych[:], ybl[:], ALU.add, [Tych, Tybl], [Tych])
                                        tt("pool", ych[:], ych[:], ytmp[:], ALU.add, [Tych, Tytmp], [Tych])
                                        for k in range(KT):
                                            mm(P[1][:, :], h[:, k, hs], Win[:, k, cZ:cZ + 512], k == 0, k == KT - 1, [Th, TWin], [TP[1]])
                                        act(ztmp[:], P[1][:, :], AF.Silu, [TP[1]], [Tztmp])
                                        tt("dve", ych[:], ych[:], ztmp[:], ALU.mult, [Tych, Tztmp], [Tych])
                                        mset("dve", ssq[:, 0:1], 0.0, [Tssq])
                                        act(ztmp[:], ych[:], AF.Square, [Tych, Tssq], [Tztmp, Tssq], accum=ssq[:, 0:1])
                                        act(ssq[:, 1:2], ssq[:, 0:1], AF.Ln, [Tssq, Teps], [Tssq], bias=epsb[:, 0:1], scale=1.0 / 512)
                                        act(ssq[:, 2:3], ssq[:, 1:2], AF.Exp, [Tssq], [Tssq], scale=-0.5)
                                        act(ytok[:], ych[:], AF.Identity, [Tych, Tssq], [Tytok], scale=ssq[:, 2:3])
                                        for j in range(4):
                                            tr(PT[:, 128 * j:128 * j + 128], ytok[:, 128 * j:128 * j + 128], ident_b, [Tytok, Tcstb], [TPT])
                                        for j in range(4):
                                            cp("act", mixT[:, j, cs], PT[:, 128 * j:128 * j + 128], [TPT], [TmixT])
                                        if "sgu" in SKIP:
                                            return
                                        for j in range(2 if "notr" not in SKIP else 0):
                                            if "sgx" in SKIP:
                                                tr(PT[:, 512 + 128 * j:640 + 128 * j], xbc[:, 4 + j, cs], ident_b, [Txbcl, Tcstb], [TPT])
                                            else:
                                                tr(PT[:, 512 + 128 * j:640 + 128 * j], sq[:, 4 + j, cs], ident_b, [Tsq, Tcstb], [TPT])
                                        if "nocp" not in SKIP:
                                            cp("dve", vtok[:], PT[:, 512:768], [TPT], [Tvtok])
                                        if "sg1" in SKIP:
                                            return
                                        for hh in range(4):
                                            mm(P[6][:, 64 * hh:64 * hh + 64], sguw[:, hh, :], vtok[:, 64 * hh:64 * hh + 64], True, True, [Tsguw, Tvtok], [TP[6]])
                                        tt("dve", mixtok[:].rearrange("p (a b) -> p a b", a=4), P[6][:, 0:256].rearrange("p (a b) -> p a b", a=4),
                                           sgub[:].unsqueeze(2).broadcast_to([128, 4, 64]), ALU.add, [TP[6], Tsgub], [Tmixtok])
                                        if "sg2" in SKIP:
                                            return
                                        for j in range(2):
                                            tr(PT[:, 768 + 128 * j:896 + 128 * j], mixtok[:, 128 * j:128 * j + 128], ident_b, [Tmixtok, Tcstb], [TPT])
                                        if "sg3" in SKIP:
                                            return
                                        for j in range(2):
                                            tt("dve", uvg[:, j, cs], uvg[:, j, cs], PT[:, 768 + 128 * j:896 + 128 * j], ALU.mult, [Tuvg, TPT], [Tuvg])

                                    def chainB(c=c):
                                        for (hf, q) in steps[4 * c:4 * c + 4]:
                                            s5_step(i, hf, q)
                                    fw.emit_merged(fw.record(chainA), fw.record(chainB))
                                norm_stage(uvg[:, 0:2, :], Tuvg, 2, sq, Tsq, mixT[:, 6:8, :], TmixT, TB, 256, rtmp, Trt, pbank=3, eng="pool")
                                y5, Ty5 = y5d, Ty5d
                                for k in range(2):
                                    tt("pool", y5[:, k, :], y5d[:, k, :], yb5l[:, k, :], ALU.add, [Ty5d, Tyb5l], [Ty5])
                                    cp("pool", acc[k][:], uT[:, k, :], [TuT], [Tacc[k]])
                                    stt(y5[:, k, :], acc[k][:], s5vec[:, k, 0:1], y5[:, k, :], ALU.mult, ALU.add, [Tacc[k], Ts5vec, Ty5], [Ty5])
                                    act(y5[:, k, :], y5[:, k, :], AF.Gelu_apprx_tanh, [Ty5], [Ty5])
                                cp("pool", y5g[:], y5[:], [Ty5], [Ty5g])
                                for m in range(2):
                                    for k in range(2):
                                        mm(P[m][:, 0:TB], gluw[:, k, 128 * m:128 * m + 128], y5g[:, k, :], k == 0, k == 1, [Tgluw, Ty5g], [TP[m]])
                                    act(acc[m][:], P[m][:, 0:TB], AF.Sigmoid, [TP[m], Ts5vec], [Tacc[m]], bias=s5vec[:, m, 1:2], scale=1.0)
                                    tt("dve", y5[:, m, :], y5[:, m, :], acc[m][:], ALU.mult, [Ty5, Tacc[m]], [Ty5])
                                norm_stage(y5[:], Ty5, 2, sq, Tsq, mixT[:, 4:6, :], TmixT, TB, 256, rtmp, Trt, pbank=3, eng="pool")
                                for m in range(KT if "out" not in SKIP else 0):
                                    pm = m % 3
                                    for k in range(KT):
                                        mm(P[pm][:, 0:TB], Wout[:, k, 128 * m:128 * m + 128], mixT[:, k, :], k == 0, k == KT - 1, [TWout, TmixT], [TP[pm]])
                                    tt("dve", xt[:, m, :], xt[:, m, :], P[pm][:, 0:TB], ALU.add, [Txt, TP[pm]], [Txt])
                                dma(xB[:, i], xt[:], R=[Txt], W=xtr(TxB, t0, TB))
                        fw.new_phase()
                with contextlib.ExitStack() as sw:
                    ffncw = sbt(sw, "ffncw", [128, NFF, 4], F32); Tffncw = T()
                    dma(ffncw[:], ffncw_d[l], W=[Tffncw])
                    Wup = sbt(sw, "Wup", [128, KT, 2 * DFF], BF16); TWup = T()
                    Wdn = sbt(sw, "Wdn", [128, 22, D], BF16); TWdn = T()
                    xt2 = [sbt(sw, "fx%d" % i, [128, KT, TF], F32) for i in range(2)]; Txt2 = [T(), T()]
                    sq = sbt(sw, "fsq", [128, KT, TF], BF16); Tsq = T()
                    hT = [sbt(sw, "fh%d" % i, [128, KT, TF + 2], BF16) for i in range(3)]; ThT = [T(), T(), T()]
                    rtmp = sbt(sw, "frt", [128, TF], F32); Trt = T()
                    actb = sbt(sw, "actb", [128, 22, TF], BF16); Tactb = T()
                    acc = [sbt(sw, "facc%d" % i, [128, TF], F32) for i in range(6)]; Tacc = [T() for _ in range(6)]
                    stg = [sbt(sw, "stg%d" % i, [128, 1024], F32) for i in range(4)]; Tstg = [T() for _ in range(4)]
                    usrc = w_up_d[l].rearrange("(k p) n -> p k n", p=128)
                    load_w(stg, Tstg, Wup, TWup, lambda k, c0, c1: usrc[:, k, c0:c1], KT, 2 * DFF, nrm[:, 1, :], Tnrm)
                    dsrc = w_down_d[l].rearrange("(k p) n -> p k n", p=128)
                    load_w(stg, Tstg, Wdn, TWdn, lambda k, c0, c1: dsrc[:, k, c0:c1], 22, D)
                    if l == 1 and fw.dry:
                        print("sbuf remaining ffn:", nc.sbuf_bytes_remaining)

                    def stageAF(i):
                        t0 = i * TF
                        xt, Txt = xt2[i % 2], Txt2[i % 2]
                        for k in range(KT):
                            dma(xt[:, k, :], xB[:, t0 // 512, k, t0 % 512:t0 % 512 + TF], R=xtr(TxB, t0, TF), W=[Txt])
                        norm_stage(xt[:], Txt, KT, sq, Tsq, hT[i % 3][:, :, 1:TF + 1], ThT[i % 3], TF, D, rtmp, Trt)

                    def halosF(i):
                        h, Th = hT[i % 3], ThT[i % 3]
                        t0 = i * TF
                        for side in (0, 1):
                            j = i - 1 if side == 0 else i + 1
                            col = 0 if side == 0 else TF + 1
                            if j < 0 or j >= NTF:
                                mset("pool", h[:, :, col:col + 1], 0.0, [Th])
                                continue
                            hn, Tn = hT[j % 3], ThT[j % 3]
                            scol = TF if side == 0 else 1
                            bnd = t0 if side == 0 else t0 + TF
                            if bnd % SEG == 0:
                                ts("pool", h[:, :, col:col + 1], hn[:, :, scol:scol + 1], cm[:, bnd // SEG:bnd // SEG + 1], ALU.mult, [Tn, Tcm], [Th])
                            else:
                                cp("pool", h[:, :, col:col + 1], hn[:, :, scol:scol + 1], [Tn], [Th])

                    stageAF(0)
                    for i in range(NTF):
                        t0 = i * TF
                        if i + 1 < NTF:
                            stageAF(i + 1)
                        halosF(i)
                        h, Th = hT[i % 3], ThT[i % 3]
                        xt, Txt = xt2[i % 2], Txt2[i % 2]
                        for jj in range(22):
                            accs = [None, None]

                            def halfops(half, jj=jj):
                                j = jj + 22 * half
                                pm = (0, 1, 2, 3, 4, 5, 6)[(2 * jj + half) % 7]
                                a = acc[2 * (jj % 3) + half]; Ta = Tacc[2 * (jj % 3) + half]
                                for k in range(KT):
                                    mm(P[pm][:, 0:TF + 2], Wup[:, k, 128 * j:128 * j + 128], h[:, k, 0:TF + 2], k == 0, k == KT - 1, [TWup, Th], [TP[pm]])
                                w0, w1, w2, bb = [ffncw[:, j, z:z + 1] for z in range(4)]
                                act(a[:], P[pm][:, 1:TF + 1], AF.Identity, [TP[pm], Tffncw], [Ta], bias=bb, scale=w1)
                                stt(a[:], P[pm][:, 0:TF], w0, a[:], ALU.mult, ALU.add, [TP[pm], Ta, Tffncw], [Ta])
                                stt(a[:], P[pm][:, 2:TF + 2], w2, a[:], ALU.mult, ALU.add, [TP[pm], Ta, Tffncw], [Ta])
                                accs[half] = (a, Ta)
                            fw.emit_merged(fw.record(lambda: halfops(0)), fw.record(lambda: halfops(1)))
                            (ag, Tag), (av, Tav) = accs
                            act(ag[:], ag[:], AF.Silu, [Tag], [Tag])
                            tt("pool", actb[:, jj, :], ag[:], av[:], ALU.mult, [Tag, Tav], [Tactb])
                        for m in range(KT):
                            pm = 5 + (m % 2)
                            for k in range(22):
                                mm(P[pm][:, 0:TF], Wdn[:, k, 128 * m:128 * m + 128], actb[:, k, :], k == 0, k == 21, [TWdn, Tactb], [TP[pm]])
                            tt("dve", xt[:, m, :], xt[:, m, :], P[pm][:, 0:TF], ALU.add, [Txt, TP[pm]], [Txt])
                            dma(xA[:, t0 // 512, m, t0 % 512:t0 % 512 + TF], xt[:, m, :], R=[Txt], W=xtr(TxA, t0, TF))
                    fw.new_phase()
                nrmprev_keep = None
            if not last_layer:
                nrmprev = sbt(top, "nrmprev", [128, 4, 8], F32); Tnrmprev = T()
                dma(nrmprev[:], nrm_d[l], W=[Tnrmprev])

        with contextlib.ExitStack() as sw:
            TE = 512
            nrm = sbt(sw, "enrm", [128, 4, 8], F32); Tnrm = T()
            dma(nrm[:], nrm_d[L - 1], W=[Tnrm])
            Wg = sbt(sw, "eWg", [128, KT, D], BF16); TWg = T()
            Wp = sbt(sw, "eWp", [128, 2, D], BF16); TWp = T()
            stg = [sbt(sw, "estg%d" % i, [128, 1024], F32) for i in range(4)]; Tstg = [T() for _ in range(4)]
            gsrc = ple_gate_d[L - 1].rearrange("(k p) n -> p k n", p=128)
            load_w(stg, Tstg, Wg, TWg, lambda k, c0, c1: gsrc[:, k, c0:c1], KT, D, nrm[:, 2, :], Tnrm)
            psrc = ple_proj_d[L - 1].rearrange("(k p) n -> p k n", p=128)
            load_w(stg, Tstg, Wp, TWp, lambda k, c0, c1: psrc[:, k, c0:c1], 2, D)
            xt2 = [sbt(sw, "ex%d" % i, [128, KT, TE], F32) for i in range(2)]; Txt2 = [T(), T()]
            sq = sbt(sw, "esq", [128, KT, TE], BF16); Tsq = T()
            hb = [sbt(sw, "eh%d" % i, [128, KT, TE], BF16) for i in range(2)]; Thb = [T(), T()]
            rtmp = sbt(sw, "ert", [128, TE], F32); Trt = T()
            pt = sbt(sw, "ept", [128, 2, TE], F32); Tpt = T()
            ptb = sbt(sw, "eptb", [128, 2, TE], BF16); Tptb = T()
            gsb = sbt(sw, "egsb", [128, TE], F32); Tgsb = T()
            ot = [sbt(sw, "eo%d" % i, [128, KT, TE], F32) for i in range(2)]; Tot = [T(), T()]
            for i in range(NT // TE):
                t0 = i * TE
                xt, Txt = xt2[i % 2], Txt2[i % 2]
                h, Th = hb[i % 2], Thb[i % 2]
                o, To = ot[i % 2], Tot[i % 2]
                dma(xt[:], xA[:, i], R=xtr(TxA, t0, TE), W=[Txt])
                norm_stage(xt[:], Txt, KT, sq, Tsq, h[:], Th, TE, D, rtmp, Trt)
                dma(pt[:], pT[L - 1][:, i], W=[Tpt])
                cp("pool", ptb[:], pt[:], [Tpt], [Tptb])
                for m in range(KT):
                    pg, pe_ = 2 * (m % 2), 2 * (m % 2) + 1
                    for k in range(KT):
                        mm(P[pg][:, 0:TE], Wg[:, k, m * 128:(m + 1) * 128], h[:, k, :], k == 0, k == KT - 1, [TWg, Th], [TP[pg]])
                    for k in range(2):
                        mm(P[pe_][:, 0:TE], Wp[:, k, m * 128:(m + 1) * 128], ptb[:, k, :], k == 0, k == 1, [TWp, Tptb], [TP[pe_]])
                    act(gsb[:], P[pg][:, 0:TE], AF.Sigmoid, [TP[pg]], [Tgsb])
                    tt("dve", gsb[:], gsb[:], P[pe_][:, 0:TE], ALU.mult, [Tgsb, TP[pe_]], [Tgsb])
                    tt("pool", xt[:, m, :], xt[:, m, :], gsb[:], ALU.add, [Txt, Tgsb], [Txt])
                act(sq[:], xt[:], AF.Square, [Txt], [Tsq])
                for k in range(KT):
                    mm(P[6][:, 0:TE], ones_b, sq[:, k, :], k == 0, k == KT - 1, [Tcstb, Tsq], [TP[6]])
                act(rtmp[:], P[6][:, 0:TE], AF.Ln, [TP[6], Teps], [Trt], bias=epsb[:, 0:1], scale=1.0 / D)
                act(rtmp[:], rtmp[:], AF.Exp, [Trt], [Trt], scale=-0.5)
                for k in range(KT):
                    stt(o[:, k, :], xt[:, k, :], nrm[:, 3, k:k + 1], rtmp[:], ALU.mult, ALU.mult, [Txt, Tnrm, Trt], [To])
                dma(yT[:, i], o[:], R=[To])
        fw.barrier()
        print("FW idx", {k: v for k, v in fw.idx.items() if k[0] != "q"}, "dma", fw.ndma, "ep", fw.ep)
    return nc


def _prep_weights(inp, L):
    f = lambda a: np.ascontiguousarray(np.asarray(a, dtype=np.float32))
    w = {}
    w["w_in"] = f(inp["w_in"][:L]); w["w_out"] = f(inp["w_out"][:L]); w["w_up"] = f(inp["ffn_w_up"][:L])
    w["w_down"] = f(inp["ffn_w_down"][:L]); w["ple_proj"] = f(inp["ple_proj"][:L]); w["ple_gate"] = f(inp["ple_gate_w"][:L])
    w["glu_w"] = f(inp["s5_glu_w"][:L])
    w["sgu_wT"] = f(np.transpose(np.asarray(inp["sgu_w"][:L]), (0, 3, 1, 2)))
    fin = np.broadcast_to(np.asarray(inp["final_norm"])[None], (L, D))
    nr = np.stack([inp["norm_mix"][:L], inp["norm_ffn"][:L], inp["ple_norm"][:L], fin], 1)
    w["nrm"] = f(nr.reshape(L, 4, 8, 128).transpose(0, 3, 1, 2))
    mx = np.concatenate([inp["ssd_norm"][:L], inp["s5_out_norm"][:L], inp["sgu_out_norm"][:L]], 1)
    w["mixn"] = f(mx.reshape(L, 8, 128).transpose(0, 2, 1))
    cw = np.concatenate([np.asarray(inp["ssd_conv_w"][:L]), np.asarray(inp["ssd_conv_b"][:L])[:, None]], 1)
    w["ssdcw"] = f(cw.reshape(L, 4, 8, 128).transpose(0, 3, 2, 1))
    fc = np.concatenate([np.asarray(inp["ffn_conv_w"][:L]), np.asarray(inp["ffn_conv_b"][:L])[:, None]], 1)
    w["ffncw"] = f(fc.reshape(L, 4, NFF, 128).transpose(0, 3, 2, 1))
    dr = np.zeros((L, 3, 16), np.float32)
    dr[:, 0] = np.asarray(inp["ssd_dt_bias"][:L]).reshape(L, 16)
    dr[:, 1] = np.asarray(inp["ssd_a_log"][:L]).reshape(L, 16)
    dr[:, 2, :8] = np.asarray(inp["ssd_d"][:L])
    w["dtrow"] = dr
    sv = np.zeros((L, 6, 256), np.float32)
    sv[:, 0] = inp["s5_d"][:L]; sv[:, 1] = inp["s5_glu_b"][:L]; sv[:, 2] = inp["sgu_norm_w"][:L]; sv[:, 3] = inp["sgu_norm_b"][:L]
    w["s5vec"] = f(sv.reshape(L, 6, 2, 128).transpose(0, 3, 2, 1))
    w["sgub"] = f(np.transpose(np.asarray(inp["sgu_b"][:L]), (0, 2, 1)))
    lam = np.stack([np.asarray(inp["s5_lambda_re"][:L]), np.asarray(inp["s5_lambda_im"][:L]),
                    np.broadcast_to(np.asarray(inp["s5_log_step"][:L])[..., None], (L, 2, 16, 64))], 2)
    ls = lam.reshape(L, 2, 3, 8, 2, 64).transpose(0, 1, 4, 5, 2, 3).reshape(L, 2, 128, 3, 8)
    w["lamS"] = f(ls)
    w["lamR"] = f(lam.reshape(L, 2, 3, 1024))
    bsp = np.zeros((L, 2, 128, 8, 128), np.float32)
    csp = np.zeros((L, 2, 128, 8, 128), np.float32)
    for ri, (bn, cn) in enumerate((("s5_b_re", "s5_c_re"), ("s5_b_im", "s5_c_im"))):
        b = np.asarray(inp[bn][:L]); c = np.asarray(inp[cn][:L])
        for g in range(16):
            q, hh, r0 = g // 2, (g % 2) * 64, (g % 8) * 16
            bsp[:, ri, r0:r0 + 16, q, hh:hh + 64] = np.transpose(b[:, g], (0, 2, 1))
            csp[:, ri, hh:hh + 64, q, r0:r0 + 16] = np.transpose(c[:, g], (0, 2, 1))
    w["bsp"] = bsp; w["csp"] = csp
    cst = np.zeros((128, 6, 128), np.float32)
    s = np.arange(128)[:, None]; t = np.arange(128)[None, :]
    cst[:, 0] = (s == t); cst[:, 1] = (s <= t); cst[:, 2] = (s >= t); cst[:, 3] = (s > t); cst[:, 4] = (s < t); cst[:, 5] = 1.0
    w["consts"] = cst
    return w


_NC_CACHE = {}


def run_streams(inp, streams, L, NSEG, SEG):
    key = (L, NSEG, SEG)
    if key not in _NC_CACHE:
        _NC_CACHE[key] = build(L, NSEG, SEG)
    nc = _NC_CACHE[key]
    w = _prep_weights(inp, L)
    in_maps = []
    for s in streams:
        m = dict(w)
        NTs = s["x"].shape[0]
        m["xT"] = np.ascontiguousarray(s["x"].astype(np.float32).reshape(NTs // 512, 512, 8, 128).transpose(3, 0, 2, 1))
        m["pT"] = np.ascontiguousarray(s["p"].astype(np.float32).reshape(L, NTs // 512, 512, 2, 128).transpose(0, 4, 1, 3, 2))
        m["cm"] = np.ascontiguousarray(np.broadcast_to(np.asarray(s["cm"], np.float32)[None], (128, NSEG + 1)))
        in_maps.append(m)
    res = run_bass_kernel_spmd(nc, in_maps, core_ids=list(range(len(streams))))
    return [np.ascontiguousarray(np.transpose(r["yT"], (1, 3, 2, 0)).reshape(-1, D)) for r in res.results]


def kernel(**inp):
    L, NSEG, SEG = 4, 4, 2048
    NT = NSEG * SEG
    xp = np.asarray(inp["x_prompt"]); xs = np.asarray(inp["x_sample"])
    pp = np.asarray(inp["p_prompt"]); ps = np.asarray(inp["p_sample"])
    streams = []
    for c in range(8):
        if c < 4:
            x = xp[4 * c:4 * c + 4].reshape(NT, D)
            p = pp[:, 4 * c:4 * c + 4].reshape(L, NT, PLE)
            cmv = np.zeros(NSEG + 1, np.float32)
        elif c < 6:
            x = np.zeros((NT, D), np.float32); p = np.zeros((L, NT, PLE), np.float32)
            cmv = np.zeros(NSEG + 1, np.float32)
        else:
            x = xs[c - 6]; p = ps[:, c - 6]
            cmv = np.zeros(NSEG + 1, np.float32); cmv[1:NSEG] = 1.0
        streams.append(dict(x=x, p=p, cm=cmv))
    outs = run_streams(inp, streams, L, NSEG, SEG)
    y_prompt = np.stack([outs[c].reshape(4, SEG, D) for c in range(4)], 0).reshape(16, SEG, D).astype(np.float32)
    y_sample = np.stack([outs[6], outs[7]], 0).astype(np.float32)
    return (y_prompt, y_sample)
```
